# Optimizing a Trainium2 kernel written in Bass

```python
import math
import jax
import jax.numpy as jnp
from jax import lax
import numpy as np

D_MODEL = 1024
BATCH = 4
SEQ = 4096
DEPTH = 4
DEC_BATCH = 128
DEC_SEQ = 8
PAST_LEN = 2048
PAGE_SIZE = 128

N_EVEN = (DEPTH + 1) // 2
N_ODD = DEPTH // 2
POOL_WIDTH = D_MODEL // 2
POOL_WINDOWS = (2, 4, 8, 16)
POOL_GROUPS = len(POOL_WINDOWS)
POOL_GROUP_DIM = POOL_WIDTH // POOL_GROUPS
POOL_BUF = max(POOL_WINDOWS) - 1
DA_HEADS = 4
DA_QK_DIM = 64
DA_V_DIM = 2 * DA_QK_DIM
DA_WIDTH = DA_HEADS * DA_V_DIM
DA_QK_WIDTH = DA_HEADS * 2 * DA_QK_DIM
ROT_DIM = DA_QK_DIM // 4
ROPE_THETA = 500000.0
Q_BLOCK = 128
GLA_HEADS = 4
GLA_K_DIM = D_MODEL // 2 // GLA_HEADS
GLA_V_DIM = D_MODEL // GLA_HEADS
GLA_GATE_RANK = 16
GLA_TAU = 16.0
GLA_CHUNK = 64
EVEN_IN = 2 * POOL_WIDTH + 2 * DA_QK_WIDTH + 2 * DA_WIDTH
EVEN_SPLITS = (POOL_WIDTH, 2 * POOL_WIDTH, 2 * POOL_WIDTH + DA_QK_WIDTH,
               2 * POOL_WIDTH + 2 * DA_QK_WIDTH, 2 * POOL_WIDTH + 2 * DA_QK_WIDTH + DA_WIDTH)
EVEN_OUT_IN = POOL_WIDTH + DA_WIDTH
ODD_IN = 2 * GLA_HEADS * GLA_K_DIM + 2 * GLA_HEADS * GLA_V_DIM
ODD_SPLITS = (GLA_HEADS * GLA_K_DIM, 2 * GLA_HEADS * GLA_K_DIM, 2 * GLA_HEADS * GLA_K_DIM + GLA_HEADS * GLA_V_DIM)
ODD_OUT_IN = GLA_HEADS * GLA_V_DIM
DN_ALPHA = (2 * DEPTH) ** 0.25
DN_BETA = (8 * DEPTH) ** -0.25
LN_EPS = 1e-5
RMS_EPS = 1e-5

kernel_name = 'hybrid_pool_diffattn_gla_decoder'


def layer_norm(x, g, b):
    xf = x.astype(jnp.float32)
    mu = jnp.mean(xf, -1, keepdims=True)
    xc = xf - mu
    var = jnp.mean(xc * xc, -1, keepdims=True)
    return (xc * lax.rsqrt(var + LN_EPS) * g.astype(jnp.float32) + b.astype(jnp.float32)).astype(x.dtype)


def rms_norm(x, w):
    xf = x.astype(jnp.float32)
    return xf * lax.rsqrt(jnp.mean(xf * xf, -1, keepdims=True) + RMS_EPS) * w.astype(jnp.float32)


def rope_partial(x, pos):
    half = ROT_DIM // 2
    inv_freq = jnp.power(jnp.float32(ROPE_THETA), -jnp.arange(0, ROT_DIM, 2, dtype=jnp.float32) / ROT_DIM)
    ang = pos.astype(jnp.float32)[:, None] * inv_freq[None, :]
    cos = jnp.cos(ang)[None, :, None, None, :]
    sin = jnp.sin(ang)[None, :, None, None, :]
    xf = x.astype(jnp.float32)
    x1 = xf[..., :half]
    x2 = xf[..., half:ROT_DIM]
    return jnp.concatenate([x1 * cos - x2 * sin, x2 * cos + x1 * sin, xf[..., ROT_DIM:]], -1).astype(x.dtype)


def pool_mixer(u, buf, pos0, w_lin, scale):
    b, t, _ = u.shape
    u_ext = u if buf is None else jnp.concatenate([buf.astype(u.dtype), u], axis=1)
    n_ext = u_ext.shape[1]
    n_past = n_ext - t
    cs = jnp.cumsum(u_ext.astype(jnp.float32), axis=1)
    cs = jnp.pad(cs, ((0, 0), (1, 0), (0, 0))).reshape(b, n_ext + 1, POOL_GROUPS, POOL_GROUP_DIM)
    hi = jnp.arange(t) + n_past + 1
    pos = pos0 + jnp.arange(t)
    uf = u.astype(jnp.float32).reshape(b, t, POOL_GROUPS, POOL_GROUP_DIM)
    outs = []
    for g, w in enumerate(POOL_WINDOWS):
        lo = jnp.maximum(hi - w, 0)
        win_sum = cs[:, hi, g] - cs[:, lo, g]
        count = jnp.minimum(pos + 1, w).astype(jnp.float32)[None, :, None]
        outs.append(win_sum / count - uf[:, :, g])
    d = jnp.stack(outs, axis=2)
    y = jnp.einsum('btgc,gce->btge', d, w_lin.astype(jnp.float32)).reshape(b, t, POOL_WIDTH)
    y = y * scale.astype(jnp.float32)
    return y.astype(u.dtype), u_ext[:, -POOL_BUF:]


def diff_lambda(lam, layer_idx):
    lam = lam.astype(jnp.float32)
    lam_init = 0.8 - 0.6 * math.exp(-0.3 * layer_idx)
    lam_full = jnp.exp(jnp.sum(lam[0] * lam[1])) - jnp.exp(jnp.sum(lam[2] * lam[3])) + lam_init
    return lam_full, lam_init


def diff_attn_prompt(q, k, v, lam_full):
    b, s, h, _, d = q.shape
    qb_len = math.gcd(s, Q_BLOCK)
    n_blk = s // qb_len
    qb = q.reshape(b, n_blk, qb_len, h, 2, d).swapaxes(0, 1)
    kpos = jnp.arange(s)

    def block(args):
        qi, bi = args
        sc = jnp.einsum('bqhcd,bkhcd->bhcqk', qi, k)
        qpos = bi * qb_len + jnp.arange(qb_len)
        sc = jnp.where((qpos[:, None] >= kpos[None, :])[None, None, None], sc, -jnp.inf)
        p = jax.nn.softmax(sc, axis=-1)
        a = p[:, :, 0] - lam_full * p[:, :, 1]
        return jnp.einsum('bhqk,bkhe->bqhe', a, v)

    o = lax.map(block, (qb, jnp.arange(n_blk)))
    return o.swapaxes(0, 1).reshape(b, s, h, v.shape[-1])


def diff_attn_sample(q, k_new, v_new, k_past, v_past, lam_full):
    t = q.shape[1]
    n_past = k_past.shape[1]
    s_past = jnp.einsum('nqhcd,nkhcd->nhcqk', q, k_past)
    s_new = jnp.einsum('nqhcd,nkhcd->nhcqk', q, k_new)
    causal = jnp.arange(t)[:, None] >= jnp.arange(t)[None, :]
    s_new = jnp.where(causal[None, None, None], s_new, -jnp.inf)
    p = jax.nn.softmax(jnp.concatenate([s_past, s_new], axis=-1), axis=-1)
    a = p[:, :, 0] - lam_full * p[:, :, 1]
    return (jnp.einsum('nhqk,nkhe->nqhe', a[..., :n_past], v_past)
            + jnp.einsum('nhqk,nkhe->nqhe', a[..., n_past:], v_new))


def even_layer(x, pos0, layer_idx, pool_buf, k_past, v_past,
               w_in, w_pool_lin, pool_scale, lam, subln_w, w_out, ln_g, ln_b):
    b, t, _ = x.shape
    pu, pg, q, k, v, ag = jnp.split(x @ w_in, EVEN_SPLITS, axis=-1)
    pool_out, new_buf = pool_mixer(pu, pool_buf, pos0, w_pool_lin, pool_scale)
    pos = pos0 + jnp.arange(t)
    q = rope_partial(q.reshape(b, t, DA_HEADS, 2, DA_QK_DIM), pos) * DA_QK_DIM ** -0.5
    k = rope_partial(k.reshape(b, t, DA_HEADS, 2, DA_QK_DIM), pos)
    v = v.reshape(b, t, DA_HEADS, DA_V_DIM)
    lam_full, lam_init = diff_lambda(lam, layer_idx)
    qf, kf, vf = q.astype(jnp.float32), k.astype(jnp.float32), v.astype(jnp.float32)
    if k_past is None:
        o = diff_attn_prompt(qf, kf, vf, lam_full)
    else:
        o = diff_attn_sample(qf, kf, vf, k_past.astype(jnp.float32), v_past.astype(jnp.float32), lam_full)
    o = (rms_norm(o, subln_w) * (1.0 - lam_init)).reshape(b, t, DA_WIDTH).astype(x.dtype)
    mix = jnp.concatenate([pool_out * jax.nn.silu(pg), o * jax.nn.silu(ag)], axis=-1)
    y = layer_norm(DN_ALPHA * x + mix @ w_out, ln_g, ln_b)
    return y, new_buf, k.reshape(b, t, DA_HEADS, 2 * DA_QK_DIM), v


def gla_chunked(q, k, v, log_a, s0):
    b, l, h, _ = q.shape
    dv = v.shape[-1]
    c = math.gcd(l, GLA_CHUNK)
    n = l // c

    def to_chunks(z):
        return z.astype(jnp.float32).reshape(b, n, c, h, z.shape[-1]).swapaxes(0, 1)

    causal = jnp.arange(c)[:, None] >= jnp.arange(c)[None, :]

    def step(s, inp):
        qc, kc, vc, gc = inp
        cb = jnp.cumsum(gc, axis=1)
        cb_last = cb[:, -1]
        o_inter = jnp.einsum('bihk,bhkv->bihv', qc * jnp.exp(cb), s)
        rel = jnp.where(causal[None, :, :, None, None], cb[:, :, None] - cb[:, None, :], -jnp.inf)
        att = jnp.sum(qc[:, :, None] * kc[:, None] * jnp.exp(rel), axis=-1)
        o_intra = jnp.einsum('bijh,bjhv->bihv', att, vc)
        s_new = jnp.exp(cb_last)[..., None] * s + jnp.einsum(
            'bjhk,bjhv->bhkv', kc * jnp.exp(cb_last[:, None] - cb), vc)
        return s_new, o_inter + o_intra

    s_end, o = lax.scan(step, s0, (to_chunks(q), to_chunks(k), to_chunks(v), to_chunks(log_a)))
    return o.swapaxes(0, 1).reshape(b, l, h, dv), s_end


def odd_layer(x, s0, w_in, w_ga, w_gb, b_g, norm_w, w_out, ln_g, ln_b):
    b, t, _ = x.shape
    q, k, v, g = jnp.split(x @ w_in, ODD_SPLITS, axis=-1)
    log_a = jax.nn.log_sigmoid(((x @ w_ga) @ w_gb + b_g).astype(jnp.float32)) / GLA_TAU
    q = q.reshape(b, t, GLA_HEADS, GLA_K_DIM) * GLA_K_DIM ** -0.5
    k = k.reshape(b, t, GLA_HEADS, GLA_K_DIM)
    v = v.reshape(b, t, GLA_HEADS, GLA_V_DIM)
    log_a = log_a.reshape(b, t, GLA_HEADS, GLA_K_DIM)
    if s0 is None:
        s0 = jnp.zeros((b, GLA_HEADS, GLA_K_DIM, GLA_V_DIM), jnp.float32)
    o, s_end = gla_chunked(q, k, v, log_a, s0.astype(jnp.float32))
    o = rms_norm(o, norm_w).reshape(b, t, ODD_OUT_IN).astype(x.dtype) * jax.nn.silu(g)
    y = layer_norm(DN_ALPHA * x + o @ w_out, ln_g, ln_b)
    return y, s_end.astype(x.dtype)


def setup_inputs(seed: int = 0) -> dict:
    key = jax.random.key(seed)
    ks = jax.random.split(key, 24)

    def nrm(k, shape, scale=1.0):
        return jax.random.normal(k, shape, jnp.float32) * scale

    n_pages = PAST_LEN // PAGE_SIZE
    n_used = DEC_BATCH * n_pages
    n_pool = n_used + max(1, n_used // 4)
    page_table = jax.random.permutation(ks[0], n_pool)[:n_used].reshape(DEC_BATCH, n_pages).astype(jnp.int32)
    return {
        'x_prompt': nrm(ks[1], (BATCH, SEQ, D_MODEL)),
        'x_sample': nrm(ks[2], (DEC_BATCH, DEC_SEQ, D_MODEL)),
        'cache_k': nrm(ks[3], (n_pool, N_EVEN, PAGE_SIZE, DA_HEADS, 2 * DA_QK_DIM)),
        'cache_v': nrm(ks[4], (n_pool, N_EVEN, PAGE_SIZE, DA_HEADS, DA_V_DIM)),
        'state_pool': nrm(ks[5], (DEC_BATCH, N_EVEN, POOL_BUF, POOL_WIDTH)),
        'state_gla': nrm(ks[6], (DEC_BATCH, N_ODD, GLA_HEADS, GLA_K_DIM, GLA_V_DIM)),
        'page_table': page_table,
        'w_in_even': nrm(ks[7], (N_EVEN, D_MODEL, EVEN_IN), D_MODEL ** -0.5),
        'w_pool_lin': nrm(ks[8], (N_EVEN, POOL_GROUPS, POOL_GROUP_DIM, POOL_GROUP_DIM), POOL_GROUP_DIM ** -0.5),
        'pool_scale': 1.0 + nrm(ks[9], (N_EVEN, POOL_WIDTH), 0.1),
        'diff_lambda_params': nrm(ks[10], (N_EVEN, 4, DA_QK_DIM), 0.1),
        'subln_w': 1.0 + nrm(ks[11], (N_EVEN, DA_V_DIM), 0.1),
        'w_out_even': nrm(ks[12], (N_EVEN, EVEN_OUT_IN, D_MODEL), EVEN_OUT_IN ** -0.5 * DN_BETA),
        'w_in_odd': nrm(ks[13], (N_ODD, D_MODEL, ODD_IN), D_MODEL ** -0.5),
        'w_gate_a': nrm(ks[14], (N_ODD, D_MODEL, GLA_GATE_RANK), D_MODEL ** -0.5),
        'w_gate_b': nrm(ks[15], (N_ODD, GLA_GATE_RANK, GLA_HEADS * GLA_K_DIM), GLA_GATE_RANK ** -0.5),
        'b_gate': nrm(ks[16], (N_ODD, GLA_HEADS * GLA_K_DIM), 0.1),
        'gla_norm_w': 1.0 + nrm(ks[17], (N_ODD, GLA_V_DIM), 0.1),
        'w_out_odd': nrm(ks[18], (N_ODD, ODD_OUT_IN, D_MODEL), ODD_OUT_IN ** -0.5 * DN_BETA),
        'ln_g': 1.0 + nrm(ks[19], (DEPTH, D_MODEL), 0.1),
        'ln_b': nrm(ks[20], (DEPTH, D_MODEL), 0.1),
    }


def reference(x_prompt, x_sample, cache_k, cache_v, state_pool, state_gla, page_table,
              w_in_even, w_pool_lin, pool_scale, diff_lambda_params, subln_w, w_out_even,
              w_in_odd, w_gate_a, w_gate_b, b_gate, gla_norm_w, w_out_odd, ln_g, ln_b):
    n_seq, n_pages = page_table.shape
    past_len = n_pages * cache_k.shape[2]
    yp, ys = x_prompt, x_sample
    kp, vp, pp, gp = [], [], [], []
    ksm, vsm, psm, gsm = [], [], [], []
    for i in range(DEPTH):
        j = i // 2
        if i % 2 == 0:
            ew = (w_in_even[j], w_pool_lin[j], pool_scale[j], diff_lambda_params[j], subln_w[j],
                  w_out_even[j], ln_g[i], ln_b[i])
            yp, buf_p, k_p, v_p = even_layer(yp, 0, i, None, None, None, *ew)
            k_past = cache_k[page_table, j].reshape(n_seq, past_len, DA_HEADS, 2, DA_QK_DIM)
            v_past = cache_v[page_table, j].reshape(n_seq, past_len, DA_HEADS, DA_V_DIM)
            ys, buf_s, k_s, v_s = even_layer(ys, past_len, i, state_pool[:, j], k_past, v_past, *ew)
            kp.append(k_p)
            vp.append(v_p)
            pp.append(buf_p)
            ksm.append(k_s)
            vsm.append(v_s)
            psm.append(buf_s)
        else:
            ow = (w_in_odd[j], w_gate_a[j], w_gate_b[j], b_gate[j], gla_norm_w[j], w_out_odd[j], ln_g[i], ln_b[i])
            yp, s_p = odd_layer(yp, None, *ow)
            ys, s_s = odd_layer(ys, state_gla[:, j], *ow)
            gp.append(s_p)
            gsm.append(s_s)
    new_k_prompt = jnp.stack(kp, axis=1)
    new_v_prompt = jnp.stack(vp, axis=1)
    new_pool_prompt = jnp.stack(pp, axis=1)
    new_gla_prompt = jnp.stack(gp, axis=1)
    new_k_sample = jnp.stack(ksm, axis=1)
    new_v_sample = jnp.stack(vsm, axis=1)
    new_pool_sample = jnp.stack(psm, axis=1)
    new_gla_sample = jnp.stack(gsm, axis=1)
    return (yp, ys, new_k_prompt, new_v_prompt, new_pool_prompt, new_gla_prompt,
            new_k_sample, new_v_sample, new_pool_sample, new_gla_sample)
```

```python
import math
from contextlib import ExitStack
import numpy as np
import concourse.bass as bass
import concourse.mybir as mybir
from concourse.bass_utils import run_bass_kernel_spmd

F32 = mybir.dt.float32
BF16 = mybir.dt.bfloat16
I32 = mybir.dt.int32
AF = mybir.ActivationFunctionType
ALU = mybir.AluOpType

import os as _os
SAME_ENGINE_SYNC = bool(int(_os.environ.get("SES", "1")))
DBG_STAGE = int(_os.environ.get("DBG_STAGE", "0"))
DN_ALPHA = 8 ** 0.25
NE = 2312
NO = 2304


class Buf:
    __slots__ = ("name", "w", "r", "psum", "parent", "children")

    def __init__(self, name, parent=None):
        self.name = name
        self.w = None
        self.r = {}
        self.psum = False
        self.parent = parent
        self.children = []
        if parent is not None:
            parent.children.append(self)

    def fam(self):
        return [self] + self.children + ([self.parent] if self.parent is not None else [])


class Ctr:
    def __init__(self, key, sem, step):
        self.key = key
        self.sem = sem
        self.step = step
        self.count = 0


class Eng:
    def __init__(self, key, sem):
        self.key = key
        self.ctr = Ctr(key, sem, 1)
        self.prog = []
        self.waited = {}
        self.pending = False


class _Rec:
    def __getattr__(self, name):
        return lambda *a, **k: (name, a, k)


_REC = _Rec()


class Prog:
    def __init__(self, nc, sems, n_lanes=8):
        self.nc = nc
        self.ctrs = {}
        self.engs = {}
        it = iter(sems)
        for key in ("pe", "act", "dve", "pool", "sp"):
            e = Eng(key, next(it))
            self.engs[key] = e
            self.ctrs[key] = e.ctr
        self.lanes = []
        for i in range(n_lanes):
            c = Ctr("lane%d" % i, next(it), 16)
            self.lanes.append(c)
            self.ctrs[c.key] = c
        self.lane_rr = 0
        self.n_ins = 0

    def _deps(self, reads, writes):
        deps = {}
        for b0 in reads:
            for b in b0.fam():
                if b.w is not None:
                    k, c = b.w
                    deps[k] = max(deps.get(k, 0), c)
                if b0.psum:
                    for k, c in b.r.items():
                        deps[k] = max(deps.get(k, 0), c)
        for b0 in writes:
            for b in b0.fam():
                if b.w is not None:
                    k, c = b.w
                    deps[k] = max(deps.get(k, 0), c)
                for k, c in b.r.items():
                    deps[k] = max(deps.get(k, 0), c)
        return deps

    def _emit_waits(self, E, deps):
        for k, c in deps.items():
            if k == E.key:
                if not SAME_ENGINE_SYNC or E.key == "pe":
                    continue
                if c > E.ctr.count:
                    continue
            else:
                assert c <= self.ctrs[k].count, "dep on pending milestone %s" % k
            if E.waited.get(k, 0) >= c:
                continue
            E.waited[k] = c
            E.prog.append(("wait", self.ctrs[k].sem, c))

    def op(self, ek, fn, reads=(), writes=(), inc=True):
        E = self.engs[ek]
        self._emit_waits(E, self._deps(reads, writes))
        E.prog.append(("ins", fn(_REC), inc))
        self.n_ins += 1
        mark = E.ctr.count + 1
        if inc:
            E.ctr.count += 1
            E.pending = False
        else:
            E.pending = True
        for b in reads:
            b.r[E.key] = mark
        for b in writes:
            b.w = (E.key, mark)
            b.r = {}

    def dma(self, qk, fn, reads=(), writes=()):
        Q = self.engs[qk]
        lane = self.lanes[self.lane_rr]
        self.lane_rr = (self.lane_rr + 1) % len(self.lanes)
        deps = self._deps(reads, writes)
        if lane.count > 0:
            deps[lane.key] = max(deps.get(lane.key, 0), lane.count)
        self._emit_waits(Q, deps)
        Q.prog.append(("dma", fn(_REC), lane.sem))
        self.n_ins += 1
        lane.count += 16
        for b in reads:
            b.r[lane.key] = lane.count
        for b in writes:
            b.w = (lane.key, lane.count)
            b.r = {}

    def finish(self):
        for e in self.engs.values():
            assert not e.pending
        Q = self.engs["sp"]
        for k, c in self.ctrs.items():
            if k == "sp" or c.count == 0:
                continue
            if Q.waited.get(k, 0) < c.count:
                Q.prog.append(("wait", c.sem, c.count))

    def replay(self, block):
        engs = self.engs

        def run(E, eng):
            for item in E.prog:
                if item[0] == "wait":
                    eng.wait_ge(item[1], item[2])
                elif item[0] == "ins":
                    c = item[1]
                    ins = getattr(eng, c[0])(*c[1], **c[2])
                    if item[2]:
                        ins.then_inc(E.ctr.sem, 1)
                else:
                    c = item[1]
                    ins = getattr(eng, c[0])(*c[1], **c[2])
                    ins.then_inc(item[2], 16)

        @block.tensor
        def _(e):
            run(engs["pe"], e)

        @block.scalar
        def _(e):
            run(engs["act"], e)

        @block.vector
        def _(e):
            run(engs["dve"], e)

        @block.gpsimd
        def _(e):
            run(engs["pool"], e)

        @block.sync
        def _(e):
            run(engs["sp"], e)


class T:
    def __init__(self, t, name):
        self.t = t
        self.b = Buf(name)

    def __getitem__(self, k):
        return self.t[k]


def build(NPT=8, groups=((0, 1), (2, 3)), do_sample=True, n_layers_out=4, NPG=2560):
    nc = bass.Bass("TRN2", target_bir_lowering=False)
    SEQ = NPT * 512

    def din(name, shape, dt=F32):
        return nc.dram_tensor(name, list(shape), dt, kind="ExternalInput").ap()

    def dout(name, shape, dt=F32):
        return nc.dram_tensor(name, list(shape), dt, kind="ExternalOutput").ap()

    xp = din("xp", [SEQ, 1024])
    xs = din("xs", [128, 1024])
    ck = din("ck", [NPG * 2 * 128, 512])
    cv = din("cv", [NPG * 2 * 128, 512])
    spst = din("spst", [16, 2, 15, 512])
    sgst = din("sgst", [16, 2, 4, 128, 256])
    ptb_d = din("ptb", [128, 256], I32)
    wie = din("wie", [2, 1024, 3072])
    woe = din("woe", [2, 1024, 1024])
    wio = din("wio", [2, 1024, 3072])
    woo = din("woo", [2, 1024, 1024])
    wpl = din("wpl", [2, 4, 128, 128])
    wga = din("wga", [2, 1024, 16])
    wgb = din("wgb", [2, 32, 512])
    ble = din("ble", [2, 128, NE])
    blo = din("blo", [2, 128, NO])
    ident_d = din("ident", [128, 128])
    tri_d = din("tri", [128, 128])
    triblk_d = din("triblk", [128, 128])
    seqm_d = din("seqm", [128, 16])
    iota_d = din("iota", [128, 1])
    cosp = din("cosp", [SEQ, 8])
    sinp = din("sinp", [SEQ, 8])
    coss = din("coss", [128, 8])
    sins = din("sins", [128, 8])
    rc0_d = din("rc0", [128, 64])

    yp = dout("yp", [SEQ, 1024])
    ys = dout("ys", [128, 1024])
    nkp = dout("nkp", [2, SEQ, 512])
    nvp = dout("nvp", [2, SEQ, 512])
    npp = dout("npp", [2, 15, 512])
    ngp = dout("ngp", [2, 4, 128, 256])
    nks = dout("nks", [16, 2, 8, 512])
    nvs = dout("nvs", [16, 2, 8, 512])
    nps = dout("nps", [16, 2, 15, 512])
    ngs = dout("ngs", [16, 2, 4, 128, 256])
    xmp = nc.dram_tensor("xmp", [SEQ, 1024], F32, kind="Internal").ap()
    xms = nc.dram_tensor("xms", [128, 1024], F32, kind="Internal").ap()

    es = ExitStack()
    with es:
        def sb(name, shape, dt=F32):
            return T(es.enter_context(nc.sbuf_tensor(name, list(shape), dt)), name)

        def psb(name, shape, dt=F32):
            t_ = T(es.enter_context(nc.psum_tensor(name, list(shape), dt)), name)
            t_.b.psum = True
            return t_

        sems = [es.enter_context(nc.semaphore("s%d" % i)) for i in range(5 + 10)]
        P = Prog(nc, sems, n_lanes=10)

        x_tok = sb("x_tok", [128, 4, 1024])
        xT = sb("xT", [128, 8, 512], BF16)
        wsl = [sb("wsl%d" % i, [128, 8, 512], BF16) for i in range(2)]
        blob = sb("blob", [128, NE])
        mixT = sb("mixT", [128, 8, 512], BF16)
        KT = sb("KT", [128, 4, max(SEQ, 4096)], BF16)
        VX = sb("VX", [128, max(SEQ // 128, 32), 4, 128], BF16)
        ident = sb("ident_s", [128, 128])
        tri = sb("tri_s", [128, 128])
        tri_bf = sb("tri_bf", [128, 128], BF16)
        triblk = sb("triblk_s", [128, 128])
        triblk_bf = sb("triblk_bf", [128, 128], BF16)
        ones_bf = sb("ones_bf", [128, 128], BF16)
        seqm = sb("seqm_s", [128, 16])
        iota = sb("iota_s", [128, 1])
        cs = sb("cs", [128, 4, 16])
        rc0 = sb("rc0_s", [128, 64])
        wplb = sb("wplb", [128, 4, 128], BF16)
        puT = sb("puT", [128, 4, 15 + 512])
        tf = [sb("tf%d" % i, [128, 1024]) for i in range(5)]
        tb = [sb("tb%d" % i, [128, 4, 512], BF16) for i in range(6)]
        PTb = [Buf("pt%d" % i, parent=tb[4].b) for i in range(4)]
        zr = sb("zr", [128, 32])
        ones_f = sb("ones_f", [128, 128])
        sm = sb("sm", [128, 64])
        rt = sb("rt", [128, 4, 64])
        Sst = sb("Sst", [128, 4, 256])
        Sbf = sb("Sbf", [128, 4, 256], BF16)
        wgab = sb("wgab", [128, 8, 16], BF16)
        wgbb = sb("wgbb", [32, 512], BF16)
        zT = sb("zT", [32, 512], BF16)
        ptb = sb("ptb_s", [128, 256], I32)
        ptf = sb("ptf", [128, 256])
        idx = sb("idx", [128, 2, 256], I32)
        kpg = [sb("kpg%d" % i, [128, 512]) for i in range(1)] * 2
        vpg = [sb("vpg%d" % i, [128, 512]) for i in range(1)] * 2
        KTs = sb("KTs", [128, 4, 128], BF16)
        Vs = sb("Vs", [128, 4, 128], BF16)

        bk = [psb("bk%d" % i, [128, 512]) for i in range(8)]
        pS2 = [[bk[0], bk[1]], [bk[2], bk[3]]]
        pS = [bk[0], bk[2]]
        pO = [bk[4], bk[5]]
        pZ = [bk[6], bk[7]]
        gpool = [bk, 0]

        def set_gen(pool):
            gpool[0] = pool
            gpool[1] = 0

        def gen():
            gpool[1] = (gpool[1] + 1) % len(gpool[0])
            return gpool[0][gpool[1]]

        dma = P.dma
        op = P.op

        for (tt, dd) in ((ident, ident_d), (tri, tri_d), (triblk, triblk_d), (seqm, seqm_d), (iota, iota_d), (rc0, rc0_d)):
            dma("sp", lambda e, tt=tt, dd=dd: e.dma_start(out=tt[:], in_=dd[:, :]), writes=[tt.b])
        dma("sp", lambda e: e.dma_start(out=ptb[:], in_=ptb_d[:, :]), writes=[ptb.b])
        op("dve", lambda e: e.tensor_copy(out=tri_bf[:], in_=tri[:]), reads=[tri.b], writes=[tri_bf.b])
        op("dve", lambda e: e.tensor_copy(out=triblk_bf[:], in_=triblk[:]), reads=[triblk.b], writes=[triblk_bf.b])
        op("dve", lambda e: e.memset(ones_bf[:], 1.0), writes=[ones_bf.b])
        op("dve", lambda e: e.memset(ones_f[:], 1.0), writes=[ones_f.b])
        op("dve", lambda e: e.memset(tb[1][:], 0.0), writes=[tb[1].b])
        op("dve", lambda e: e.memset(tb[2][:], 0.0), writes=[tb[2].b])
        op("dve", lambda e: e.memset(zT[:], 1.0), writes=[zT.b])
        op("dve", lambda e: e.tensor_copy(out=ptf[:], in_=ptb[:]), reads=[ptb.b], writes=[ptf.b])
        op("dve", lambda e: e.tensor_scalar(out=ptf[:], in0=ptf[:], scalar1=256.0, scalar2=iota[:, 0:1], op0=ALU.mult, op1=ALU.add),
           reads=[ptf.b, iota.b], writes=[ptf.b])
        op("dve", lambda e: e.tensor_copy(out=idx[:, 0, :], in_=ptf[:]), reads=[ptf.b], writes=[idx.b])
        op("dve", lambda e: e.tensor_scalar(out=ptf[:], in0=ptf[:], scalar1=128.0, scalar2=None, op0=ALU.add), reads=[ptf.b], writes=[ptf.b])
        op("dve", lambda e: e.tensor_copy(out=idx[:, 1, :], in_=ptf[:]), reads=[ptf.b], writes=[idx.b])

        wslot_i = [0]

        def load_w(src_ap):
            w = wsl[wslot_i[0]]
            wslot_i[0] ^= 1
            dma("pool", lambda e, w=w, src_ap=src_ap: e.dma_start(out=w[:], in_=src_ap.rearrange("(kc p) n -> p kc n", p=128)),
                writes=[w.b])
            return w

        def proj_F(w, m, out_ps, TS):
            for kc in range(8):
                op("pe", lambda e, kc=kc: e.matmul(out_ps[:, 0:TS], lhsT=w[:, kc, m * 128:(m + 1) * 128], rhs=xT[:, kc, 0:TS],
                                                   start=(kc == 0), stop=(kc == 7)),
                   reads=[w.b, xT.b], writes=[out_ps.b], inc=(kc == 7))

        def proj_T(w, s, out_ps, src=None, srcb=None):
            src = xT if src is None else src
            for kc in range(8):
                op("pe", lambda e, kc=kc: e.matmul(out_ps[:, :], lhsT=src[:, kc, s * 128:(s + 1) * 128], rhs=w[:, kc, :],
                                                   start=(kc == 0), stop=(kc == 7)),
                   reads=[w.b, src.b], writes=[out_ps.b], inc=(kc == 7))

        def transpose_to(dst_ap, dst_b, src_ap, src_b, rows=128, cols=128, eng="dve"):
            ps = gen()
            op("pe", lambda e: e.transpose(out=ps[0:cols, 0:rows], in_=src_ap, identity=ident[0:rows, 0:rows]),
               reads=[src_b, ident.b], writes=[ps.b])
            if eng == "dve":
                op("dve", lambda e: e.tensor_copy(out=dst_ap, in_=ps[0:cols, 0:rows]), reads=[ps.b], writes=[dst_b])
            else:
                op("act", lambda e: e.activation(out=dst_ap, in_=ps[0:cols, 0:rows], func=AF.Copy), reads=[ps.b], writes=[dst_b])

        def transpose4(dst_ap3, dst_b, src_aps, src_b, eng="dve"):
            ps = gen()
            for i, sa in enumerate(src_aps):
                op("pe", lambda e, i=i, sa=sa: e.transpose(out=ps[:, i * 128:(i + 1) * 128], in_=sa, identity=ident[:, :]),
                   reads=[src_b, ident.b], writes=[ps.b])
            src3 = ps[:, :].rearrange("p (g i) -> p g i", g=4)
            if eng == "dve":
                op("dve", lambda e: e.tensor_copy(out=dst_ap3, in_=src3), reads=[ps.b], writes=[dst_b])
            else:
                op("act", lambda e: e.activation(out=dst_ap3, in_=src3, func=AF.Copy), reads=[ps.b], writes=[dst_b])

        def make_xT(nsub):
            k = 0
            for s in range(nsub):
                for k4 in range(2):
                    transpose4(xT[:, k4 * 4:(k4 + 1) * 4, s * 128:(s + 1) * 128], xT.b,
                               [x_tok[:, s, kc * 128:(kc + 1) * 128] for kc in range(k4 * 4, k4 * 4 + 4)], x_tok.b,
                               eng=("dve" if k % 2 == 0 else "act"))
                    k += 1

        def layer_norm(nsub, goff, boff):
            for s in range(nsub):
                st6 = tf[4]
                for c in range(2):
                    op("dve", lambda e, c=c: e.bn_stats(out=st6[:, c * 6:(c + 1) * 6], in_=x_tok[:, s, c * 512:(c + 1) * 512]),
                       reads=[x_tok.b], writes=[st6.b])
                op("dve", lambda e: e.bn_aggr(out=sm[:, 0:2], in_=st6[:, 0:12].rearrange("p (c k) -> p c k", c=2)),
                   reads=[st6.b], writes=[sm.b])
                op("act", lambda e: e.activation(out=sm[:, 2:3], in_=sm[:, 1:2], func=AF.Sqrt, bias=1e-5, scale=1.0), reads=[sm.b], writes=[sm.b])
                op("dve", lambda e: e.reciprocal(out=sm[:, 2:3], in_=sm[:, 2:3]), reads=[sm.b], writes=[sm.b])
                op("dve", lambda e: e.tensor_scalar(out=x_tok[:, s, :], in0=x_tok[:, s, :], scalar1=sm[:, 0:1], scalar2=sm[:, 2:3],
                                                    op0=ALU.subtract, op1=ALU.mult), reads=[x_tok.b, sm.b], writes=[x_tok.b])
                op("pool", lambda e: e.tensor_tensor(out=x_tok[:, s, :], in0=x_tok[:, s, :], in1=blob[:, goff:goff + 1024], op=ALU.mult),
                   reads=[x_tok.b, blob.b], writes=[x_tok.b])
                op("pool", lambda e: e.tensor_tensor(out=x_tok[:, s, :], in0=x_tok[:, s, :], in1=blob[:, boff:boff + 1024], op=ALU.add),
                   reads=[x_tok.b, blob.b], writes=[x_tok.b])

        def out_proj(w_dram, nsub):
            for nb in range(2):
                w = load_w(w_dram[:, nb * 512:(nb + 1) * 512])
                for s in range(nsub):
                    ps = gen()
                    proj_T(w, s, ps, src=mixT)
                    op("dve", lambda e, s=s, nb=nb, ps=ps: e.scalar_tensor_tensor(
                        out=x_tok[:, s, nb * 512:(nb + 1) * 512], in0=x_tok[:, s, nb * 512:(nb + 1) * 512], scalar=DN_ALPHA,
                        in1=ps[:, :], op0=ALU.mult, op1=ALU.add), reads=[x_tok.b, ps.b], writes=[x_tok.b])

        def even_layer(j, li, tile_i, is_sample):
            nsub = 1 if is_sample else 4
            TS = 128 * nsub
            nseg = 16 if is_sample else 1
            seglen = 8 if is_sample else 512
            dma("sp", lambda e: e.dma_start(out=blob[:, 0:NE], in_=ble[j, :, :]), writes=[blob.b])
            dma("pool", lambda e: e.dma_start(out=wplb[:], in_=wpl[j].rearrange("g c e -> c g e")), writes=[wplb.b])
            LAM, PSC, SUBW = 2048, 2304, 2308
            lam_init = 0.8 - 0.6 * math.exp(-0.3 * li)
            op("dve", lambda e: e.tensor_tensor(out=rt[:, 0, :], in0=blob[:, LAM:LAM + 64], in1=blob[:, LAM + 64:LAM + 128], op=ALU.mult),
               reads=[blob.b], writes=[rt.b])
            op("dve", lambda e: e.tensor_tensor(out=rt[:, 1, :], in0=blob[:, LAM + 128:LAM + 192], in1=blob[:, LAM + 192:LAM + 256], op=ALU.mult),
               reads=[blob.b], writes=[rt.b])
            op("act", lambda e: e.activation(out=rt[:, 2, :], in_=rt[:, 0, :], func=AF.Copy, accum_out=sm[:, 4:5]), reads=[rt.b], writes=[rt.b, sm.b])
            op("act", lambda e: e.activation(out=rt[:, 2, :], in_=rt[:, 1, :], func=AF.Copy, accum_out=sm[:, 5:6]), reads=[rt.b], writes=[rt.b, sm.b])
            op("act", lambda e: e.activation(out=sm[:, 6:8], in_=sm[:, 4:6], func=AF.Exp), reads=[sm.b], writes=[sm.b])
            op("dve", lambda e: e.scalar_tensor_tensor(out=sm[:, 8:9], in0=sm[:, 6:7], scalar=lam_init, in1=sm[:, 7:8], op0=ALU.add, op1=ALU.subtract),
               reads=[sm.b], writes=[sm.b])
            op("dve", lambda e: e.tensor_scalar(out=sm[:, 9:10], in0=blob[:, SUBW:SUBW + 1], scalar1=(1.0 - lam_init), scalar2=None, op0=ALU.mult),
               reads=[blob.b], writes=[sm.b])
            if is_sample:
                dma("sp", lambda e: e.dma_start(out=cs[:, 0, 0:8], in_=coss[:, :]), writes=[cs.b])
                dma("sp", lambda e: e.dma_start(out=cs[:, 0, 8:16], in_=sins[:, :]), writes=[cs.b])
            else:
                r0 = tile_i * 512
                dma("sp", lambda e: e.dma_start(out=cs[:, :, 0:8], in_=cosp[r0:r0 + 512, :].rearrange("(s p) d -> p s d", p=128)), writes=[cs.b])
                dma("sp", lambda e: e.dma_start(out=cs[:, :, 8:16], in_=sinp[r0:r0 + 512, :].rearrange("(s p) d -> p s d", p=128)), writes=[cs.b])

            W = wie[j]
            pgT, qT0, qT1, agT, PTa, dbf = tb
            set_gen(bk)
            op("dve", lambda e: e.memset(qT0[64:128, :, :], 0.0), writes=[qT0.b])
            op("dve", lambda e: e.memset(qT1[0:64, :, :], 0.0), writes=[qT1.b])
            w = load_w(W[:, 512:1024])
            for m in range(4):
                ps = gen()
                proj_F(w, m, ps, TS)
                op("act", lambda e, m=m, ps=ps: e.activation(out=pgT[:, m, 0:TS], in_=ps[:, 0:TS], func=AF.Silu), reads=[ps.b], writes=[pgT.b])
            if DBG_STAGE == 1:
                return
            w = load_w(W[:, 0:512])
            if is_sample:
                for half in range(2):
                    hb = tf[0]
                    for q8 in range(8):
                        dma("sp", lambda e, half=half, q8=q8: e.dma_start(out=hb[q8 * 15:(q8 + 1) * 15, 0:512], in_=spst[half * 8 + q8, j, :, :]),
                            writes=[hb.b])
                    for g in range(4):
                        ps = gen()
                        op("pe", lambda e, g=g, ps=ps: e.transpose(out=ps[:, 0:120], in_=hb[0:120, g * 128:(g + 1) * 128], identity=ident[0:120, 0:120]),
                           reads=[hb.b, ident.b], writes=[ps.b])
                        op("dve", lambda e, g=g, ps=ps, half=half: e.tensor_copy(
                            out=puT[:, g, 0:368].rearrange("p (s t) -> p s t", t=23)[:, half * 8:(half + 1) * 8, 0:15],
                            in_=ps[:, 0:120].rearrange("p (s t) -> p s t", t=15)), reads=[ps.b], writes=[puT.b])
                    dma("sp", lambda e, half=half: e.dma_start(out=nps[half * 8:(half + 1) * 8, j, 0:7, :], in_=spst[half * 8:(half + 1) * 8, j, 8:15, :]))
            elif tile_i == 0:
                op("dve", lambda e: e.memset(puT[:, :, 0:15], 0.0), writes=[puT.b])
            for g in range(4):
                ps = gen()
                proj_F(w, g, ps, TS)
                if is_sample:
                    op("act", lambda e, g=g, ps=ps: e.activation(out=puT[:, g, 0:368].rearrange("p (s t) -> p s t", t=23)[:, :, 15:23],
                                                                 in_=ps[:, 0:128].rearrange("p (s t) -> p s t", t=8), func=AF.Copy),
                       reads=[ps.b], writes=[puT.b])
                else:
                    op("act", lambda e, g=g, ps=ps: e.activation(out=puT[:, g, 15:527], in_=ps[:, 0:512], func=AF.Copy), reads=[ps.b], writes=[puT.b])
            if is_sample:
                ps = gen()
                proj_T(w, 0, ps)
                ob = tf[1]
                op("act", lambda e, ps=ps: e.activation(out=ob[:, 0:512], in_=ps[:, :], func=AF.Copy), reads=[ps.b], writes=[ob.b])
                for sq in range(16):
                    dma("sp", lambda e, sq=sq: e.dma_start(out=nps[sq, j, 7:15, :], in_=ob[sq * 8:(sq + 1) * 8, 0:512]), reads=[ob.b])
            elif tile_i == NPT - 1:
                ps = gen()
                for kc in range(8):
                    op("pe", lambda e, kc=kc, ps=ps: e.matmul(ps[0:15, :], lhsT=xT[:, kc, 497:512], rhs=w[:, kc, :], start=(kc == 0), stop=(kc == 7)),
                       reads=[w.b, xT.b], writes=[ps.b], inc=(kc == 7))
                ob = tf[1]
                op("act", lambda e, ps=ps: e.activation(out=ob[0:15, 0:512], in_=ps[0:15, :], func=AF.Copy), reads=[ps.b], writes=[ob.b])
                dma("sp", lambda e: e.dma_start(out=npp[j, :, :], in_=ob[0:15, 0:512]), reads=[ob.b])
            L = 23 if is_sample else 527

            def seg(ap2d):
                return ap2d.rearrange("p (s t) -> p s t", t=L)

            for g in range(4):
                wnd = (2, 4, 8, 16)[g]
                cur = seg(puT[:, g, 0:nseg * L])
                curb = puT.b
                tmpi = 0
                sh = 1
                while sh < wnd:
                    dst_t = tf[tmpi]
                    dst = seg(dst_t[:, 0:nseg * L])
                    op("dve", lambda e, dst=dst, cur=cur, sh=sh: e.tensor_tensor(out=dst[:, :, sh:L], in0=cur[:, :, sh:L], in1=cur[:, :, 0:L - sh], op=ALU.add),
                       reads=[curb], writes=[dst_t.b])
                    if sh < 15:
                        op("dve", lambda e, dst=dst, cur=cur, sh=sh: e.tensor_copy(out=dst[:, :, 0:sh], in_=cur[:, :, 0:sh]), reads=[curb], writes=[dst_t.b])
                    cur, curb = dst, dst_t.b
                    tmpi ^= 1
                    sh *= 2
                dd = tf[2]
                ddv = dd[:, 0:TS].rearrange("p (s t) -> p s t", t=seglen)
                uv = seg(puT[:, g, 0:nseg * L])[:, :, 15:L]
                op("dve", lambda e, ddv=ddv, cur=cur, uv=uv, wnd=wnd: e.scalar_tensor_tensor(
                    out=ddv, in0=cur[:, :, 15:L], scalar=1.0 / wnd, in1=uv, op0=ALU.mult, op1=ALU.subtract),
                   reads=[curb, puT.b], writes=[dd.b])
                if (not is_sample) and tile_i == 0:
                    op("dve", lambda e, cur=cur, g=g: e.tensor_tensor(out=rt[:, 3, 0:16], in0=cur[:, 0, 15:31], in1=rc0[:, g * 16:(g + 1) * 16], op=ALU.mult),
                       reads=[curb, rc0.b], writes=[rt.b])
                    op("dve", lambda e, g=g: e.tensor_tensor(out=dd[:, 0:16], in0=rt[:, 3, 0:16], in1=puT[:, g, 15:31], op=ALU.subtract),
                       reads=[rt.b, puT.b], writes=[dd.b])
                op("act", lambda e, g=g: e.activation(out=dbf[:, g, 0:TS], in_=dd[:, 0:TS], func=AF.Copy), reads=[dd.b], writes=[dbf.b])
                ps = gen()
                op("pe", lambda e, g=g, ps=ps: e.matmul(ps[:, 0:TS], lhsT=wplb[:, g, :], rhs=dbf[:, g, 0:TS], start=True, stop=True),
                   reads=[wplb.b, dbf.b], writes=[ps.b])
                op("dve", lambda e, g=g, ps=ps: e.scalar_tensor_tensor(out=mixT[:, g, 0:TS], in0=ps[:, 0:TS], scalar=blob[:, PSC + g:PSC + g + 1],
                                                                       in1=pgT[:, g, 0:TS], op0=ALU.mult, op1=ALU.mult),
                   reads=[ps.b, blob.b, pgT.b], writes=[mixT.b])
            if not is_sample:
                op("dve", lambda e: e.tensor_copy(out=puT[:, :, 0:15], in_=puT[:, :, 512:527]), reads=[puT.b], writes=[puT.b])

            def rotary(src_ps, dst, s, scale):
                op("act", lambda e: e.activation(out=dst[:, 0:512], in_=src_ps[:, :], func=AF.Copy, scale=scale), reads=[src_ps.b], writes=[dst.b])
                dv = dst[:, 0:512].rearrange("p (g d) -> p g d", d=64)
                x1 = dv[:, :, 0:8]
                x2 = dv[:, :, 8:16]
                cosb = cs[:, s, 0:8].unsqueeze(1).broadcast_to([128, 8, 8])
                sinb = cs[:, s, 8:16].unsqueeze(1).broadcast_to([128, 8, 8])
                r = [rt[:, i, :].rearrange("p (g d) -> p g d", d=8) for i in range(4)]
                op("dve", lambda e: e.tensor_tensor(out=r[0], in0=x1, in1=cosb, op=ALU.mult), reads=[dst.b, cs.b], writes=[rt.b])
                op("dve", lambda e: e.tensor_tensor(out=r[1], in0=x2, in1=sinb, op=ALU.mult), reads=[dst.b, cs.b], writes=[rt.b])
                op("dve", lambda e: e.tensor_tensor(out=r[2], in0=x2, in1=cosb, op=ALU.mult), reads=[dst.b, cs.b], writes=[rt.b])
                op("dve", lambda e: e.tensor_tensor(out=r[3], in0=x1, in1=sinb, op=ALU.mult), reads=[dst.b, cs.b], writes=[rt.b])
                op("dve", lambda e: e.tensor_tensor(out=x1, in0=r[0], in1=r[1], op=ALU.subtract), reads=[rt.b], writes=[dst.b])
                op("dve", lambda e: e.tensor_tensor(out=x2, in0=r[2], in1=r[3], op=ALU.add), reads=[rt.b], writes=[dst.b])

            if DBG_STAGE == 2:
                return
            kbase = 0 if is_sample else tile_i * 512
            w = load_w(W[:, 1024:1536])
            for s in range(nsub):
                ps = gen()
                proj_T(w, s, ps)
                qt = tf[0]
                rotary(ps, qt, s, 0.125)
                for h in range(4):
                    ps2 = gen()
                    op("pe", lambda e, h=h, ps2=ps2: e.transpose(out=ps2[:, 0:128], in_=qt[:, h * 128:(h + 1) * 128], identity=ident[:, :]),
                       reads=[qt.b, ident.b], writes=[ps2.b])
                    op("dve", lambda e, h=h, s=s, ps2=ps2: e.tensor_copy(out=qT0[0:64, h, s * 128:(s + 1) * 128], in_=ps2[0:64, 0:128]),
                       reads=[ps2.b], writes=[qT0.b])
                    if DBG_STAGE != 7:
                        op("act", lambda e, h=h, s=s, ps2=ps2: e.activation(out=qT1[64:128, h, s * 128:(s + 1) * 128], in_=ps2[64:128, 0:128], func=AF.Copy),
                           reads=[ps2.b], writes=[qT1.b])
            if DBG_STAGE in (6, 7):
                return
            w = load_w(W[:, 1536:2048])
            KTd = KTs if is_sample else KT
            for s in range(nsub):
                ps = gen()
                proj_T(w, s, ps)
                kt = tf[1]
                rotary(ps, kt, s, 1.0)
                if is_sample:
                    for sq in range(16):
                        dma("sp", lambda e, sq=sq: e.dma_start(out=nks[sq, j, :, :], in_=kt[sq * 8:(sq + 1) * 8, 0:512]), reads=[kt.b])
                else:
                    r0 = tile_i * 512 + s * 128
                    dma("sp", lambda e, r0=r0: e.dma_start(out=nkp[j, r0:r0 + 128, :], in_=kt[:, 0:512]), reads=[kt.b])
                transpose4(KTd[:, :, kbase + s * 128:kbase + (s + 1) * 128], KTd.b, [kt[:, h * 128:(h + 1) * 128] for h in range(4)], kt.b,
                           eng=("dve" if s % 2 == 0 else "act"))
            if DBG_STAGE == 8:
                return
            w = load_w(W[:, 2048:2560])
            for s in range(nsub):
                ps = gen()
                proj_T(w, s, ps)
                vt = tf[3]
                op("act", lambda e, ps=ps: e.activation(out=vt[:, 0:512], in_=ps[:, :], func=AF.Copy), reads=[ps.b], writes=[vt.b])
                if is_sample:
                    op("dve", lambda e, ps=ps: e.tensor_copy(out=Vs[:].rearrange("p h e -> p (h e)"), in_=ps[:, :]), reads=[ps.b], writes=[Vs.b])
                    for sq in range(16):
                        dma("sp", lambda e, sq=sq: e.dma_start(out=nvs[sq, j, :, :], in_=vt[sq * 8:(sq + 1) * 8, 0:512]), reads=[vt.b])
                else:
                    blk = tile_i * 4 + s
                    op("dve", lambda e, ps=ps, blk=blk: e.tensor_copy(out=VX[:, blk, :, :].rearrange("p h e -> p (h e)"), in_=ps[:, :]),
                       reads=[ps.b], writes=[VX.b])
                    r0 = tile_i * 512 + s * 128
                    dma("sp", lambda e, r0=r0: e.dma_start(out=nvp[j, r0:r0 + 128, :], in_=vt[:, 0:512]), reads=[vt.b])
            w = load_w(W[:, 2560:3072])
            for m in range(4):
                ps = gen()
                proj_F(w, m, ps, TS)
                op("act", lambda e, m=m, ps=ps: e.activation(out=agT[:, m, 0:TS], in_=ps[:, 0:TS], func=AF.Silu), reads=[ps.b], writes=[agT.b])

            if DBG_STAGE == 3:
                return
            qTc = (qT0, qT1)

            def score_block(h, klhsT, klb, c0, kmask=None, pt_i=0):
                for c in range(2):
                    psc = pS2[c][pt_i]
                    ptb_ = PTb[pt_i * 2 + c]
                    op("pe", lambda e, c=c, psc=psc: e.matmul(psc[:, c0:TS], lhsT=klhsT, rhs=qTc[c][:, h, c0:TS], start=True, stop=True),
                       reads=[klb, qTc[c].b], writes=[psc.b])
                    op("act", lambda e, c=c, psc=psc: e.activation(out=PTa[:, pt_i * 2 + c, c0:TS], in_=psc[:, c0:TS], func=AF.Exp),
                       reads=[psc.b], writes=[ptb_])
                    if kmask is not None:
                        op("pool", lambda e, c=c: e.tensor_tensor(out=PTa[:, pt_i * 2 + c, c0:c0 + 128], in0=PTa[:, pt_i * 2 + c, c0:c0 + 128],
                                                                  in1=kmask[:, :], op=ALU.mult), reads=[ptb_, kmask.b], writes=[ptb_])

            def av_block(vrhs, vb, c0, first, last, pt_i=0):
                for c in range(2):
                    ptb_ = PTb[pt_i * 2 + c]
                    op("pe", lambda e, c=c: e.matmul(pO[c][:, c0:TS], lhsT=vrhs, rhs=PTa[:, pt_i * 2 + c, c0:TS], start=first, stop=last),
                       reads=[vb, ptb_], writes=[pO[c].b])
                    op("pe", lambda e, c=c: e.matmul(pZ[c][:, c0:TS], lhsT=ones_bf[:, :], rhs=PTa[:, pt_i * 2 + c, c0:TS], start=first, stop=last),
                       reads=[ones_bf.b, ptb_], writes=[pZ[c].b])

            def finalize(h, c0, c1, src=None):
                n = c1 - c0
                a1, a2, a3 = tf[0], tf[1], tf[2]
                if src is None:
                    src = [(pO[0][:, c0:c1], pO[0].b), (pO[1][:, c0:c1], pO[1].b), (pZ[0][:, c0:c1], pZ[0].b), (pZ[1][:, c0:c1], pZ[1].b)]
                (o1, o1b), (o2, o2b), (z1, z1b), (z2, z2b) = src
                op("dve", lambda e: e.reciprocal(out=a1[:, 0:n], in_=z1), reads=[z1b], writes=[a1.b])
                op("dve", lambda e: e.reciprocal(out=a2[:, 0:n], in_=z2), reads=[z2b], writes=[a2.b])
                op("dve", lambda e: e.tensor_tensor(out=a1[:, 0:n], in0=a1[:, 0:n], in1=o1, op=ALU.mult), reads=[a1.b, o1b], writes=[a1.b])
                op("dve", lambda e: e.tensor_tensor(out=a2[:, 0:n], in0=a2[:, 0:n], in1=o2, op=ALU.mult), reads=[a2.b, o2b], writes=[a2.b])
                op("dve", lambda e: e.scalar_tensor_tensor(out=a1[:, 0:n], in0=a2[:, 0:n], scalar=sm[:, 8:9], in1=a1[:, 0:n], op0=ALU.mult, op1=ALU.subtract),
                   reads=[a1.b, a2.b, sm.b], writes=[a1.b])
                sq = dbf
                op("act", lambda e: e.activation(out=sq[:, 0, 0:n], in_=a1[:, 0:n], func=AF.Square), reads=[a1.b], writes=[sq.b])
                ps = gen()
                op("pe", lambda e: e.matmul(ps[:, 0:n], lhsT=ones_bf[:, :], rhs=sq[:, 0, 0:n], start=True, stop=True), reads=[ones_bf.b, sq.b], writes=[ps.b])
                op("act", lambda e: e.activation(out=a3[:, 0:n], in_=ps[:, 0:n], func=AF.Sqrt, bias=1e-5, scale=1.0 / 128.0), reads=[ps.b], writes=[a3.b])
                op("dve", lambda e: e.reciprocal(out=a3[:, 0:n], in_=a3[:, 0:n]), reads=[a3.b], writes=[a3.b])
                op("dve", lambda e: e.scalar_tensor_tensor(out=a1[:, 0:n], in0=a1[:, 0:n], scalar=-1.0, in1=a3[:, 0:n], op0=ALU.mult, op1=ALU.mult),
                   reads=[a1.b, a3.b], writes=[a1.b])
                op("dve", lambda e: e.scalar_tensor_tensor(out=mixT[:, 4 + h, c0:c1], in0=a1[:, 0:n], scalar=sm[:, 9:10], in1=agT[:, h, c0:c1],
                                                           op0=ALU.mult, op1=ALU.mult), reads=[a1.b, sm.b, agT.b], writes=[mixT.b])

            set_gen(bk[0:4])
            if not is_sample:
                nkb = tile_i * 4 + 4
                for h in range(4):
                    prev = None
                    for kb in range(nkb):
                        jl = kb - tile_i * 4
                        c0 = 0 if jl < 0 else jl * 128
                        score_block(h, KT[:, h, kb * 128:(kb + 1) * 128], KT.b, c0, kmask=(tri_bf if jl >= 0 else None), pt_i=kb % 2)
                        if prev is not None:
                            av_block(*prev)
                        prev = (VX[:, kb, h, :], VX.b, c0, (kb == 0), (kb == nkb - 1), kb % 2)
                    av_block(*prev)
                    finalize(h, 0, 512)
            else:
                KTq = KT
                Oa = (tf[3], tf[4])
                pi = 0
                stg = [(x_tok[:, 1, 0:512], Buf("stg0", parent=x_tok.b)), (x_tok[:, 1, 512:1024], Buf("stg1", parent=x_tok.b)),
                       (x_tok[:, 2, 0:512], Buf("stg2", parent=x_tok.b)), (x_tok[:, 2, 512:1024], Buf("stg3", parent=x_tok.b)),
                       (x_tok[:, 3, 0:512], Buf("stg4", parent=x_tok.b)), (x_tok[:, 3, 512:1024], Buf("stg5", parent=x_tok.b))]
                KTh = [Buf("kth0", parent=KT.b), Buf("kth1", parent=KT.b)]
                VXh = [Buf("vxh0", parent=VX.b), Buf("vxh1", parent=VX.b)]
                for sq in range(16):
                    sp_ = sq % 2
                    kof = sp_ * 2048
                    vof = sp_ * 16
                    for pgi in range(16):
                        n = sq * 16 + pgi
                        kp, kpb = stg[(pi % 3) * 2]
                        vp, vpb = stg[(pi % 3) * 2 + 1]
                        pi += 1
                        dma("pool", lambda e, kp=kp, n=n: e.indirect_dma_start(
                            out=kp, out_offset=None, in_=ck, in_offset=bass.IndirectOffsetOnAxis(ap=idx[:, j, n:n + 1], axis=0)),
                            reads=[idx.b], writes=[kpb])
                        dma("pool", lambda e, vp=vp, n=n: e.indirect_dma_start(
                            out=vp, out_offset=None, in_=cv, in_offset=bass.IndirectOffsetOnAxis(ap=idx[:, j, n:n + 1], axis=0)),
                            reads=[idx.b], writes=[vpb])
                        transpose4(KTq[:, :, kof + pgi * 128:kof + (pgi + 1) * 128], KTh[sp_], [kp[:, h * 128:(h + 1) * 128] for h in range(4)], kpb,
                                   eng=("dve" if pgi % 2 == 0 else "act"))
                        if pgi % 2 == 0:
                            op("act", lambda e, vp=vp, pgi=pgi: e.activation(out=VX[:, vof + pgi, :, :].rearrange("p h e -> p (h e)"), in_=vp, func=AF.Copy),
                               reads=[vpb], writes=[VXh[sp_]])
                        else:
                            op("dve", lambda e, vp=vp, pgi=pgi: e.tensor_copy(out=VX[:, vof + pgi, :, :].rearrange("p h e -> p (h e)"), in_=vp),
                               reads=[vpb], writes=[VXh[sp_]])
                    q0 = sq * 8
                    for h in range(4):
                        par = (sq * 4 + h) % 2
                        psc = gen()
                        ptb_ = PTb[par]
                        for pgi in range(17):
                            lhs = KTq[:, h, kof + pgi * 128:kof + (pgi + 1) * 128] if pgi < 16 else KTs[:, h, :]
                            lb = KTh[sp_] if pgi < 16 else KTs.b
                            for c in range(2):
                                col = pgi * 16 + c * 8
                                op("pe", lambda e, c=c, col=col, lhs=lhs, psc=psc: e.matmul(psc[:, col:col + 8], lhsT=lhs, rhs=qTc[c][:, h, q0:q0 + 8],
                                                                                           start=True, stop=True), reads=[lb, qTc[c].b], writes=[psc.b],
                                   inc=(pgi == 16 and c == 1))
                        op("act", lambda e, psc=psc: e.activation(out=PTa[:, par, 0:272], in_=psc[:, 0:272], func=AF.Exp), reads=[psc.b], writes=[ptb_])
                        op("pool", lambda e: e.tensor_tensor(out=PTa[:, par, 256:272].rearrange("p (c q) -> p c q", c=2),
                                                             in0=PTa[:, par, 256:272].rearrange("p (c q) -> p c q", c=2),
                                                             in1=triblk_bf[:, q0:q0 + 8].unsqueeze(1).broadcast_to([128, 2, 8]), op=ALU.mult),
                           reads=[ptb_, triblk_bf.b], writes=[ptb_])
                        po, pz = pO[par], pZ[par]
                        for pgi in range(17):
                            vr = VX[:, vof + pgi, h, :] if pgi < 16 else Vs[:, h, :]
                            vb = VXh[sp_] if pgi < 16 else Vs.b
                            op("pe", lambda e, pgi=pgi, vr=vr, po=po: e.matmul(po[:, 0:16], lhsT=vr, rhs=PTa[:, par, pgi * 16:(pgi + 1) * 16],
                                                                             start=(pgi == 0), stop=(pgi == 16)), reads=[vb, ptb_], writes=[po.b], inc=(pgi == 16))
                        op("dve", lambda e: e.tensor_reduce(out=zr[:, par * 16:(par + 1) * 16], in_=PTa[:, par, 0:272].rearrange("p (g c) -> p c g", c=16),
                                                            axis=mybir.AxisListType.X, op=ALU.add), reads=[ptb_], writes=[zr.b])
                        op("pe", lambda e, pz=pz: e.matmul(pz[:, 0:16], lhsT=ones_f[:, :], rhs=zr[:, par * 16:(par + 1) * 16], start=True, stop=True),
                           reads=[ones_f.b, zr.b], writes=[pz.b])
                        op("dve", lambda e, po=po: e.tensor_copy(out=Oa[0][:, h * 256:(h + 1) * 256].rearrange("p (c q) -> p c q", c=2)[:, :, q0:q0 + 8],
                                                                in_=po[:, 0:16].rearrange("p (c q) -> p c q", c=2)), reads=[po.b], writes=[Oa[0].b])
                        op("act", lambda e, pz=pz: e.activation(out=Oa[1][:, h * 256:(h + 1) * 256].rearrange("p (c q) -> p c q", c=2)[:, :, q0:q0 + 8],
                                                               in_=pz[:, 0:16].rearrange("p (c q) -> p c q", c=2), func=AF.Copy), reads=[pz.b], writes=[Oa[1].b])
                for h in range(4):
                    finalize(h, 0, 128, src=[(Oa[0][:, h * 256:h * 256 + 128], Oa[0].b), (Oa[0][:, h * 256 + 128:h * 256 + 256], Oa[0].b),
                                             (Oa[1][:, h * 256:h * 256 + 128], Oa[1].b), (Oa[1][:, h * 256 + 128:h * 256 + 256], Oa[1].b)])
            set_gen(bk)
            if DBG_STAGE == 4:
                return
            out_proj(woe[j], nsub)
            if DBG_STAGE == 5:
                return
            layer_norm(nsub, 0, 1024)

        def vgh(s, h):
            t_ = tb[1] if h < 2 else tb[2]
            return t_[:, s, (h % 2) * 256:(h % 2 + 1) * 256], t_.b

        gs = sb("gs", [128, 4, 1024], BF16)
        Zq = sb("Zq", [128, 4, 128], BF16)
        op("dve", lambda e: e.memset(Zq[:], 0.0), writes=[Zq.b])

        def odd_layer(j, li, tile_i, is_sample):
            nsub = 1 if is_sample else 4
            TS = 128 * nsub
            dma("sp", lambda e: e.dma_start(out=blob[:, 0:NO], in_=blo[j, :, :]), writes=[blob.b])
            dma("pool", lambda e: e.dma_start(out=wgab[:], in_=wga[j].rearrange("(kc p) r -> p kc r", p=128)), writes=[wgab.b])
            dma("pool", lambda e: e.dma_start(out=wgbb[:], in_=wgb[j, :, :]), writes=[wgbb.b])
            W = wio[j]
            qTg, kTg, ktk, keT = tb[0], tb[3], tb[4], tb[5]
            set_gen(bk)
            msk = triblk if is_sample else tri
            if (not is_sample) and tile_i == 0:
                op("dve", lambda e: e.memset(Sst[:], 0.0), writes=[Sst.b])
                op("dve", lambda e: e.memset(Sbf[:], 0.0), writes=[Sbf.b])
            ps = gen()
            for kc in range(8):
                op("pe", lambda e, kc=kc, ps=ps: e.matmul(ps[0:16, 0:TS], lhsT=wgab[:, kc, :], rhs=xT[:, kc, 0:TS], start=(kc == 0), stop=(kc == 7)),
                   reads=[wgab.b, xT.b], writes=[ps.b], inc=(kc == 7))
            op("act", lambda e, ps=ps: e.activation(out=zT[0:16, 0:TS], in_=ps[0:16, 0:TS], func=AF.Copy), reads=[ps.b], writes=[zT.b])
            w = load_w(W[:, 0:512])
            for h in range(4):
                ps = gen()
                proj_F(w, h, ps, TS)
                op("act", lambda e, h=h, ps=ps: e.activation(out=qTg[:, h, 0:TS], in_=ps[:, 0:TS], func=AF.Copy, scale=128.0 ** -0.5), reads=[ps.b], writes=[qTg.b])
            w = load_w(W[:, 512:1024])
            for h in range(4):
                ps = gen()
                proj_F(w, h, ps, TS)
                op("dve", lambda e, h=h, ps=ps: e.tensor_copy(out=kTg[:, h, 0:TS], in_=ps[:, 0:TS]), reads=[ps.b], writes=[kTg.b])
            for s in range(nsub):
                ps = gen()
                proj_T(w, s, ps)
                op("act", lambda e, s=s, ps=ps: e.activation(out=ktk[:, s, :], in_=ps[:, :], func=AF.Copy), reads=[ps.b], writes=[ktk.b])
            for nb in range(2):
                w = load_w(W[:, 1024 + nb * 512:1536 + nb * 512])
                for s in range(nsub):
                    ps = gen()
                    proj_T(w, s, ps)
                    op("dve", lambda e, s=s, nb=nb, ps=ps: e.tensor_copy(out=tb[1 + nb][:, s, :], in_=ps[:, :]), reads=[ps.b], writes=[tb[1 + nb].b])
            for nb in range(2):
                w = load_w(W[:, 2048 + nb * 512:2560 + nb * 512])
                for s in range(nsub):
                    ps = gen()
                    proj_T(w, s, ps)
                    op("act", lambda e, s=s, nb=nb, ps=ps: e.activation(out=gs[:, s, nb * 512:(nb + 1) * 512], in_=ps[:, :], func=AF.Silu), reads=[ps.b], writes=[gs.b])
            ob = (pO[0], pO[1], pZ[0], pZ[1])
            set_gen(bk[0:4])
            for s in range(nsub):
                c0 = s * 128
                la, ecb, eq, ek, mt = tf[0], tf[1], tf[2], tf[3], tf[4]
                ps = gen()
                op("pe", lambda e, ps=ps: e.matmul(ps[:, :], lhsT=zT[0:32, c0:c0 + 128], rhs=wgbb[0:32, :], start=True, stop=True), reads=[zT.b, wgbb.b], writes=[ps.b])
                op("act", lambda e, ps=ps: e.activation(out=la[:, 0:512], in_=ps[:, :], func=AF.Exp, scale=-1.0), reads=[ps.b], writes=[la.b])
                op("act", lambda e: e.activation(out=la[:, 0:512], in_=la[:, 0:512], func=AF.Ln, bias=1.0, scale=1.0), reads=[la.b], writes=[la.b])
                op("dve", lambda e: e.tensor_scalar(out=la[:, 0:512], in0=la[:, 0:512], scalar1=-1.0 / 16.0, scalar2=None, op0=ALU.mult), reads=[la.b], writes=[la.b])
                ps = gen()
                op("pe", lambda e, ps=ps: e.matmul(ps[:, :], lhsT=msk[:, :], rhs=la[:, 0:512], start=True, stop=True), reads=[msk.b, la.b], writes=[ps.b])
                op("act", lambda e, ps=ps: e.activation(out=ecb[:, 0:512], in_=ps[:, :], func=AF.Exp, scale=-1.0), reads=[ps.b], writes=[ecb.b])
                op("dve", lambda e, s=s: e.tensor_tensor(out=ktk[:, s, :], in0=ktk[:, s, :], in1=ecb[:, 0:512], op=ALU.mult), reads=[ktk.b, ecb.b], writes=[ktk.b])
                ps = gen()
                for h in range(4):
                    op("pe", lambda e, h=h, ps=ps: e.matmul(ps[:, h * 128:(h + 1) * 128], lhsT=la[:, h * 128:(h + 1) * 128], rhs=msk[:, :], start=True, stop=True),
                       reads=[msk.b, la.b], writes=[ps.b])
                op("act", lambda e, ps=ps: e.activation(out=eq[:, 0:512], in_=ps[:, :], func=AF.Exp), reads=[ps.b], writes=[eq.b])
                op("act", lambda e, ps=ps: e.activation(out=ek[:, 0:512], in_=ps[:, :], func=AF.Exp, scale=-1.0), reads=[ps.b], writes=[ek.b])
                op("dve", lambda e: e.tensor_tensor(out=keT[:, :, 0:128], in0=qTg[:, :, c0:c0 + 128], in1=eq[:, 0:512].rearrange("p (h i) -> p h i", h=4), op=ALU.mult),
                   reads=[qTg.b, eq.b], writes=[keT.b])
                op("dve", lambda e: e.tensor_tensor(out=keT[:, :, 128:256], in0=kTg[:, :, c0:c0 + 128], in1=ek[:, 0:512].rearrange("p (h i) -> p h i", h=4), op=ALU.mult),
                   reads=[kTg.b, ek.b], writes=[keT.b])
                for h in range(4):
                    ps = gen()
                    op("pe", lambda e, h=h, ps=ps: e.matmul(ps[:, 0:128], lhsT=keT[:, h, 128:256], rhs=keT[:, h, 0:128], start=True, stop=True), reads=[keT.b], writes=[ps.b])
                    op("dve", lambda e, h=h, ps=ps: e.tensor_tensor(out=keT[:, h, 256:384], in0=ps[:, 0:128], in1=msk[:, :], op=ALU.mult), reads=[ps.b, msk.b], writes=[keT.b])
                if not is_sample:
                    for h in range(4):
                        o = ob[h]
                        op("pe", lambda e, h=h, o=o: e.matmul(o[:, 0:256], lhsT=keT[:, h, 0:128], rhs=Sbf[:, h, :], start=True, stop=False), reads=[keT.b, Sbf.b], writes=[o.b], inc=False)
                        op("pe", lambda e, h=h, o=o: e.matmul(o[:, 0:256], lhsT=keT[:, h, 256:384], rhs=vgh(s, h)[0], start=False, stop=True), reads=[keT.b, vgh(s, h)[1]], writes=[o.b])
                    for h in range(4):
                        ps = gen()
                        op("pe", lambda e, h=h, ps=ps: e.matmul(ps[:, 0:256], lhsT=ktk[:, s, h * 128:(h + 1) * 128], rhs=vgh(s, h)[0], start=True, stop=True),
                           reads=[ktk.b, vgh(s, h)[1]], writes=[ps.b])
                        ecl = eq[:, h * 128 + 127:h * 128 + 128]
                        op("dve", lambda e, h=h, ecl=ecl: e.tensor_scalar(out=Sst[:, h, :], in0=Sst[:, h, :], scalar1=ecl, scalar2=None, op0=ALU.mult), reads=[Sst.b, eq.b], writes=[Sst.b])
                        op("dve", lambda e, h=h, ps=ps, ecl=ecl: e.scalar_tensor_tensor(out=Sst[:, h, :], in0=ps[:, 0:256], scalar=ecl, in1=Sst[:, h, :], op0=ALU.mult, op1=ALU.add),
                           reads=[ps.b, eq.b, Sst.b], writes=[Sst.b])
                        op("act", lambda e, h=h: e.activation(out=Sbf[:, h, :], in_=Sst[:, h, :], func=AF.Copy), reads=[Sst.b], writes=[Sbf.b])
                    if tile_i == NPT - 1 and s == nsub - 1:
                        dma("sp", lambda e: e.dma_start(out=ngp[j].rearrange("h k v -> k h v"), in_=Sst[:]), reads=[Sst.b])
                else:
                    for sq in range(16):
                        q0 = sq * 8
                        dma("sp", lambda e, sq=sq: e.dma_start(out=Sst[:], in_=sgst[sq, j].rearrange("h k v -> k h v")), writes=[Sst.b])
                        op("act", lambda e: e.activation(out=Sbf[:], in_=Sst[:], func=AF.Copy), reads=[Sst.b], writes=[Sbf.b])
                        op("dve", lambda e, q0=q0: e.tensor_copy(out=Zq[:, :, q0:q0 + 8], in_=keT[:, :, q0:q0 + 8]), reads=[keT.b], writes=[Zq.b])
                        for h in range(4):
                            o = ob[h]
                            op("pe", lambda e, h=h, o=o, sq=sq: e.matmul(o[:, 0:256], lhsT=Zq[:, h, :], rhs=Sbf[:, h, :], start=(sq == 0), stop=False),
                               reads=[Zq.b, Sbf.b], writes=[o.b])
                        op("dve", lambda e, q0=q0: e.memset(Zq[:, :, q0:q0 + 8], 0.0), writes=[Zq.b])
                        km = tb[0]
                        km = mixT
                        op("dve", lambda e, sq=sq: e.tensor_scalar(out=km[:, 0, :], in0=ktk[:, 0, :], scalar1=seqm[:, sq:sq + 1], scalar2=None, op0=ALU.mult),
                           reads=[ktk.b, seqm.b], writes=[km.b])
                        for h in range(4):
                            ps = gen()
                            op("pe", lambda e, h=h, ps=ps: e.matmul(ps[:, 0:256], lhsT=km[:, 0, h * 128:(h + 1) * 128], rhs=vgh(0, h)[0], start=True, stop=True),
                               reads=[km.b, vgh(0, h)[1]], writes=[ps.b])
                            ecl = eq[:, h * 128 + q0 + 7:h * 128 + q0 + 8]
                            op("dve", lambda e, h=h, ecl=ecl: e.tensor_scalar(out=Sst[:, h, :], in0=Sst[:, h, :], scalar1=ecl, scalar2=None, op0=ALU.mult), reads=[Sst.b, eq.b], writes=[Sst.b])
                            op("dve", lambda e, h=h, ps=ps, ecl=ecl: e.scalar_tensor_tensor(out=Sst[:, h, :], in0=ps[:, 0:256], scalar=ecl, in1=Sst[:, h, :], op0=ALU.mult, op1=ALU.add),
                               reads=[ps.b, eq.b, Sst.b], writes=[Sst.b])
                        dma("sp", lambda e, sq=sq: e.dma_start(out=ngs[sq, j].rearrange("h k v -> k h v"), in_=Sst[:]), reads=[Sst.b])
                    for h in range(4):
                        o = ob[h]
                        op("pe", lambda e, h=h, o=o: e.matmul(o[:, 0:256], lhsT=keT[:, h, 256:384], rhs=vgh(0, h)[0], start=False, stop=True), reads=[keT.b, vgh(0, h)[1]], writes=[o.b])
                for h in range(4):
                    o = ob[h]
                    op("act", lambda e, h=h, o=o: e.activation(out=mt[:, h * 256:(h + 1) * 256], in_=o[:, 0:256], func=AF.Square, accum_out=sm[:, 16 + h:17 + h]),
                       reads=[o.b], writes=[mt.b, sm.b])
                op("act", lambda e: e.activation(out=sm[:, 20:24], in_=sm[:, 16:20], func=AF.Sqrt, bias=1e-5, scale=1.0 / 256.0), reads=[sm.b], writes=[sm.b])
                op("dve", lambda e: e.reciprocal(out=sm[:, 20:24], in_=sm[:, 20:24]), reads=[sm.b], writes=[sm.b])
                for h in range(4):
                    o = ob[h]
                    op("dve", lambda e, h=h, o=o: e.scalar_tensor_tensor(out=mt[:, h * 256:(h + 1) * 256], in0=o[:, 0:256], scalar=sm[:, 20 + h:21 + h],
                                                                        in1=blob[:, 2048:2304], op0=ALU.mult, op1=ALU.mult), reads=[o.b, sm.b, blob.b], writes=[mt.b])
                op("pool", lambda e, s=s: e.tensor_tensor(out=mt[:, 0:1024], in0=mt[:, 0:1024], in1=gs[:, s, :], op=ALU.mult), reads=[mt.b, gs.b], writes=[mt.b])
                for k4 in range(2):
                    transpose4(mixT[:, k4 * 4:(k4 + 1) * 4, c0:c0 + 128], mixT.b, [mt[:, kc * 128:(kc + 1) * 128] for kc in range(k4 * 4, k4 * 4 + 4)], mt.b,
                               eng=("dve" if k4 == 0 else "act"))
            set_gen(bk)
            out_proj(woo[j], nsub)
            layer_norm(nsub, 0, 1024)

        n_even = 0
        for gi_, grp in enumerate(groups):
            first_grp = (gi_ == 0)
            last_grp = (gi_ == len(groups) - 1)
            tiles = [(t, False) for t in range(NPT)] + ([(0, True)] if do_sample else [])
            for (t, is_s) in tiles:
                nsub = 1 if is_s else 4
                if is_s:
                    src = xs if first_grp else xms
                    dma("sp", lambda e, src=src: e.dma_start(out=x_tok[:, 0, :], in_=src[:, :]), writes=[x_tok.b])
                else:
                    src = xp if first_grp else xmp
                    dma("sp", lambda e, src=src, t=t: e.dma_start(out=x_tok[:], in_=src[t * 512:(t + 1) * 512, :].rearrange("(s p) d -> p s d", p=128)), writes=[x_tok.b])
                make_xT(nsub)
                for li in grp:
                    if li % 2 == 0:
                        even_layer(li // 2, li, t, is_s)
                    else:
                        odd_layer(li // 2, li, t, is_s)
                    if li != grp[-1]:
                        make_xT(nsub)
                if is_s:
                    dst = ys if last_grp else xms
                    dma("sp", lambda e, dst=dst: e.dma_start(out=dst[:, :], in_=x_tok[:, 0, :]), reads=[x_tok.b])
                else:
                    dst = yp if last_grp else xmp
                    dma("sp", lambda e, dst=dst, t=t: e.dma_start(out=dst[t * 512:(t + 1) * 512, :].rearrange("(s p) d -> p s d", p=128), in_=x_tok[:]), reads=[x_tok.b])
        P.finish()
        with nc.Block() as block:
            P.replay(block)
    return nc, P


def _consts(SEQ):
    inv = np.power(np.float32(500000.0), -np.arange(0, 16, 2, dtype=np.float32) / 16).astype(np.float32)
    pos = np.arange(SEQ, dtype=np.float32)
    ang = pos[:, None] * inv[None, :]
    poss = (2048 + np.arange(8, dtype=np.float32))
    angs = np.tile(poss[:, None] * inv[None, :], (16, 1)).astype(np.float32)
    jj = np.arange(128)
    tri = (jj[:, None] <= jj[None, :]).astype(np.float32)
    same = (jj[:, None] // 8) == (jj[None, :] // 8)
    triblk = (tri * same).astype(np.float32)
    seqm = ((jj[:, None] // 8) == np.arange(16)[None, :]).astype(np.float32)
    rc0 = np.zeros((128, 64), np.float32)
    for g, w in enumerate((2, 4, 8, 16)):
        rc0[:, g * 16:(g + 1) * 16] = 1.0 / np.minimum(np.arange(16) + 1, w).astype(np.float32)
    return dict(ident=np.eye(128, dtype=np.float32), tri=tri, triblk=triblk, seqm=seqm,
                iota=np.arange(128, dtype=np.float32).reshape(128, 1),
                cosp=np.cos(ang).astype(np.float32), sinp=np.sin(ang).astype(np.float32),
                coss=np.cos(angs).astype(np.float32), sins=np.sin(angs).astype(np.float32), rc0=rc0)


def _weights(w_in_even, w_pool_lin, pool_scale, diff_lambda_params, subln_w, w_out_even, w_in_odd, w_gate_a, w_gate_b,
             b_gate, gla_norm_w, w_out_odd, ln_g, ln_b):
    f = np.float32
    ble = np.zeros((2, 128, NE), f)
    blo = np.zeros((2, 128, NO), f)
    for j in range(2):
        ble[j, :, 0:1024] = ln_g[2 * j][None, :]
        ble[j, :, 1024:2048] = ln_b[2 * j][None, :]
        ble[j, :, 2048:2304] = diff_lambda_params[j].reshape(1, 256)
        ble[j, :, 2304:2308] = pool_scale[j].reshape(4, 128).T
        ble[j, :, 2308] = subln_w[j]
        blo[j, :, 0:1024] = ln_g[2 * j + 1][None, :]
        blo[j, :, 1024:2048] = ln_b[2 * j + 1][None, :]
        blo[j, :, 2048:2304] = gla_norm_w[j][None, :]
    wgb = np.zeros((2, 32, 512), f)
    wgb[:, 0:16] = w_gate_b
    wgb[:, 16] = b_gate
    c = np.ascontiguousarray
    return dict(wie=c(w_in_even, dtype=f), woe=c(w_out_even, dtype=f), wio=c(w_in_odd, dtype=f), woo=c(w_out_odd, dtype=f),
                wpl=c(w_pool_lin, dtype=f), wga=c(w_gate_a, dtype=f), wgb=wgb, ble=ble, blo=blo)


_CACHE = {}


def kernel(x_prompt, x_sample, cache_k, cache_v, state_pool, state_gla, page_table,
           w_in_even, w_pool_lin, pool_scale, diff_lambda_params, subln_w, w_out_even,
           w_in_odd, w_gate_a, w_gate_b, b_gate, gla_norm_w, w_out_odd, ln_g, ln_b):
    f = np.float32
    if "nc" not in _CACHE:
        _CACHE["nc"] = build()[0]
    nc = _CACHE["nc"]
    common = _weights(np.asarray(w_in_even), np.asarray(w_pool_lin), np.asarray(pool_scale), np.asarray(diff_lambda_params),
                      np.asarray(subln_w), np.asarray(w_out_even), np.asarray(w_in_odd), np.asarray(w_gate_a), np.asarray(w_gate_b),
                      np.asarray(b_gate), np.asarray(gla_norm_w), np.asarray(w_out_odd), np.asarray(ln_g), np.asarray(ln_b))
    common.update(_consts(4096))
    common["ck"] = np.asarray(cache_k, dtype=f).reshape(2560 * 2 * 128, 512)
    common["cv"] = np.asarray(cache_v, dtype=f).reshape(2560 * 2 * 128, 512)
    xp = np.asarray(x_prompt, dtype=f)
    xs = np.asarray(x_sample, dtype=f)
    sp = np.asarray(state_pool, dtype=f)
    sg = np.asarray(state_gla, dtype=f)
    pt = np.asarray(page_table).astype(np.int32)
    in_maps = []
    for c in range(8):
        m = dict(common)
        m["xp"] = np.ascontiguousarray(xp[c % 4])
        m["xs"] = np.ascontiguousarray(xs[c * 16:(c + 1) * 16].reshape(128, 1024))
        m["spst"] = np.ascontiguousarray(sp[c * 16:(c + 1) * 16])
        m["sgst"] = np.ascontiguousarray(sg[c * 16:(c + 1) * 16])
        m["ptb"] = np.ascontiguousarray(np.broadcast_to(pt[c * 16:(c + 1) * 16].reshape(1, 256), (128, 256)))
        in_maps.append(m)
    res = run_bass_kernel_spmd(nc, in_maps, core_ids=list(range(8))).results
    yp = np.stack([res[b]["yp"] for b in range(4)])
    ys = np.concatenate([res[c]["ys"].reshape(16, 8, 1024) for c in range(8)])
    nkp = np.stack([res[b]["nkp"].reshape(2, 4096, 4, 128) for b in range(4)])
    nvp = np.stack([res[b]["nvp"].reshape(2, 4096, 4, 128) for b in range(4)])
    npp = np.stack([res[b]["npp"] for b in range(4)])
    ngp = np.stack([res[b]["ngp"] for b in range(4)])
    nks = np.concatenate([res[c]["nks"].reshape(16, 2, 8, 4, 128) for c in range(8)])
    nvs = np.concatenate([res[c]["nvs"].reshape(16, 2, 8, 4, 128) for c in range(8)])
    nps = np.concatenate([res[c]["nps"] for c in range(8)])
    ngs = np.concatenate([res[c]["ngs"] for c in range(8)])
    return (yp, ys, nkp, nvp, npp, ngp, nks, nvs, nps, ngs)
```

```python
import math
from contextlib import ExitStack
import numpy as np
import concourse.bass as bass
import concourse.mybir as mybir
from concourse.bass_utils import run_bass_kernel_spmd

F32 = mybir.dt.float32
BF16 = mybir.dt.bfloat16
I32 = mybir.dt.int32
AF = mybir.ActivationFunctionType
ALU = mybir.AluOpType

import os as _os
SAME_ENGINE_SYNC = bool(int(_os.environ.get("SES", "1")))
DBG_STAGE = int(_os.environ.get("DBG_STAGE", "0"))
DN_ALPHA = 8 ** 0.25
NE = 2312
NO = 2304


class Buf:
    __slots__ = ("name", "w", "r", "psum", "parent", "children")

    def __init__(self, name, parent=None):
        self.name = name
        self.w = None
        self.r = {}
        self.psum = False
        self.parent = parent
        self.children = []
        if parent is not None:
            parent.children.append(self)

    def fam(self):
        return [self] + self.children + ([self.parent] if self.parent is not None else [])


class Ctr:
    def __init__(self, key, sem, step):
        self.key = key
        self.sem = sem
        self.step = step
        self.count = 0


class Eng:
    def __init__(self, key, sem):
        self.key = key
        self.ctr = Ctr(key, sem, 1)
        self.prog = []
        self.waited = {}
        self.pending = False


class _Rec:
    def __getattr__(self, name):
        return lambda *a, **k: (name, a, k)


_REC = _Rec()


class Prog:
    def __init__(self, nc, sems, n_lanes=8):
        self.nc = nc
        self.ctrs = {}
        self.engs = {}
        it = iter(sems)
        for key in ("pe", "act", "dve", "pool", "sp"):
            e = Eng(key, next(it))
            self.engs[key] = e
            self.ctrs[key] = e.ctr
        self.lanes = []
        for i in range(n_lanes):
            c = Ctr("lane%d" % i, next(it), 16)
            self.lanes.append(c)
            self.ctrs[c.key] = c
        self.lane_rr = 0
        self.n_ins = 0

    def _deps(self, reads, writes):
        deps = {}
        for b0 in reads:
            for b in b0.fam():
                if b.w is not None:
                    k, c = b.w
                    deps[k] = max(deps.get(k, 0), c)
                if b0.psum:
                    for k, c in b.r.items():
                        deps[k] = max(deps.get(k, 0), c)
        for b0 in writes:
            for b in b0.fam():
                if b.w is not None:
                    k, c = b.w
                    deps[k] = max(deps.get(k, 0), c)
                for k, c in b.r.items():
                    deps[k] = max(deps.get(k, 0), c)
        return deps

    def _emit_waits(self, E, deps):
        for k, c in deps.items():
            if k == E.key:
                if not SAME_ENGINE_SYNC or E.key == "pe":
                    continue
                if c > E.ctr.count:
                    continue
            else:
                assert c <= self.ctrs[k].count, "dep on pending milestone %s" % k
            if E.waited.get(k, 0) >= c:
                continue
            E.waited[k] = c
            E.prog.append(("wait", self.ctrs[k].sem, c))

    def op(self, ek, fn, reads=(), writes=(), inc=True):
        E = self.engs[ek]
        self._emit_waits(E, self._deps(reads, writes))
        E.prog.append(("ins", fn(_REC), inc))
        self.n_ins += 1
        mark = E.ctr.count + 1
        if inc:
            E.ctr.count += 1
            E.pending = False
        else:
            E.pending = True
        for b in reads:
            b.r[E.key] = mark
        for b in writes:
            b.w = (E.key, mark)
            b.r = {}

    def dma(self, qk, fn, reads=(), writes=()):
        Q = self.engs[qk]
        lane = self.lanes[self.lane_rr]
        self.lane_rr = (self.lane_rr + 1) % len(self.lanes)
        deps = self._deps(reads, writes)
        if lane.count > 0:
            deps[lane.key] = max(deps.get(lane.key, 0), lane.count)
        self._emit_waits(Q, deps)
        Q.prog.append(("dma", fn(_REC), lane.sem))
        self.n_ins += 1
        lane.count += 16
        for b in reads:
            b.r[lane.key] = lane.count
        for b in writes:
            b.w = (lane.key, lane.count)
            b.r = {}

    def finish(self):
        for e in self.engs.values():
            assert not e.pending
        Q = self.engs["sp"]
        for k, c in self.ctrs.items():
            if k == "sp" or c.count == 0:
                continue
            if Q.waited.get(k, 0) < c.count:
                Q.prog.append(("wait", c.sem, c.count))

    def replay(self, block):
        engs = self.engs

        def run(E, eng):
            for item in E.prog:
                if item[0] == "wait":
                    eng.wait_ge(item[1], item[2])
                elif item[0] == "ins":
                    c = item[1]
                    ins = getattr(eng, c[0])(*c[1], **c[2])
                    if item[2]:
                        ins.then_inc(E.ctr.sem, 1)
                else:
                    c = item[1]
                    ins = getattr(eng, c[0])(*c[1], **c[2])
                    ins.then_inc(item[2], 16)

        @block.tensor
        def _(e):
            run(engs["pe"], e)

        @block.scalar
        def _(e):
            run(engs["act"], e)

        @block.vector
        def _(e):
            run(engs["dve"], e)

        @block.gpsimd
        def _(e):
            run(engs["pool"], e)

        @block.sync
        def _(e):
            run(engs["sp"], e)


class T:
    def __init__(self, t, name):
        self.t = t
        self.b = Buf(name)

    def __getitem__(self, k):
        return self.t[k]


def build(NPT=8, groups=((0, 1), (2, 3)), do_sample=True, n_layers_out=4, NPG=2560):
    nc = bass.Bass("TRN2", target_bir_lowering=False)
    SEQ = NPT * 512

    def din(name, shape, dt=F32):
        return nc.dram_tensor(name, list(shape), dt, kind="ExternalInput").ap()

    def dout(name, shape, dt=F32):
        return nc.dram_tensor(name, list(shape), dt, kind="ExternalOutput").ap()

    xp = din("xp", [SEQ, 1024])
    xs = din("xs", [128, 1024])
    ck = din("ck", [NPG * 2 * 128, 512])
    cv = din("cv", [NPG * 2 * 128, 512])
    spst = din("spst", [16, 2, 15, 512])
    sgst = din("sgst", [16, 2, 4, 128, 256])
    ptb_d = din("ptb", [128, 256], I32)
    wie = din("wie", [2, 1024, 3072])
    woe = din("woe", [2, 1024, 1024])
    wio = din("wio", [2, 1024, 3072])
    woo = din("woo", [2, 1024, 1024])
    wpl = din("wpl", [2, 4, 128, 128])
    wga = din("wga", [2, 1024, 16])
    wgb = din("wgb", [2, 32, 512])
    ble = din("ble", [2, 128, NE])
    blo = din("blo", [2, 128, NO])
    ident_d = din("ident", [128, 128])
    tri_d = din("tri", [128, 128])
    triblk_d = din("triblk", [128, 128])
    seqm_d = din("seqm", [128, 16])
    iota_d = din("iota", [128, 1])
    cosp = din("cosp", [SEQ, 8])
    sinp = din("sinp", [SEQ, 8])
    coss = din("coss", [128, 8])
    sins = din("sins", [128, 8])
    rc0_d = din("rc0", [128, 64])

    yp = dout("yp", [SEQ, 1024])
    ys = dout("ys", [128, 1024])
    nkp = dout("nkp", [2, SEQ, 512])
    nvp = dout("nvp", [2, SEQ, 512])
    npp = dout("npp", [2, 15, 512])
    ngp = dout("ngp", [2, 4, 128, 256])
    nks = dout("nks", [16, 2, 8, 512])
    nvs = dout("nvs", [16, 2, 8, 512])
    nps = dout("nps", [16, 2, 15, 512])
    ngs = dout("ngs", [16, 2, 4, 128, 256])
    xmp = nc.dram_tensor("xmp", [SEQ, 1024], F32, kind="Internal").ap()
    xms = nc.dram_tensor("xms", [128, 1024], F32, kind="Internal").ap()

    es = ExitStack()
    with es:
        def sb(name, shape, dt=F32):
            return T(es.enter_context(nc.sbuf_tensor(name, list(shape), dt)), name)

        def psb(name, shape, dt=F32):
            t_ = T(es.enter_context(nc.psum_tensor(name, list(shape), dt)), name)
            t_.b.psum = True
            return t_

        sems = [es.enter_context(nc.semaphore("s%d" % i)) for i in range(5 + 24)]
        P = Prog(nc, sems, n_lanes=24)

        x_tok = sb("x_tok", [128, 4, 1024])
        xT = sb("xT", [128, 8, 512], BF16)
        wsl = [sb("wsl%d" % i, [128, 8, 512], BF16) for i in range(2)]
        blob = sb("blob", [128, NE])
        mixT = sb("mixT", [128, 8, 512], BF16)
        KT = sb("KT", [128, 4, max(SEQ, 4096)], BF16)
        VX = sb("VX", [128, max(SEQ // 128, 32), 4, 128], BF16)
        ident = sb("ident_s", [128, 128])
        tri = sb("tri_s", [128, 128])
        tri_bf = sb("tri_bf", [128, 128], BF16)
        triblk = sb("triblk_s", [128, 128])
        triblk_bf = sb("triblk_bf", [128, 128], BF16)
        ones_bf = sb("ones_bf", [128, 128], BF16)
        seqm = sb("seqm_s", [128, 16])
        iota = sb("iota_s", [128, 1])
        cs = sb("cs", [128, 4, 16])
        rc0 = sb("rc0_s", [128, 64])
        wplb = sb("wplb", [128, 4, 128], BF16)
        puT = sb("puT", [128, 4, 15 + 512])
        tf = [sb("tf%d" % i, [128, 1024]) for i in range(5)]
        tb = [sb("tb%d" % i, [128, 4, 512], BF16) for i in range(6)]
        PTb = [Buf("pt%d" % i, parent=tb[4].b) for i in range(4)]
        zr = sb("zr", [128, 32])
        ones_f = sb("ones_f", [128, 128])
        sm = sb("sm", [128, 64])
        rt = sb("rt", [128, 4, 64])
        Sst = sb("Sst", [128, 4, 256])
        Sbf = sb("Sbf", [128, 4, 256], BF16)
        wgab = sb("wgab", [128, 8, 16], BF16)
        wgbb = sb("wgbb", [32, 512], BF16)
        zT = sb("zT", [32, 512], BF16)
        ptb = sb("ptb_s", [128, 256], I32)
        ptf = sb("ptf", [128, 256])
        idx = sb("idx", [128, 2, 256], I32)
        kpg = [sb("kpg%d" % i, [128, 512]) for i in range(1)] * 2
        vpg = [sb("vpg%d" % i, [128, 512]) for i in range(1)] * 2
        KTs = sb("KTs", [128, 4, 128], BF16)
        Vs = sb("Vs", [128, 4, 128], BF16)

        bk = [psb("bk%d" % i, [128, 512]) for i in range(8)]
        pS2 = [[bk[0], bk[1]], [bk[2], bk[3]]]
        pS = [bk[0], bk[2]]
        pO = [bk[4], bk[5]]
        pZ = [bk[6], bk[7]]
        gpool = [bk, 0]

        def set_gen(pool):
            gpool[0] = pool
            gpool[1] = 0

        def gen():
            gpool[1] = (gpool[1] + 1) % len(gpool[0])
            return gpool[0][gpool[1]]

        dma = P.dma
        op = P.op

        for (tt, dd) in ((ident, ident_d), (tri, tri_d), (triblk, triblk_d), (seqm, seqm_d), (iota, iota_d), (rc0, rc0_d)):
            dma("sp", lambda e, tt=tt, dd=dd: e.dma_start(out=tt[:], in_=dd[:, :]), writes=[tt.b])
        dma("sp", lambda e: e.dma_start(out=ptb[:], in_=ptb_d[:, :]), writes=[ptb.b])
        op("dve", lambda e: e.tensor_copy(out=tri_bf[:], in_=tri[:]), reads=[tri.b], writes=[tri_bf.b])
        op("dve", lambda e: e.tensor_copy(out=triblk_bf[:], in_=triblk[:]), reads=[triblk.b], writes=[triblk_bf.b])
        op("dve", lambda e: e.memset(ones_bf[:], 1.0), writes=[ones_bf.b])
        op("dve", lambda e: e.memset(ones_f[:], 1.0), writes=[ones_f.b])
        op("dve", lambda e: e.memset(tb[1][:], 0.0), writes=[tb[1].b])
        op("dve", lambda e: e.memset(tb[2][:], 0.0), writes=[tb[2].b])
        op("dve", lambda e: e.memset(zT[:], 1.0), writes=[zT.b])
        op("dve", lambda e: e.tensor_copy(out=ptf[:], in_=ptb[:]), reads=[ptb.b], writes=[ptf.b])
        op("dve", lambda e: e.tensor_scalar(out=ptf[:], in0=ptf[:], scalar1=256.0, scalar2=iota[:, 0:1], op0=ALU.mult, op1=ALU.add),
           reads=[ptf.b, iota.b], writes=[ptf.b])
        op("dve", lambda e: e.tensor_copy(out=idx[:, 0, :], in_=ptf[:]), reads=[ptf.b], writes=[idx.b])
        op("dve", lambda e: e.tensor_scalar(out=ptf[:], in0=ptf[:], scalar1=128.0, scalar2=None, op0=ALU.add), reads=[ptf.b], writes=[ptf.b])
        op("dve", lambda e: e.tensor_copy(out=idx[:, 1, :], in_=ptf[:]), reads=[ptf.b], writes=[idx.b])

        wslot_i = [0]

        def load_w(src_ap):
            w = wsl[wslot_i[0]]
            wslot_i[0] ^= 1
            dma("pool", lambda e, w=w, src_ap=src_ap: e.dma_start(out=w[:], in_=src_ap.rearrange("(kc p) n -> p kc n", p=128)),
                writes=[w.b])
            return w

        def proj_F(w, m, out_ps, TS):
            for kc in range(8):
                op("pe", lambda e, kc=kc: e.matmul(out_ps[:, 0:TS], lhsT=w[:, kc, m * 128:(m + 1) * 128], rhs=xT[:, kc, 0:TS],
                                                   start=(kc == 0), stop=(kc == 7)),
                   reads=[w.b, xT.b], writes=[out_ps.b], inc=(kc == 7))

        def proj_T(w, s, out_ps, src=None, srcb=None):
            src = xT if src is None else src
            for kc in range(8):
                op("pe", lambda e, kc=kc: e.matmul(out_ps[:, :], lhsT=src[:, kc, s * 128:(s + 1) * 128], rhs=w[:, kc, :],
                                                   start=(kc == 0), stop=(kc == 7)),
                   reads=[w.b, src.b], writes=[out_ps.b], inc=(kc == 7))

        def transpose_to(dst_ap, dst_b, src_ap, src_b, rows=128, cols=128, eng="dve"):
            ps = gen()
            op("pe", lambda e: e.transpose(out=ps[0:cols, 0:rows], in_=src_ap, identity=ident[0:rows, 0:rows]),
               reads=[src_b, ident.b], writes=[ps.b])
            if eng == "dve":
                op("dve", lambda e: e.tensor_copy(out=dst_ap, in_=ps[0:cols, 0:rows]), reads=[ps.b], writes=[dst_b])
            else:
                op("act", lambda e: e.activation(out=dst_ap, in_=ps[0:cols, 0:rows], func=AF.Copy), reads=[ps.b], writes=[dst_b])

        def transpose4(dst_ap3, dst_b, src_aps, src_b, eng="dve"):
            ps = gen()
            for i, sa in enumerate(src_aps):
                op("pe", lambda e, i=i, sa=sa: e.transpose(out=ps[:, i * 128:(i + 1) * 128], in_=sa, identity=ident[:, :]),
                   reads=[src_b, ident.b], writes=[ps.b])
            src3 = ps[:, :].rearrange("p (g i) -> p g i", g=4)
            if eng == "dve":
                op("dve", lambda e: e.tensor_copy(out=dst_ap3, in_=src3), reads=[ps.b], writes=[dst_b])
            else:
                op("act", lambda e: e.activation(out=dst_ap3, in_=src3, func=AF.Copy), reads=[ps.b], writes=[dst_b])

        def make_xT(nsub):
            k = 0
            for s in range(nsub):
                for k4 in range(2):
                    transpose4(xT[:, k4 * 4:(k4 + 1) * 4, s * 128:(s + 1) * 128], xT.b,
                               [x_tok[:, s, kc * 128:(kc + 1) * 128] for kc in range(k4 * 4, k4 * 4 + 4)], x_tok.b,
                               eng=("dve" if k % 2 == 0 else "act"))
                    k += 1

        LNB = [Buf("lnb%d" % i) for i in range(4)]
        XSB = [Buf("xsb%d" % i, parent=x_tok.b) for i in range(4)]

        def layer_norm(nsub, goff, boff):
            for s in range(nsub):
                st6 = tf[4]
                o6 = 64 + s * 16
                o3 = 32 + s * 4
                lb = LNB[s]
                xb = XSB[s]
                for c in range(2):
                    op("dve", lambda e, c=c: e.bn_stats(out=st6[:, o6 + c * 6:o6 + (c + 1) * 6], in_=x_tok[:, s, c * 512:(c + 1) * 512]),
                       reads=[xb], writes=[lb])
                op("dve", lambda e: e.bn_aggr(out=sm[:, o3:o3 + 2], in_=st6[:, o6:o6 + 12].rearrange("p (c k) -> p c k", c=2)),
                   reads=[lb], writes=[lb])
                op("act", lambda e: e.activation(out=sm[:, o3 + 2:o3 + 3], in_=sm[:, o3 + 1:o3 + 2], func=AF.Sqrt, bias=1e-5, scale=1.0), reads=[lb], writes=[lb])
                op("dve", lambda e: e.reciprocal(out=sm[:, o3 + 2:o3 + 3], in_=sm[:, o3 + 2:o3 + 3]), reads=[lb], writes=[lb])
                op("dve", lambda e: e.tensor_scalar(out=x_tok[:, s, :], in0=x_tok[:, s, :], scalar1=sm[:, o3:o3 + 1], scalar2=sm[:, o3 + 2:o3 + 3],
                                                    op0=ALU.subtract, op1=ALU.mult), reads=[xb, lb], writes=[xb])
                op("pool", lambda e: e.tensor_tensor(out=x_tok[:, s, :], in0=x_tok[:, s, :], in1=blob[:, goff:goff + 1024], op=ALU.mult),
                   reads=[xb, blob.b], writes=[xb])
                op("pool", lambda e: e.tensor_tensor(out=x_tok[:, s, :], in0=x_tok[:, s, :], in1=blob[:, boff:boff + 1024], op=ALU.add),
                   reads=[xb, blob.b], writes=[xb])

        def out_proj(w_dram, nsub):
            for nb in range(2):
                w = load_w(w_dram[:, nb * 512:(nb + 1) * 512])
                for s in range(nsub):
                    ps = gen()
                    proj_T(w, s, ps, src=mixT)
                    op("dve", lambda e, s=s, nb=nb, ps=ps: e.scalar_tensor_tensor(
                        out=x_tok[:, s, nb * 512:(nb + 1) * 512], in0=x_tok[:, s, nb * 512:(nb + 1) * 512], scalar=DN_ALPHA,
                        in1=ps[:, :], op0=ALU.mult, op1=ALU.add), reads=[x_tok.b, ps.b], writes=[x_tok.b])

        def even_layer(j, li, tile_i, is_sample):
            nsub = 1 if is_sample else 4
            TS = 128 * nsub
            nseg = 16 if is_sample else 1
            seglen = 8 if is_sample else 512
            dma("sp", lambda e: e.dma_start(out=blob[:, 0:NE], in_=ble[j, :, :]), writes=[blob.b])
            dma("pool", lambda e: e.dma_start(out=wplb[:], in_=wpl[j].rearrange("g c e -> c g e")), writes=[wplb.b])
            LAM, PSC, SUBW = 2048, 2304, 2308
            lam_init = 0.8 - 0.6 * math.exp(-0.3 * li)
            op("dve", lambda e: e.tensor_tensor(out=rt[:, 0, :], in0=blob[:, LAM:LAM + 64], in1=blob[:, LAM + 64:LAM + 128], op=ALU.mult),
               reads=[blob.b], writes=[rt.b])
            op("dve", lambda e: e.tensor_tensor(out=rt[:, 1, :], in0=blob[:, LAM + 128:LAM + 192], in1=blob[:, LAM + 192:LAM + 256], op=ALU.mult),
               reads=[blob.b], writes=[rt.b])
            op("act", lambda e: e.activation(out=rt[:, 2, :], in_=rt[:, 0, :], func=AF.Copy, accum_out=sm[:, 4:5]), reads=[rt.b], writes=[rt.b, sm.b])
            op("act", lambda e: e.activation(out=rt[:, 2, :], in_=rt[:, 1, :], func=AF.Copy, accum_out=sm[:, 5:6]), reads=[rt.b], writes=[rt.b, sm.b])
            op("act", lambda e: e.activation(out=sm[:, 6:8], in_=sm[:, 4:6], func=AF.Exp), reads=[sm.b], writes=[sm.b])
            op("dve", lambda e: e.scalar_tensor_tensor(out=sm[:, 8:9], in0=sm[:, 6:7], scalar=lam_init, in1=sm[:, 7:8], op0=ALU.add, op1=ALU.subtract),
               reads=[sm.b], writes=[sm.b])
            op("dve", lambda e: e.tensor_scalar(out=sm[:, 9:10], in0=blob[:, SUBW:SUBW + 1], scalar1=(1.0 - lam_init), scalar2=None, op0=ALU.mult),
               reads=[blob.b], writes=[sm.b])
            if is_sample:
                dma("sp", lambda e: e.dma_start(out=cs[:, 0, 0:8], in_=coss[:, :]), writes=[cs.b])
                dma("sp", lambda e: e.dma_start(out=cs[:, 0, 8:16], in_=sins[:, :]), writes=[cs.b])
            else:
                r0 = tile_i * 512
                dma("sp", lambda e: e.dma_start(out=cs[:, :, 0:8], in_=cosp[r0:r0 + 512, :].rearrange("(s p) d -> p s d", p=128)), writes=[cs.b])
                dma("sp", lambda e: e.dma_start(out=cs[:, :, 8:16], in_=sinp[r0:r0 + 512, :].rearrange("(s p) d -> p s d", p=128)), writes=[cs.b])

            W = wie[j]
            pgT, qT0, qT1, agT, PTa, dbf = tb
            set_gen(bk)
            op("dve", lambda e: e.memset(qT0[64:128, :, :], 0.0), writes=[qT0.b])
            op("dve", lambda e: e.memset(qT1[0:64, :, :], 0.0), writes=[qT1.b])
            w = load_w(W[:, 512:1024])
            for m in range(4):
                ps = gen()
                proj_F(w, m, ps, TS)
                op("act", lambda e, m=m, ps=ps: e.activation(out=pgT[:, m, 0:TS], in_=ps[:, 0:TS], func=AF.Silu), reads=[ps.b], writes=[pgT.b])
            if DBG_STAGE == 1:
                return
            w = load_w(W[:, 0:512])
            if is_sample:
                for half in range(2):
                    hb = tf[0]
                    for q8 in range(8):
                        dma("sp", lambda e, half=half, q8=q8: e.dma_start(out=hb[q8 * 15:(q8 + 1) * 15, 0:512], in_=spst[half * 8 + q8, j, :, :]),
                            writes=[hb.b])
                    for g in range(4):
                        ps = gen()
                        op("pe", lambda e, g=g, ps=ps: e.transpose(out=ps[:, 0:120], in_=hb[0:120, g * 128:(g + 1) * 128], identity=ident[0:120, 0:120]),
                           reads=[hb.b, ident.b], writes=[ps.b])
                        op("dve", lambda e, g=g, ps=ps, half=half: e.tensor_copy(
                            out=puT[:, g, 0:368].rearrange("p (s t) -> p s t", t=23)[:, half * 8:(half + 1) * 8, 0:15],
                            in_=ps[:, 0:120].rearrange("p (s t) -> p s t", t=15)), reads=[ps.b], writes=[puT.b])
                    dma("sp", lambda e, half=half: e.dma_start(out=nps[half * 8:(half + 1) * 8, j, 0:7, :], in_=spst[half * 8:(half + 1) * 8, j, 8:15, :]))
            elif tile_i == 0:
                op("dve", lambda e: e.memset(puT[:, :, 0:15], 0.0), writes=[puT.b])
            for g in range(4):
                ps = gen()
                proj_F(w, g, ps, TS)
                if is_sample:
                    op("act", lambda e, g=g, ps=ps: e.activation(out=puT[:, g, 0:368].rearrange("p (s t) -> p s t", t=23)[:, :, 15:23],
                                                                 in_=ps[:, 0:128].rearrange("p (s t) -> p s t", t=8), func=AF.Copy),
                       reads=[ps.b], writes=[puT.b])
                else:
                    op("act", lambda e, g=g, ps=ps: e.activation(out=puT[:, g, 15:527], in_=ps[:, 0:512], func=AF.Copy), reads=[ps.b], writes=[puT.b])
            if is_sample:
                ps = gen()
                proj_T(w, 0, ps)
                ob = tf[1]
                op("act", lambda e, ps=ps: e.activation(out=ob[:, 0:512], in_=ps[:, :], func=AF.Copy), reads=[ps.b], writes=[ob.b])
                for sq in range(16):
                    dma("sp", lambda e, sq=sq: e.dma_start(out=nps[sq, j, 7:15, :], in_=ob[sq * 8:(sq + 1) * 8, 0:512]), reads=[ob.b])
            elif tile_i == NPT - 1:
                ps = gen()
                for kc in range(8):
                    op("pe", lambda e, kc=kc, ps=ps: e.matmul(ps[0:15, :], lhsT=xT[:, kc, 497:512], rhs=w[:, kc, :], start=(kc == 0), stop=(kc == 7)),
                       reads=[w.b, xT.b], writes=[ps.b], inc=(kc == 7))
                ob = tf[1]
                op("act", lambda e, ps=ps: e.activation(out=ob[0:15, 0:512], in_=ps[0:15, :], func=AF.Copy), reads=[ps.b], writes=[ob.b])
                dma("sp", lambda e: e.dma_start(out=npp[j, :, :], in_=ob[0:15, 0:512]), reads=[ob.b])
            L = 23 if is_sample else 527

            def seg(ap2d):
                return ap2d.rearrange("p (s t) -> p s t", t=L)

            for g in range(4):
                wnd = (2, 4, 8, 16)[g]
                cur = seg(puT[:, g, 0:nseg * L])
                curb = puT.b
                tmpi = 0
                sh = 1
                while sh < wnd:
                    dst_t = tf[tmpi]
                    dst = seg(dst_t[:, 0:nseg * L])
                    op("dve", lambda e, dst=dst, cur=cur, sh=sh: e.tensor_tensor(out=dst[:, :, sh:L], in0=cur[:, :, sh:L], in1=cur[:, :, 0:L - sh], op=ALU.add),
                       reads=[curb], writes=[dst_t.b])
                    if sh < 15:
                        op("dve", lambda e, dst=dst, cur=cur, sh=sh: e.tensor_copy(out=dst[:, :, 0:sh], in_=cur[:, :, 0:sh]), reads=[curb], writes=[dst_t.b])
                    cur, curb = dst, dst_t.b
                    tmpi ^= 1
                    sh *= 2
                dd = tf[2]
                ddv = dd[:, 0:TS].rearrange("p (s t) -> p s t", t=seglen)
                uv = seg(puT[:, g, 0:nseg * L])[:, :, 15:L]
                op("dve", lambda e, ddv=ddv, cur=cur, uv=uv, wnd=wnd: e.scalar_tensor_tensor(
                    out=ddv, in0=cur[:, :, 15:L], scalar=1.0 / wnd, in1=uv, op0=ALU.mult, op1=ALU.subtract),
                   reads=[curb, puT.b], writes=[dd.b])
                if (not is_sample) and tile_i == 0:
                    op("dve", lambda e, cur=cur, g=g: e.tensor_tensor(out=rt[:, 3, 0:16], in0=cur[:, 0, 15:31], in1=rc0[:, g * 16:(g + 1) * 16], op=ALU.mult),
                       reads=[curb, rc0.b], writes=[rt.b])
                    op("dve", lambda e, g=g: e.tensor_tensor(out=dd[:, 0:16], in0=rt[:, 3, 0:16], in1=puT[:, g, 15:31], op=ALU.subtract),
                       reads=[rt.b, puT.b], writes=[dd.b])
                op("act", lambda e, g=g: e.activation(out=dbf[:, g, 0:TS], in_=dd[:, 0:TS], func=AF.Copy), reads=[dd.b], writes=[dbf.b])
                ps = gen()
                op("pe", lambda e, g=g, ps=ps: e.matmul(ps[:, 0:TS], lhsT=wplb[:, g, :], rhs=dbf[:, g, 0:TS], start=True, stop=True),
                   reads=[wplb.b, dbf.b], writes=[ps.b])
                op("dve", lambda e, g=g, ps=ps: e.scalar_tensor_tensor(out=mixT[:, g, 0:TS], in0=ps[:, 0:TS], scalar=blob[:, PSC + g:PSC + g + 1],
                                                                       in1=pgT[:, g, 0:TS], op0=ALU.mult, op1=ALU.mult),
                   reads=[ps.b, blob.b, pgT.b], writes=[mixT.b])
            if not is_sample:
                op("dve", lambda e: e.tensor_copy(out=puT[:, :, 0:15], in_=puT[:, :, 512:527]), reads=[puT.b], writes=[puT.b])

            def rotary(src_ps, dst, s, scale):
                op("act", lambda e: e.activation(out=dst[:, 0:512], in_=src_ps[:, :], func=AF.Copy, scale=scale), reads=[src_ps.b], writes=[dst.b])
                dv = dst[:, 0:512].rearrange("p (g d) -> p g d", d=64)
                x1 = dv[:, :, 0:8]
                x2 = dv[:, :, 8:16]
                cosb = cs[:, s, 0:8].unsqueeze(1).broadcast_to([128, 8, 8])
                sinb = cs[:, s, 8:16].unsqueeze(1).broadcast_to([128, 8, 8])
                r = [rt[:, i, :].rearrange("p (g d) -> p g d", d=8) for i in range(4)]
                op("dve", lambda e: e.tensor_tensor(out=r[0], in0=x1, in1=cosb, op=ALU.mult), reads=[dst.b, cs.b], writes=[rt.b])
                op("dve", lambda e: e.tensor_tensor(out=r[1], in0=x2, in1=sinb, op=ALU.mult), reads=[dst.b, cs.b], writes=[rt.b])
                op("dve", lambda e: e.tensor_tensor(out=r[2], in0=x2, in1=cosb, op=ALU.mult), reads=[dst.b, cs.b], writes=[rt.b])
                op("dve", lambda e: e.tensor_tensor(out=r[3], in0=x1, in1=sinb, op=ALU.mult), reads=[dst.b, cs.b], writes=[rt.b])
                op("dve", lambda e: e.tensor_tensor(out=x1, in0=r[0], in1=r[1], op=ALU.subtract), reads=[rt.b], writes=[dst.b])
                op("dve", lambda e: e.tensor_tensor(out=x2, in0=r[2], in1=r[3], op=ALU.add), reads=[rt.b], writes=[dst.b])

            if DBG_STAGE == 2:
                return
            kbase = 0 if is_sample else tile_i * 512
            w = load_w(W[:, 1024:1536])
            for s in range(nsub):
                ps = gen()
                proj_T(w, s, ps)
                qt = tf[0] if s % 2 == 0 else tf[2]
                rotary(ps, qt, s, 0.125)
                for h in range(4):
                    ps2 = gen()
                    op("pe", lambda e, h=h, ps2=ps2: e.transpose(out=ps2[:, 0:128], in_=qt[:, h * 128:(h + 1) * 128], identity=ident[:, :]),
                       reads=[qt.b, ident.b], writes=[ps2.b])
                    op("dve", lambda e, h=h, s=s, ps2=ps2: e.tensor_copy(out=qT0[0:64, h, s * 128:(s + 1) * 128], in_=ps2[0:64, 0:128]),
                       reads=[ps2.b], writes=[qT0.b])
                    if DBG_STAGE != 7:
                        op("act", lambda e, h=h, s=s, ps2=ps2: e.activation(out=qT1[64:128, h, s * 128:(s + 1) * 128], in_=ps2[64:128, 0:128], func=AF.Copy),
                           reads=[ps2.b], writes=[qT1.b])
            if DBG_STAGE in (6, 7):
                return
            w = load_w(W[:, 1536:2048])
            KTd = KTs if is_sample else KT
            for s in range(nsub):
                ps = gen()
                proj_T(w, s, ps)
                kt = tf[1] if s % 2 == 0 else tf[2]
                rotary(ps, kt, s, 1.0)
                if is_sample:
                    for sq in range(16):
                        dma("sp", lambda e, sq=sq: e.dma_start(out=nks[sq, j, :, :], in_=kt[sq * 8:(sq + 1) * 8, 0:512]), reads=[kt.b])
                else:
                    r0 = tile_i * 512 + s * 128
                    dma("sp", lambda e, r0=r0: e.dma_start(out=nkp[j, r0:r0 + 128, :], in_=kt[:, 0:512]), reads=[kt.b])
                transpose4(KTd[:, :, kbase + s * 128:kbase + (s + 1) * 128], KTd.b, [kt[:, h * 128:(h + 1) * 128] for h in range(4)], kt.b,
                           eng=("dve" if s % 2 == 0 else "act"))
            if DBG_STAGE == 8:
                return
            w = load_w(W[:, 2048:2560])
            for s in range(nsub):
                ps = gen()
                proj_T(w, s, ps)
                vt = tf[3] if s % 2 == 0 else tf[0]
                op("act", lambda e, ps=ps: e.activation(out=vt[:, 0:512], in_=ps[:, :], func=AF.Copy), reads=[ps.b], writes=[vt.b])
                if is_sample:
                    op("dve", lambda e, ps=ps: e.tensor_copy(out=Vs[:].rearrange("p h e -> p (h e)"), in_=ps[:, :]), reads=[ps.b], writes=[Vs.b])
                    for sq in range(16):
                        dma("sp", lambda e, sq=sq: e.dma_start(out=nvs[sq, j, :, :], in_=vt[sq * 8:(sq + 1) * 8, 0:512]), reads=[vt.b])
                else:
                    blk = tile_i * 4 + s
                    op("dve", lambda e, ps=ps, blk=blk: e.tensor_copy(out=VX[:, blk, :, :].rearrange("p h e -> p (h e)"), in_=ps[:, :]),
                       reads=[ps.b], writes=[VX.b])
                    r0 = tile_i * 512 + s * 128
                    dma("sp", lambda e, r0=r0: e.dma_start(out=nvp[j, r0:r0 + 128, :], in_=vt[:, 0:512]), reads=[vt.b])
            w = load_w(W[:, 2560:3072])
            for m in range(4):
                ps = gen()
                proj_F(w, m, ps, TS)
                op("act", lambda e, m=m, ps=ps: e.activation(out=agT[:, m, 0:TS], in_=ps[:, 0:TS], func=AF.Silu), reads=[ps.b], writes=[agT.b])

            if DBG_STAGE == 3:
                return
            qTc = (qT0, qT1)

            def score_block(h, klhsT, klb, c0, kmask=None, pt_i=0):
                for c in range(2):
                    psc = pS2[c][pt_i]
                    ptb_ = PTb[pt_i * 2 + c]
                    op("pe", lambda e, c=c, psc=psc: e.matmul(psc[:, c0:TS], lhsT=klhsT, rhs=qTc[c][:, h, c0:TS], start=True, stop=True),
                       reads=[klb, qTc[c].b], writes=[psc.b])
                    op("act", lambda e, c=c, psc=psc: e.activation(out=PTa[:, pt_i * 2 + c, c0:TS], in_=psc[:, c0:TS], func=AF.Exp),
                       reads=[psc.b], writes=[ptb_])
                    if kmask is not None:
                        op("pool", lambda e, c=c: e.tensor_tensor(out=PTa[:, pt_i * 2 + c, c0:c0 + 128], in0=PTa[:, pt_i * 2 + c, c0:c0 + 128],
                                                                  in1=kmask[:, :], op=ALU.mult), reads=[ptb_, kmask.b], writes=[ptb_])

            def av_block(vrhs, vb, c0, first, last, pt_i=0):
                for c in range(2):
                    ptb_ = PTb[pt_i * 2 + c]
                    op("pe", lambda e, c=c: e.matmul(pO[c][:, c0:TS], lhsT=vrhs, rhs=PTa[:, pt_i * 2 + c, c0:TS], start=first, stop=last),
                       reads=[vb, ptb_], writes=[pO[c].b])
                    op("pe", lambda e, c=c: e.matmul(pZ[c][:, c0:TS], lhsT=ones_bf[:, :], rhs=PTa[:, pt_i * 2 + c, c0:TS], start=first, stop=last),
                       reads=[ones_bf.b, ptb_], writes=[pZ[c].b])

            def finalize(h, c0, c1, src=None):
                n = c1 - c0
                a1, a2, a3 = tf[0], tf[1], tf[2]
                if src is None:
                    src = [(pO[0][:, c0:c1], pO[0].b), (pO[1][:, c0:c1], pO[1].b), (pZ[0][:, c0:c1], pZ[0].b), (pZ[1][:, c0:c1], pZ[1].b)]
                (o1, o1b), (o2, o2b), (z1, z1b), (z2, z2b) = src
                op("dve", lambda e: e.reciprocal(out=a1[:, 0:n], in_=z1), reads=[z1b], writes=[a1.b])
                op("dve", lambda e: e.reciprocal(out=a2[:, 0:n], in_=z2), reads=[z2b], writes=[a2.b])
                op("dve", lambda e: e.tensor_tensor(out=a1[:, 0:n], in0=a1[:, 0:n], in1=o1, op=ALU.mult), reads=[a1.b, o1b], writes=[a1.b])
                op("dve", lambda e: e.tensor_tensor(out=a2[:, 0:n], in0=a2[:, 0:n], in1=o2, op=ALU.mult), reads=[a2.b, o2b], writes=[a2.b])
                op("dve", lambda e: e.scalar_tensor_tensor(out=a1[:, 0:n], in0=a2[:, 0:n], scalar=sm[:, 8:9], in1=a1[:, 0:n], op0=ALU.mult, op1=ALU.subtract),
                   reads=[a1.b, a2.b, sm.b], writes=[a1.b])
                sq = dbf
                op("act", lambda e: e.activation(out=sq[:, 0, 0:n], in_=a1[:, 0:n], func=AF.Square), reads=[a1.b], writes=[sq.b])
                ps = gen()
                op("pe", lambda e: e.matmul(ps[:, 0:n], lhsT=ones_bf[:, :], rhs=sq[:, 0, 0:n], start=True, stop=True), reads=[ones_bf.b, sq.b], writes=[ps.b])
                op("act", lambda e: e.activation(out=a3[:, 0:n], in_=ps[:, 0:n], func=AF.Sqrt, bias=1e-5, scale=1.0 / 128.0), reads=[ps.b], writes=[a3.b])
                op("dve", lambda e: e.reciprocal(out=a3[:, 0:n], in_=a3[:, 0:n]), reads=[a3.b], writes=[a3.b])
                op("dve", lambda e: e.scalar_tensor_tensor(out=a1[:, 0:n], in0=a1[:, 0:n], scalar=-1.0, in1=a3[:, 0:n], op0=ALU.mult, op1=ALU.mult),
                   reads=[a1.b, a3.b], writes=[a1.b])
                op("dve", lambda e: e.scalar_tensor_tensor(out=mixT[:, 4 + h, c0:c1], in0=a1[:, 0:n], scalar=sm[:, 9:10], in1=agT[:, h, c0:c1],
                                                           op0=ALU.mult, op1=ALU.mult), reads=[a1.b, sm.b, agT.b], writes=[mixT.b])

            set_gen(bk[0:4])
            if not is_sample:
                nkb = tile_i * 4 + 4
                for h in range(4):
                    prev = None
                    for kb in range(nkb):
                        jl = kb - tile_i * 4
                        c0 = 0 if jl < 0 else jl * 128
                        score_block(h, KT[:, h, kb * 128:(kb + 1) * 128], KT.b, c0, kmask=(tri_bf if jl >= 0 else None), pt_i=kb % 2)
                        if prev is not None:
                            av_block(*prev)
                        prev = (VX[:, kb, h, :], VX.b, c0, (kb == 0), (kb == nkb - 1), kb % 2)
                    av_block(*prev)
                    finalize(h, 0, 512)
            else:
                KTq = KT
                Oa = (tf[3], tf[4])
                pi = 0
                stg = [(x_tok[:, 1, 0:512], Buf("stg0", parent=x_tok.b)), (x_tok[:, 1, 512:1024], Buf("stg1", parent=x_tok.b)),
                       (x_tok[:, 2, 0:512], Buf("stg2", parent=x_tok.b)), (x_tok[:, 2, 512:1024], Buf("stg3", parent=x_tok.b)),
                       (x_tok[:, 3, 0:512], Buf("stg4", parent=x_tok.b)), (x_tok[:, 3, 512:1024], Buf("stg5", parent=x_tok.b))]
                KTh = [Buf("kth0", parent=KT.b), Buf("kth1", parent=KT.b)]
                VXh = [Buf("vxh0", parent=VX.b), Buf("vxh1", parent=VX.b)]
                for sq in range(16):
                    sp_ = sq % 2
                    kof = sp_ * 2048
                    vof = sp_ * 16
                    for pgi in range(16):
                        n = sq * 16 + pgi
                        kp, kpb = stg[(pi % 3) * 2]
                        vp, vpb = stg[(pi % 3) * 2 + 1]
                        pi += 1
                        dma("pool", lambda e, kp=kp, n=n: e.indirect_dma_start(
                            out=kp, out_offset=None, in_=ck, in_offset=bass.IndirectOffsetOnAxis(ap=idx[:, j, n:n + 1], axis=0)),
                            reads=[idx.b], writes=[kpb])
                        dma("pool", lambda e, vp=vp, n=n: e.indirect_dma_start(
                            out=vp, out_offset=None, in_=cv, in_offset=bass.IndirectOffsetOnAxis(ap=idx[:, j, n:n + 1], axis=0)),
                            reads=[idx.b], writes=[vpb])
                        transpose4(KTq[:, :, kof + pgi * 128:kof + (pgi + 1) * 128], KTh[sp_], [kp[:, h * 128:(h + 1) * 128] for h in range(4)], kpb,
                                   eng=("dve" if pgi % 2 == 0 else "act"))
                        if pgi % 2 == 0:
                            op("act", lambda e, vp=vp, pgi=pgi: e.activation(out=VX[:, vof + pgi, :, :].rearrange("p h e -> p (h e)"), in_=vp, func=AF.Copy),
                               reads=[vpb], writes=[VXh[sp_]])
                        else:
                            op("dve", lambda e, vp=vp, pgi=pgi: e.tensor_copy(out=VX[:, vof + pgi, :, :].rearrange("p h e -> p (h e)"), in_=vp),
                               reads=[vpb], writes=[VXh[sp_]])
                    q0 = sq * 8
                    for h in range(4):
                        par = (sq * 4 + h) % 2
                        psc = gen()
                        ptb_ = PTb[par]
                        for pgi in range(17):
                            lhs = KTq[:, h, kof + pgi * 128:kof + (pgi + 1) * 128] if pgi < 16 else KTs[:, h, :]
                            lb = KTh[sp_] if pgi < 16 else KTs.b
                            for c in range(2):
                                col = pgi * 16 + c * 8
                                op("pe", lambda e, c=c, col=col, lhs=lhs, psc=psc: e.matmul(psc[:, col:col + 8], lhsT=lhs, rhs=qTc[c][:, h, q0:q0 + 8],
                                                                                           start=True, stop=True), reads=[lb, qTc[c].b], writes=[psc.b],
                                   inc=(pgi == 16 and c == 1))
                        op("act", lambda e, psc=psc: e.activation(out=PTa[:, par, 0:272], in_=psc[:, 0:272], func=AF.Exp), reads=[psc.b], writes=[ptb_])
                        op("pool", lambda e: e.tensor_tensor(out=PTa[:, par, 256:272].rearrange("p (c q) -> p c q", c=2),
                                                             in0=PTa[:, par, 256:272].rearrange("p (c q) -> p c q", c=2),
                                                             in1=triblk_bf[:, q0:q0 + 8].unsqueeze(1).broadcast_to([128, 2, 8]), op=ALU.mult),
                           reads=[ptb_, triblk_bf.b], writes=[ptb_])
                        po, pz = pO[par], pZ[par]
                        for pgi in range(17):
                            vr = VX[:, vof + pgi, h, :] if pgi < 16 else Vs[:, h, :]
                            vb = VXh[sp_] if pgi < 16 else Vs.b
                            op("pe", lambda e, pgi=pgi, vr=vr, po=po: e.matmul(po[:, 0:16], lhsT=vr, rhs=PTa[:, par, pgi * 16:(pgi + 1) * 16],
                                                                             start=(pgi == 0), stop=(pgi == 16)), reads=[vb, ptb_], writes=[po.b], inc=(pgi == 16))
                        op("dve", lambda e: e.tensor_reduce(out=zr[:, par * 16:(par + 1) * 16], in_=PTa[:, par, 0:272].rearrange("p (g c) -> p c g", c=16),
                                                            axis=mybir.AxisListType.X, op=ALU.add), reads=[ptb_], writes=[zr.b])
                        op("pe", lambda e, pz=pz: e.matmul(pz[:, 0:16], lhsT=ones_f[:, :], rhs=zr[:, par * 16:(par + 1) * 16], start=True, stop=True),
                           reads=[ones_f.b, zr.b], writes=[pz.b])
                        op("dve", lambda e, po=po: e.tensor_copy(out=Oa[0][:, h * 256:(h + 1) * 256].rearrange("p (c q) -> p c q", c=2)[:, :, q0:q0 + 8],
                                                                in_=po[:, 0:16].rearrange("p (c q) -> p c q", c=2)), reads=[po.b], writes=[Oa[0].b])
                        op("act", lambda e, pz=pz: e.activation(out=Oa[1][:, h * 256:(h + 1) * 256].rearrange("p (c q) -> p c q", c=2)[:, :, q0:q0 + 8],
                                                               in_=pz[:, 0:16].rearrange("p (c q) -> p c q", c=2), func=AF.Copy), reads=[pz.b], writes=[Oa[1].b])
                for h in range(4):
                    finalize(h, 0, 128, src=[(Oa[0][:, h * 256:h * 256 + 128], Oa[0].b), (Oa[0][:, h * 256 + 128:h * 256 + 256], Oa[0].b),
                                             (Oa[1][:, h * 256:h * 256 + 128], Oa[1].b), (Oa[1][:, h * 256 + 128:h * 256 + 256], Oa[1].b)])
            set_gen(bk)
            if DBG_STAGE == 4:
                return
            out_proj(woe[j], nsub)
            if DBG_STAGE == 5:
                return
            layer_norm(nsub, 0, 1024)

        def vgh(s, h):
            t_ = tb[1] if h < 2 else tb[2]
            return t_[:, s, (h % 2) * 256:(h % 2 + 1) * 256], t_.b

        gs = sb("gs", [128, 4, 1024], BF16)
        Zq = sb("Zq", [128, 4, 128], BF16)
        op("dve", lambda e: e.memset(Zq[:], 0.0), writes=[Zq.b])

        def odd_layer(j, li, tile_i, is_sample):
            nsub = 1 if is_sample else 4
            TS = 128 * nsub
            dma("sp", lambda e: e.dma_start(out=blob[:, 0:NO], in_=blo[j, :, :]), writes=[blob.b])
            dma("pool", lambda e: e.dma_start(out=wgab[:], in_=wga[j].rearrange("(kc p) r -> p kc r", p=128)), writes=[wgab.b])
            dma("pool", lambda e: e.dma_start(out=wgbb[:], in_=wgb[j, :, :]), writes=[wgbb.b])
            W = wio[j]
            qTg, kTg, ktk, keT = tb[0], tb[3], tb[4], tb[5]
            set_gen(bk)
            msk = triblk if is_sample else tri
            if (not is_sample) and tile_i == 0:
                op("dve", lambda e: e.memset(Sst[:], 0.0), writes=[Sst.b])
                op("dve", lambda e: e.memset(Sbf[:], 0.0), writes=[Sbf.b])
            ps = gen()
            for kc in range(8):
                op("pe", lambda e, kc=kc, ps=ps: e.matmul(ps[0:16, 0:TS], lhsT=wgab[:, kc, :], rhs=xT[:, kc, 0:TS], start=(kc == 0), stop=(kc == 7)),
                   reads=[wgab.b, xT.b], writes=[ps.b], inc=(kc == 7))
            op("act", lambda e, ps=ps: e.activation(out=zT[0:16, 0:TS], in_=ps[0:16, 0:TS], func=AF.Copy), reads=[ps.b], writes=[zT.b])
            w = load_w(W[:, 0:512])
            for h in range(4):
                ps = gen()
                proj_F(w, h, ps, TS)
                op("act", lambda e, h=h, ps=ps: e.activation(out=qTg[:, h, 0:TS], in_=ps[:, 0:TS], func=AF.Copy, scale=128.0 ** -0.5), reads=[ps.b], writes=[qTg.b])
            w = load_w(W[:, 512:1024])
            for h in range(4):
                ps = gen()
                proj_F(w, h, ps, TS)
                op("dve", lambda e, h=h, ps=ps: e.tensor_copy(out=kTg[:, h, 0:TS], in_=ps[:, 0:TS]), reads=[ps.b], writes=[kTg.b])
            for s in range(nsub):
                ps = gen()
                proj_T(w, s, ps)
                op("act", lambda e, s=s, ps=ps: e.activation(out=ktk[:, s, :], in_=ps[:, :], func=AF.Copy), reads=[ps.b], writes=[ktk.b])
            for nb in range(2):
                w = load_w(W[:, 1024 + nb * 512:1536 + nb * 512])
                for s in range(nsub):
                    ps = gen()
                    proj_T(w, s, ps)
                    op("dve", lambda e, s=s, nb=nb, ps=ps: e.tensor_copy(out=tb[1 + nb][:, s, :], in_=ps[:, :]), reads=[ps.b], writes=[tb[1 + nb].b])
            for nb in range(2):
                w = load_w(W[:, 2048 + nb * 512:2560 + nb * 512])
                for s in range(nsub):
                    ps = gen()
                    proj_T(w, s, ps)
                    op("act", lambda e, s=s, nb=nb, ps=ps: e.activation(out=gs[:, s, nb * 512:(nb + 1) * 512], in_=ps[:, :], func=AF.Silu), reads=[ps.b], writes=[gs.b])
            ob = (pO[0], pO[1], pZ[0], pZ[1])
            set_gen(bk[0:4])
            for s in range(nsub):
                c0 = s * 128
                la, ecb, eq, ek, mt = tf[0], tf[1], tf[2], tf[3], tf[4]
                ps = gen()
                op("pe", lambda e, ps=ps: e.matmul(ps[:, :], lhsT=zT[0:32, c0:c0 + 128], rhs=wgbb[0:32, :], start=True, stop=True), reads=[zT.b, wgbb.b], writes=[ps.b])
                op("act", lambda e, ps=ps: e.activation(out=la[:, 0:512], in_=ps[:, :], func=AF.Exp, scale=-1.0), reads=[ps.b], writes=[la.b])
                op("act", lambda e: e.activation(out=la[:, 0:512], in_=la[:, 0:512], func=AF.Ln, bias=1.0, scale=1.0), reads=[la.b], writes=[la.b])
                op("dve", lambda e: e.tensor_scalar(out=la[:, 0:512], in0=la[:, 0:512], scalar1=-1.0 / 16.0, scalar2=None, op0=ALU.mult), reads=[la.b], writes=[la.b])
                ps = gen()
                op("pe", lambda e, ps=ps: e.matmul(ps[:, :], lhsT=msk[:, :], rhs=la[:, 0:512], start=True, stop=True), reads=[msk.b, la.b], writes=[ps.b])
                op("act", lambda e, ps=ps: e.activation(out=ecb[:, 0:512], in_=ps[:, :], func=AF.Exp, scale=-1.0), reads=[ps.b], writes=[ecb.b])
                op("dve", lambda e, s=s: e.tensor_tensor(out=ktk[:, s, :], in0=ktk[:, s, :], in1=ecb[:, 0:512], op=ALU.mult), reads=[ktk.b, ecb.b], writes=[ktk.b])
                ps = gen()
                for h in range(4):
                    op("pe", lambda e, h=h, ps=ps: e.matmul(ps[:, h * 128:(h + 1) * 128], lhsT=la[:, h * 128:(h + 1) * 128], rhs=msk[:, :], start=True, stop=True),
                       reads=[msk.b, la.b], writes=[ps.b])
                op("act", lambda e, ps=ps: e.activation(out=eq[:, 0:512], in_=ps[:, :], func=AF.Exp), reads=[ps.b], writes=[eq.b])
                op("act", lambda e, ps=ps: e.activation(out=ek[:, 0:512], in_=ps[:, :], func=AF.Exp, scale=-1.0), reads=[ps.b], writes=[ek.b])
                op("dve", lambda e: e.tensor_tensor(out=keT[:, :, 0:128], in0=qTg[:, :, c0:c0 + 128], in1=eq[:, 0:512].rearrange("p (h i) -> p h i", h=4), op=ALU.mult),
                   reads=[qTg.b, eq.b], writes=[keT.b])
                op("dve", lambda e: e.tensor_tensor(out=keT[:, :, 128:256], in0=kTg[:, :, c0:c0 + 128], in1=ek[:, 0:512].rearrange("p (h i) -> p h i", h=4), op=ALU.mult),
                   reads=[kTg.b, ek.b], writes=[keT.b])
                for h in range(4):
                    ps = gen()
                    op("pe", lambda e, h=h, ps=ps: e.matmul(ps[:, 0:128], lhsT=keT[:, h, 128:256], rhs=keT[:, h, 0:128], start=True, stop=True), reads=[keT.b], writes=[ps.b])
                    op("dve", lambda e, h=h, ps=ps: e.tensor_tensor(out=keT[:, h, 256:384], in0=ps[:, 0:128], in1=msk[:, :], op=ALU.mult), reads=[ps.b, msk.b], writes=[keT.b])
                if not is_sample:
                    for h in range(4):
                        o = ob[h]
                        op("pe", lambda e, h=h, o=o: e.matmul(o[:, 0:256], lhsT=keT[:, h, 0:128], rhs=Sbf[:, h, :], start=True, stop=False), reads=[keT.b, Sbf.b], writes=[o.b], inc=False)
                        op("pe", lambda e, h=h, o=o: e.matmul(o[:, 0:256], lhsT=keT[:, h, 256:384], rhs=vgh(s, h)[0], start=False, stop=True), reads=[keT.b, vgh(s, h)[1]], writes=[o.b])
                    for h in range(4):
                        ps = gen()
                        op("pe", lambda e, h=h, ps=ps: e.matmul(ps[:, 0:256], lhsT=ktk[:, s, h * 128:(h + 1) * 128], rhs=vgh(s, h)[0], start=True, stop=True),
                           reads=[ktk.b, vgh(s, h)[1]], writes=[ps.b])
                        ecl = eq[:, h * 128 + 127:h * 128 + 128]
                        op("dve", lambda e, h=h, ecl=ecl: e.tensor_scalar(out=Sst[:, h, :], in0=Sst[:, h, :], scalar1=ecl, scalar2=None, op0=ALU.mult), reads=[Sst.b, eq.b], writes=[Sst.b])
                        op("dve", lambda e, h=h, ps=ps, ecl=ecl: e.scalar_tensor_tensor(out=Sst[:, h, :], in0=ps[:, 0:256], scalar=ecl, in1=Sst[:, h, :], op0=ALU.mult, op1=ALU.add),
                           reads=[ps.b, eq.b, Sst.b], writes=[Sst.b])
                        op("act", lambda e, h=h: e.activation(out=Sbf[:, h, :], in_=Sst[:, h, :], func=AF.Copy), reads=[Sst.b], writes=[Sbf.b])
                    if tile_i == NPT - 1 and s == nsub - 1:
                        dma("sp", lambda e: e.dma_start(out=ngp[j].rearrange("h k v -> k h v"), in_=Sst[:]), reads=[Sst.b])
                else:
                    for sq in range(16):
                        q0 = sq * 8
                        dma("sp", lambda e, sq=sq: e.dma_start(out=Sst[:], in_=sgst[sq, j].rearrange("h k v -> k h v")), writes=[Sst.b])
                        op("act", lambda e: e.activation(out=Sbf[:], in_=Sst[:], func=AF.Copy), reads=[Sst.b], writes=[Sbf.b])
                        op("dve", lambda e, q0=q0: e.tensor_copy(out=Zq[:, :, q0:q0 + 8], in_=keT[:, :, q0:q0 + 8]), reads=[keT.b], writes=[Zq.b])
                        for h in range(4):
                            o = ob[h]
                            op("pe", lambda e, h=h, o=o, sq=sq: e.matmul(o[:, 0:256], lhsT=Zq[:, h, :], rhs=Sbf[:, h, :], start=(sq == 0), stop=False),
                               reads=[Zq.b, Sbf.b], writes=[o.b])
                        op("dve", lambda e, q0=q0: e.memset(Zq[:, :, q0:q0 + 8], 0.0), writes=[Zq.b])
                        km = tb[0]
                        km = mixT
                        op("dve", lambda e, sq=sq: e.tensor_scalar(out=km[:, 0, :], in0=ktk[:, 0, :], scalar1=seqm[:, sq:sq + 1], scalar2=None, op0=ALU.mult),
                           reads=[ktk.b, seqm.b], writes=[km.b])
                        for h in range(4):
                            ps = gen()
                            op("pe", lambda e, h=h, ps=ps: e.matmul(ps[:, 0:256], lhsT=km[:, 0, h * 128:(h + 1) * 128], rhs=vgh(0, h)[0], start=True, stop=True),
                               reads=[km.b, vgh(0, h)[1]], writes=[ps.b])
                            ecl = eq[:, h * 128 + q0 + 7:h * 128 + q0 + 8]
                            op("dve", lambda e, h=h, ecl=ecl: e.tensor_scalar(out=Sst[:, h, :], in0=Sst[:, h, :], scalar1=ecl, scalar2=None, op0=ALU.mult), reads=[Sst.b, eq.b], writes=[Sst.b])
                            op("dve", lambda e, h=h, ps=ps, ecl=ecl: e.scalar_tensor_tensor(out=Sst[:, h, :], in0=ps[:, 0:256], scalar=ecl, in1=Sst[:, h, :], op0=ALU.mult, op1=ALU.add),
                               reads=[ps.b, eq.b, Sst.b], writes=[Sst.b])
                        dma("sp", lambda e, sq=sq: e.dma_start(out=ngs[sq, j].rearrange("h k v -> k h v"), in_=Sst[:]), reads=[Sst.b])
                    for h in range(4):
                        o = ob[h]
                        op("pe", lambda e, h=h, o=o: e.matmul(o[:, 0:256], lhsT=keT[:, h, 256:384], rhs=vgh(0, h)[0], start=False, stop=True), reads=[keT.b, vgh(0, h)[1]], writes=[o.b])
                for h in range(4):
                    o = ob[h]
                    op("act", lambda e, h=h, o=o: e.activation(out=mt[:, h * 256:(h + 1) * 256], in_=o[:, 0:256], func=AF.Square, accum_out=sm[:, 16 + h:17 + h]),
                       reads=[o.b], writes=[mt.b, sm.b])
                op("act", lambda e: e.activation(out=sm[:, 20:24], in_=sm[:, 16:20], func=AF.Sqrt, bias=1e-5, scale=1.0 / 256.0), reads=[sm.b], writes=[sm.b])
                op("dve", lambda e: e.reciprocal(out=sm[:, 20:24], in_=sm[:, 20:24]), reads=[sm.b], writes=[sm.b])
                for h in range(4):
                    o = ob[h]
                    op("dve", lambda e, h=h, o=o: e.scalar_tensor_tensor(out=mt[:, h * 256:(h + 1) * 256], in0=o[:, 0:256], scalar=sm[:, 20 + h:21 + h],
                                                                        in1=blob[:, 2048:2304], op0=ALU.mult, op1=ALU.mult), reads=[o.b, sm.b, blob.b], writes=[mt.b])
                op("pool", lambda e, s=s: e.tensor_tensor(out=mt[:, 0:1024], in0=mt[:, 0:1024], in1=gs[:, s, :], op=ALU.mult), reads=[mt.b, gs.b], writes=[mt.b])
                for k4 in range(2):
                    transpose4(mixT[:, k4 * 4:(k4 + 1) * 4, c0:c0 + 128], mixT.b, [mt[:, kc * 128:(kc + 1) * 128] for kc in range(k4 * 4, k4 * 4 + 4)], mt.b,
                               eng=("dve" if k4 == 0 else "act"))
            set_gen(bk)
            out_proj(woo[j], nsub)
            layer_norm(nsub, 0, 1024)

        n_even = 0
        for gi_, grp in enumerate(groups):
            first_grp = (gi_ == 0)
            last_grp = (gi_ == len(groups) - 1)
            tiles = [(t, False) for t in range(NPT)] + ([(0, True)] if do_sample else [])
            for (t, is_s) in tiles:
                nsub = 1 if is_s else 4
                if is_s:
                    src = xs if first_grp else xms
                    dma("sp", lambda e, src=src: e.dma_start(out=x_tok[:, 0, :], in_=src[:, :]), writes=[x_tok.b])
                else:
                    src = xp if first_grp else xmp
                    dma("sp", lambda e, src=src, t=t: e.dma_start(out=x_tok[:], in_=src[t * 512:(t + 1) * 512, :].rearrange("(s p) d -> p s d", p=128)), writes=[x_tok.b])
                make_xT(nsub)
                for li in grp:
                    if li % 2 == 0:
                        even_layer(li // 2, li, t, is_s)
                    else:
                        odd_layer(li // 2, li, t, is_s)
                    if li != grp[-1]:
                        make_xT(nsub)
                if is_s:
                    dst = ys if last_grp else xms
                    dma("sp", lambda e, dst=dst: e.dma_start(out=dst[:, :], in_=x_tok[:, 0, :]), reads=[x_tok.b])
                else:
                    dst = yp if last_grp else xmp
                    dma("sp", lambda e, dst=dst, t=t: e.dma_start(out=dst[t * 512:(t + 1) * 512, :].rearrange("(s p) d -> p s d", p=128), in_=x_tok[:]), reads=[x_tok.b])
        P.finish()
        with nc.Block() as block:
            P.replay(block)
    return nc, P


def _consts(SEQ):
    inv = np.power(np.float32(500000.0), -np.arange(0, 16, 2, dtype=np.float32) / 16).astype(np.float32)
    pos = np.arange(SEQ, dtype=np.float32)
    ang = pos[:, None] * inv[None, :]
    poss = (2048 + np.arange(8, dtype=np.float32))
    angs = np.tile(poss[:, None] * inv[None, :], (16, 1)).astype(np.float32)
    jj = np.arange(128)
    tri = (jj[:, None] <= jj[None, :]).astype(np.float32)
    same = (jj[:, None] // 8) == (jj[None, :] // 8)
    triblk = (tri * same).astype(np.float32)
    seqm = ((jj[:, None] // 8) == np.arange(16)[None, :]).astype(np.float32)
    rc0 = np.zeros((128, 64), np.float32)
    for g, w in enumerate((2, 4, 8, 16)):
        rc0[:, g * 16:(g + 1) * 16] = 1.0 / np.minimum(np.arange(16) + 1, w).astype(np.float32)
    return dict(ident=np.eye(128, dtype=np.float32), tri=tri, triblk=triblk, seqm=seqm,
                iota=np.arange(128, dtype=np.float32).reshape(128, 1),
                cosp=np.cos(ang).astype(np.float32), sinp=np.sin(ang).astype(np.float32),
                coss=np.cos(angs).astype(np.float32), sins=np.sin(angs).astype(np.float32), rc0=rc0)


def _weights(w_in_even, w_pool_lin, pool_scale, diff_lambda_params, subln_w, w_out_even, w_in_odd, w_gate_a, w_gate_b,
             b_gate, gla_norm_w, w_out_odd, ln_g, ln_b):
    f = np.float32
    ble = np.zeros((2, 128, NE), f)
    blo = np.zeros((2, 128, NO), f)
    for j in range(2):
        ble[j, :, 0:1024] = ln_g[2 * j][None, :]
        ble[j, :, 1024:2048] = ln_b[2 * j][None, :]
        ble[j, :, 2048:2304] = diff_lambda_params[j].reshape(1, 256)
        ble[j, :, 2304:2308] = pool_scale[j].reshape(4, 128).T
        ble[j, :, 2308] = subln_w[j]
        blo[j, :, 0:1024] = ln_g[2 * j + 1][None, :]
        blo[j, :, 1024:2048] = ln_b[2 * j + 1][None, :]
        blo[j, :, 2048:2304] = gla_norm_w[j][None, :]
    wgb = np.zeros((2, 32, 512), f)
    wgb[:, 0:16] = w_gate_b
    wgb[:, 16] = b_gate
    c = np.ascontiguousarray
    return dict(wie=c(w_in_even, dtype=f), woe=c(w_out_even, dtype=f), wio=c(w_in_odd, dtype=f), woo=c(w_out_odd, dtype=f),
                wpl=c(w_pool_lin, dtype=f), wga=c(w_gate_a, dtype=f), wgb=wgb, ble=ble, blo=blo)


_CACHE = {}


def kernel(x_prompt, x_sample, cache_k, cache_v, state_pool, state_gla, page_table,
           w_in_even, w_pool_lin, pool_scale, diff_lambda_params, subln_w, w_out_even,
           w_in_odd, w_gate_a, w_gate_b, b_gate, gla_norm_w, w_out_odd, ln_g, ln_b):
    f = np.float32
    if "nc" not in _CACHE:
        _CACHE["nc"] = build()[0]
    nc = _CACHE["nc"]
    common = _weights(np.asarray(w_in_even), np.asarray(w_pool_lin), np.asarray(pool_scale), np.asarray(diff_lambda_params),
                      np.asarray(subln_w), np.asarray(w_out_even), np.asarray(w_in_odd), np.asarray(w_gate_a), np.asarray(w_gate_b),
                      np.asarray(b_gate), np.asarray(gla_norm_w), np.asarray(w_out_odd), np.asarray(ln_g), np.asarray(ln_b))
    common.update(_consts(4096))
    common["ck"] = np.asarray(cache_k, dtype=f).reshape(2560 * 2 * 128, 512)
    common["cv"] = np.asarray(cache_v, dtype=f).reshape(2560 * 2 * 128, 512)
    xp = np.asarray(x_prompt, dtype=f)
    xs = np.asarray(x_sample, dtype=f)
    sp = np.asarray(state_pool, dtype=f)
    sg = np.asarray(state_gla, dtype=f)
    pt = np.asarray(page_table).astype(np.int32)
    in_maps = []
    for c in range(8):
        m = dict(common)
        m["xp"] = np.ascontiguousarray(xp[c % 4])
        m["xs"] = np.ascontiguousarray(xs[c * 16:(c + 1) * 16].reshape(128, 1024))
        m["spst"] = np.ascontiguousarray(sp[c * 16:(c + 1) * 16])
        m["sgst"] = np.ascontiguousarray(sg[c * 16:(c + 1) * 16])
        m["ptb"] = np.ascontiguousarray(np.broadcast_to(pt[c * 16:(c + 1) * 16].reshape(1, 256), (128, 256)))
        in_maps.append(m)
    res = run_bass_kernel_spmd(nc, in_maps, core_ids=list(range(8))).results
    yp = np.stack([res[b]["yp"] for b in range(4)])
    ys = np.concatenate([res[c]["ys"].reshape(16, 8, 1024) for c in range(8)])
    nkp = np.stack([res[b]["nkp"].reshape(2, 4096, 4, 128) for b in range(4)])
    nvp = np.stack([res[b]["nvp"].reshape(2, 4096, 4, 128) for b in range(4)])
    npp = np.stack([res[b]["npp"] for b in range(4)])
    ngp = np.stack([res[b]["ngp"] for b in range(4)])
    nks = np.concatenate([res[c]["nks"].reshape(16, 2, 8, 4, 128) for c in range(8)])
    nvs = np.concatenate([res[c]["nvs"].reshape(16, 2, 8, 4, 128) for c in range(8)])
    nps = np.concatenate([res[c]["nps"] for c in range(8)])
    ngs = np.concatenate([res[c]["ngs"] for c in range(8)])
    return (yp, ys, nkp, nvp, npp, ngp, nks, nvs, nps, ngs)
```

```python
import math
from contextlib import ExitStack
import numpy as np
import concourse.bass as bass
import concourse.mybir as mybir
from concourse.bass_utils import run_bass_kernel_spmd

F32 = mybir.dt.float32
BF16 = mybir.dt.bfloat16
I32 = mybir.dt.int32
AF = mybir.ActivationFunctionType
ALU = mybir.AluOpType

import os as _os
SAME_ENGINE_SYNC = bool(int(_os.environ.get("SES", "1")))
DBG_STAGE = int(_os.environ.get("DBG_STAGE", "0"))
DN_ALPHA = 8 ** 0.25
NE = 2312
NO = 2304


class Buf:
    __slots__ = ("name", "w", "r", "psum", "parent", "children")

    def __init__(self, name, parent=None):
        self.name = name
        self.w = None
        self.r = {}
        self.psum = False
        self.parent = parent
        self.children = []
        if parent is not None:
            parent.children.append(self)

    def fam(self):
        return [self] + self.children + ([self.parent] if self.parent is not None else [])


class Ctr:
    def __init__(self, key, sem, step):
        self.key = key
        self.sem = sem
        self.step = step
        self.count = 0


class Eng:
    def __init__(self, key, sem):
        self.key = key
        self.ctr = Ctr(key, sem, 1)
        self.prog = []
        self.waited = {}
        self.pending = False


class _Rec:
    def __getattr__(self, name):
        return lambda *a, **k: (name, a, k)


_REC = _Rec()


class Prog:
    def __init__(self, nc, sems, n_lanes=8):
        self.nc = nc
        self.ctrs = {}
        self.engs = {}
        it = iter(sems)
        for key in ("pe", "act", "dve", "pool", "sp"):
            e = Eng(key, next(it))
            self.engs[key] = e
            self.ctrs[key] = e.ctr
        self.lanes = []
        for i in range(n_lanes):
            c = Ctr("lane%d" % i, next(it), 16)
            self.lanes.append(c)
            self.ctrs[c.key] = c
        self.lane_rr = 0
        self.n_ins = 0

    def _deps(self, reads, writes):
        deps = {}
        for b0 in reads:
            for b in b0.fam():
                if b.w is not None:
                    k, c = b.w
                    deps[k] = max(deps.get(k, 0), c)
                if b0.psum:
                    for k, c in b.r.items():
                        deps[k] = max(deps.get(k, 0), c)
        for b0 in writes:
            for b in b0.fam():
                if b.w is not None:
                    k, c = b.w
                    deps[k] = max(deps.get(k, 0), c)
                for k, c in b.r.items():
                    deps[k] = max(deps.get(k, 0), c)
        return deps

    def _emit_waits(self, E, deps):
        for k, c in deps.items():
            if k == E.key:
                if not SAME_ENGINE_SYNC or E.key == "pe":
                    continue
                if c > E.ctr.count:
                    continue
            else:
                assert c <= self.ctrs[k].count, "dep on pending milestone %s" % k
            if E.waited.get(k, 0) >= c:
                continue
            E.waited[k] = c
            E.prog.append(("wait", self.ctrs[k].sem, c))

    def op(self, ek, fn, reads=(), writes=(), inc=True):
        E = self.engs[ek]
        self._emit_waits(E, self._deps(reads, writes))
        E.prog.append(("ins", fn(_REC), inc))
        self.n_ins += 1
        mark = E.ctr.count + 1
        if inc:
            E.ctr.count += 1
            E.pending = False
        else:
            E.pending = True
        for b in reads:
            b.r[E.key] = mark
        for b in writes:
            b.w = (E.key, mark)
            b.r = {}

    def dma(self, qk, fn, reads=(), writes=()):
        Q = self.engs[qk]
        lane = self.lanes[self.lane_rr]
        self.lane_rr = (self.lane_rr + 1) % len(self.lanes)
        deps = self._deps(reads, writes)
        if lane.count > 0:
            deps[lane.key] = max(deps.get(lane.key, 0), lane.count)
        self._emit_waits(Q, deps)
        Q.prog.append(("dma", fn(_REC), lane.sem))
        self.n_ins += 1
        lane.count += 16
        for b in reads:
            b.r[lane.key] = lane.count
        for b in writes:
            b.w = (lane.key, lane.count)
            b.r = {}

    def finish(self):
        for e in self.engs.values():
            assert not e.pending
        Q = self.engs["sp"]
        for k, c in self.ctrs.items():
            if k == "sp" or c.count == 0:
                continue
            if Q.waited.get(k, 0) < c.count:
                Q.prog.append(("wait", c.sem, c.count))

    def replay(self, block):
        engs = self.engs

        def run(E, eng):
            for item in E.prog:
                if item[0] == "wait":
                    eng.wait_ge(item[1], item[2])
                elif item[0] == "ins":
                    c = item[1]
                    ins = getattr(eng, c[0])(*c[1], **c[2])
                    if item[2]:
                        ins.then_inc(E.ctr.sem, 1)
                else:
                    c = item[1]
                    ins = getattr(eng, c[0])(*c[1], **c[2])
                    ins.then_inc(item[2], 16)

        @block.tensor
        def _(e):
            run(engs["pe"], e)

        @block.scalar
        def _(e):
            run(engs["act"], e)

        @block.vector
        def _(e):
            run(engs["dve"], e)

        @block.gpsimd
        def _(e):
            run(engs["pool"], e)

        @block.sync
        def _(e):
            run(engs["sp"], e)


class T:
    def __init__(self, t, name):
        self.t = t
        self.b = Buf(name)

    def __getitem__(self, k):
        return self.t[k]


def build(NPT=8, groups=((0, 1), (2, 3)), do_sample=True, n_layers_out=4, NPG=2560):
    nc = bass.Bass("TRN2", target_bir_lowering=False)
    SEQ = NPT * 512

    def din(name, shape, dt=F32):
        return nc.dram_tensor(name, list(shape), dt, kind="ExternalInput").ap()

    def dout(name, shape, dt=F32):
        return nc.dram_tensor(name, list(shape), dt, kind="ExternalOutput").ap()

    xp = din("xp", [SEQ, 1024])
    xs = din("xs", [128, 1024])
    ck = din("ck", [NPG * 2 * 128, 512])
    cv = din("cv", [NPG * 2 * 128, 512])
    spst = din("spst", [16, 2, 15, 512])
    sgst = din("sgst", [16, 2, 4, 128, 256])
    ptb_d = din("ptb", [128, 256], I32)
    wie = din("wie", [2, 1024, 3072])
    woe = din("woe", [2, 1024, 1024])
    wio = din("wio", [2, 1024, 3072])
    woo = din("woo", [2, 1024, 1024])
    wpl = din("wpl", [2, 4, 128, 128])
    wga = din("wga", [2, 1024, 16])
    wgb = din("wgb", [2, 32, 512])
    ble = din("ble", [2, 128, NE])
    blo = din("blo", [2, 128, NO])
    ident_d = din("ident", [128, 128])
    tri_d = din("tri", [128, 128])
    triblk_d = din("triblk", [128, 128])
    seqm_d = din("seqm", [128, 16])
    iota_d = din("iota", [128, 1])
    cosp = din("cosp", [SEQ, 8])
    sinp = din("sinp", [SEQ, 8])
    coss = din("coss", [128, 8])
    sins = din("sins", [128, 8])
    rc0_d = din("rc0", [128, 64])

    yp = dout("yp", [SEQ, 1024])
    ys = dout("ys", [128, 1024])
    nkp = dout("nkp", [2, SEQ, 512])
    nvp = dout("nvp", [2, SEQ, 512])
    npp = dout("npp", [2, 15, 512])
    ngp = dout("ngp", [2, 4, 128, 256])
    nks = dout("nks", [16, 2, 8, 512])
    nvs = dout("nvs", [16, 2, 8, 512])
    nps = dout("nps", [16, 2, 15, 512])
    ngs = dout("ngs", [16, 2, 4, 128, 256])
    xmp = nc.dram_tensor("xmp", [SEQ, 1024], F32, kind="Internal").ap()
    xms = nc.dram_tensor("xms", [128, 1024], F32, kind="Internal").ap()

    es = ExitStack()
    with es:
        def sb(name, shape, dt=F32):
            return T(es.enter_context(nc.sbuf_tensor(name, list(shape), dt)), name)

        def psb(name, shape, dt=F32):
            t_ = T(es.enter_context(nc.psum_tensor(name, list(shape), dt)), name)
            t_.b.psum = True
            return t_

        sems = [es.enter_context(nc.semaphore("s%d" % i)) for i in range(5 + 24)]
        P = Prog(nc, sems, n_lanes=24)

        x_tok = sb("x_tok", [128, 4, 1024])
        xT = sb("xT", [128, 8, 512], BF16)
        wsl = [sb("wsl%d" % i, [128, 8, 512], BF16) for i in range(2)]
        blob = sb("blob", [128, NE])
        mixT = sb("mixT", [128, 8, 512], BF16)
        KT = sb("KT", [128, 4, max(SEQ, 4096)], BF16)
        VX = sb("VX", [128, max(SEQ // 128, 32), 4, 128], BF16)
        ident = sb("ident_s", [128, 128])
        tri = sb("tri_s", [128, 128])
        tri_bf = sb("tri_bf", [128, 128], BF16)
        triblk = sb("triblk_s", [128, 128])
        triblk_bf = sb("triblk_bf", [128, 128], BF16)
        ones_bf = sb("ones_bf", [128, 128], BF16)
        seqm = sb("seqm_s", [128, 16])
        iota = sb("iota_s", [128, 1])
        cs = sb("cs", [128, 4, 16])
        rc0 = sb("rc0_s", [128, 64])
        wplb = sb("wplb", [128, 4, 128], BF16)
        puT = sb("puT", [128, 4, 15 + 512])
        tf = [sb("tf%d" % i, [128, 1024]) for i in range(5)]
        tb = [sb("tb%d" % i, [128, 4, 512], BF16) for i in range(6)]
        PTb = [Buf("pt%d" % i, parent=tb[4].b) for i in range(4)]
        zr = sb("zr", [128, 32])
        ones_f = sb("ones_f", [128, 128])
        sm = sb("sm", [128, 64])
        rt = sb("rt", [128, 4, 64])
        Sst = sb("Sst", [128, 4, 256])
        Sbf = sb("Sbf", [128, 4, 256], BF16)
        wgab = sb("wgab", [128, 8, 16], BF16)
        wgbb = sb("wgbb", [32, 512], BF16)
        zT = sb("zT", [32, 512], BF16)
        ptb = sb("ptb_s", [128, 256], I32)
        ptf = sb("ptf", [128, 256])
        idx = sb("idx", [128, 2, 256], I32)
        kpg = [sb("kpg%d" % i, [128, 512]) for i in range(1)] * 2
        vpg = [sb("vpg%d" % i, [128, 512]) for i in range(1)] * 2
        KTs = sb("KTs", [128, 4, 128], BF16)
        Vs = sb("Vs", [128, 4, 128], BF16)

        bk = [psb("bk%d" % i, [128, 512]) for i in range(8)]
        pS2 = [[bk[0], bk[1]], [bk[2], bk[3]]]
        pS = [bk[0], bk[2]]
        pO = [bk[4], bk[5]]
        pZ = [bk[6], bk[7]]
        gpool = [bk, 0]

        def set_gen(pool):
            gpool[0] = pool
            gpool[1] = 0

        def gen():
            gpool[1] = (gpool[1] + 1) % len(gpool[0])
            return gpool[0][gpool[1]]

        dma = P.dma
        op = P.op

        for (tt, dd) in ((ident, ident_d), (tri, tri_d), (triblk, triblk_d), (seqm, seqm_d), (iota, iota_d), (rc0, rc0_d)):
            dma("sp", lambda e, tt=tt, dd=dd: e.dma_start(out=tt[:], in_=dd[:, :]), writes=[tt.b])
        dma("sp", lambda e: e.dma_start(out=ptb[:], in_=ptb_d[:, :]), writes=[ptb.b])
        op("dve", lambda e: e.tensor_copy(out=tri_bf[:], in_=tri[:]), reads=[tri.b], writes=[tri_bf.b])
        op("dve", lambda e: e.tensor_copy(out=triblk_bf[:], in_=triblk[:]), reads=[triblk.b], writes=[triblk_bf.b])
        op("dve", lambda e: e.memset(ones_bf[:], 1.0), writes=[ones_bf.b])
        op("dve", lambda e: e.memset(ones_f[:], 1.0), writes=[ones_f.b])
        op("dve", lambda e: e.memset(tb[1][:], 0.0), writes=[tb[1].b])
        op("dve", lambda e: e.memset(tb[2][:], 0.0), writes=[tb[2].b])
        op("dve", lambda e: e.memset(zT[:], 1.0), writes=[zT.b])
        op("dve", lambda e: e.tensor_copy(out=ptf[:], in_=ptb[:]), reads=[ptb.b], writes=[ptf.b])
        op("dve", lambda e: e.tensor_scalar(out=ptf[:], in0=ptf[:], scalar1=256.0, scalar2=iota[:, 0:1], op0=ALU.mult, op1=ALU.add),
           reads=[ptf.b, iota.b], writes=[ptf.b])
        op("dve", lambda e: e.tensor_copy(out=idx[:, 0, :], in_=ptf[:]), reads=[ptf.b], writes=[idx.b])
        op("dve", lambda e: e.tensor_scalar(out=ptf[:], in0=ptf[:], scalar1=128.0, scalar2=None, op0=ALU.add), reads=[ptf.b], writes=[ptf.b])
        op("dve", lambda e: e.tensor_copy(out=idx[:, 1, :], in_=ptf[:]), reads=[ptf.b], writes=[idx.b])

        wslot_i = [0]

        def load_w(src_ap):
            w = wsl[wslot_i[0]]
            wslot_i[0] ^= 1
            dma("pool", lambda e, w=w, src_ap=src_ap: e.dma_start(out=w[:], in_=src_ap.rearrange("(kc p) n -> p kc n", p=128)),
                writes=[w.b])
            return w

        def proj_F(w, m, out_ps, TS):
            for kc in range(8):
                op("pe", lambda e, kc=kc: e.matmul(out_ps[:, 0:TS], lhsT=w[:, kc, m * 128:(m + 1) * 128], rhs=xT[:, kc, 0:TS],
                                                   start=(kc == 0), stop=(kc == 7)),
                   reads=[w.b, xT.b], writes=[out_ps.b], inc=(kc == 7))

        def proj_T(w, s, out_ps, src=None, srcb=None):
            src = xT if src is None else src
            for kc in range(8):
                op("pe", lambda e, kc=kc: e.matmul(out_ps[:, :], lhsT=src[:, kc, s * 128:(s + 1) * 128], rhs=w[:, kc, :],
                                                   start=(kc == 0), stop=(kc == 7)),
                   reads=[w.b, src.b], writes=[out_ps.b], inc=(kc == 7))

        def transpose_to(dst_ap, dst_b, src_ap, src_b, rows=128, cols=128, eng="dve"):
            ps = gen()
            op("pe", lambda e: e.transpose(out=ps[0:cols, 0:rows], in_=src_ap, identity=ident[0:rows, 0:rows]),
               reads=[src_b, ident.b], writes=[ps.b])
            if eng == "dve":
                op("dve", lambda e: e.tensor_copy(out=dst_ap, in_=ps[0:cols, 0:rows]), reads=[ps.b], writes=[dst_b])
            else:
                op("act", lambda e: e.activation(out=dst_ap, in_=ps[0:cols, 0:rows], func=AF.Copy), reads=[ps.b], writes=[dst_b])

        def transpose4(dst_ap3, dst_b, src_aps, src_b, eng="dve"):
            ps = gen()
            for i, sa in enumerate(src_aps):
                op("pe", lambda e, i=i, sa=sa: e.transpose(out=ps[:, i * 128:(i + 1) * 128], in_=sa, identity=ident[:, :]),
                   reads=[src_b, ident.b], writes=[ps.b])
            src3 = ps[:, :].rearrange("p (g i) -> p g i", g=4)
            if eng == "dve":
                op("dve", lambda e: e.tensor_copy(out=dst_ap3, in_=src3), reads=[ps.b], writes=[dst_b])
            else:
                op("act", lambda e: e.activation(out=dst_ap3, in_=src3, func=AF.Copy), reads=[ps.b], writes=[dst_b])

        def make_xT(nsub):
            k = 0
            for s in range(nsub):
                for k4 in range(2):
                    transpose4(xT[:, k4 * 4:(k4 + 1) * 4, s * 128:(s + 1) * 128], xT.b,
                               [x_tok[:, s, kc * 128:(kc + 1) * 128] for kc in range(k4 * 4, k4 * 4 + 4)], x_tok.b,
                               eng=("dve" if k % 2 == 0 else "act"))
                    k += 1

        LNB = [Buf("lnb%d" % i) for i in range(4)]
        XSB = [Buf("xsb%d" % i, parent=x_tok.b) for i in range(4)]

        def layer_norm(nsub, goff, boff):
            for s in range(nsub):
                st6 = tf[4]
                o6 = 64 + s * 16
                o3 = 32 + s * 4
                lb = LNB[s]
                xb = XSB[s]
                for c in range(2):
                    op("dve", lambda e, c=c: e.bn_stats(out=st6[:, o6 + c * 6:o6 + (c + 1) * 6], in_=x_tok[:, s, c * 512:(c + 1) * 512]),
                       reads=[xb], writes=[lb])
                op("dve", lambda e: e.bn_aggr(out=sm[:, o3:o3 + 2], in_=st6[:, o6:o6 + 12].rearrange("p (c k) -> p c k", c=2)),
                   reads=[lb], writes=[lb])
                op("act", lambda e: e.activation(out=sm[:, o3 + 2:o3 + 3], in_=sm[:, o3 + 1:o3 + 2], func=AF.Ln, bias=1e-5, scale=1.0), reads=[lb], writes=[lb])
                op("act", lambda e: e.activation(out=sm[:, o3 + 2:o3 + 3], in_=sm[:, o3 + 2:o3 + 3], func=AF.Exp, scale=-0.5), reads=[lb], writes=[lb])
                op("dve", lambda e: e.tensor_scalar(out=x_tok[:, s, :], in0=x_tok[:, s, :], scalar1=sm[:, o3:o3 + 1], scalar2=sm[:, o3 + 2:o3 + 3],
                                                    op0=ALU.subtract, op1=ALU.mult), reads=[xb, lb], writes=[xb])
                op("pool", lambda e: e.tensor_tensor(out=x_tok[:, s, :], in0=x_tok[:, s, :], in1=blob[:, goff:goff + 1024], op=ALU.mult),
                   reads=[xb, blob.b], writes=[xb])
                op("pool", lambda e: e.tensor_tensor(out=x_tok[:, s, :], in0=x_tok[:, s, :], in1=blob[:, boff:boff + 1024], op=ALU.add),
                   reads=[xb, blob.b], writes=[xb])

        def out_proj(w_dram, nsub):
            for nb in range(2):
                w = load_w(w_dram[:, nb * 512:(nb + 1) * 512])
                for s in range(nsub):
                    ps = gen()
                    proj_T(w, s, ps, src=mixT)
                    op("dve", lambda e, s=s, nb=nb, ps=ps: e.scalar_tensor_tensor(
                        out=x_tok[:, s, nb * 512:(nb + 1) * 512], in0=x_tok[:, s, nb * 512:(nb + 1) * 512], scalar=DN_ALPHA,
                        in1=ps[:, :], op0=ALU.mult, op1=ALU.add), reads=[x_tok.b, ps.b], writes=[x_tok.b])

        def even_layer(j, li, tile_i, is_sample):
            nsub = 1 if is_sample else 4
            TS = 128 * nsub
            nseg = 16 if is_sample else 1
            seglen = 8 if is_sample else 512
            dma("sp", lambda e: e.dma_start(out=blob[:, 0:NE], in_=ble[j, :, :]), writes=[blob.b])
            dma("pool", lambda e: e.dma_start(out=wplb[:], in_=wpl[j].rearrange("g c e -> c g e")), writes=[wplb.b])
            LAM, PSC, SUBW = 2048, 2304, 2308
            lam_init = 0.8 - 0.6 * math.exp(-0.3 * li)
            op("dve", lambda e: e.tensor_tensor(out=rt[:, 0, :], in0=blob[:, LAM:LAM + 64], in1=blob[:, LAM + 64:LAM + 128], op=ALU.mult),
               reads=[blob.b], writes=[rt.b])
            op("dve", lambda e: e.tensor_tensor(out=rt[:, 1, :], in0=blob[:, LAM + 128:LAM + 192], in1=blob[:, LAM + 192:LAM + 256], op=ALU.mult),
               reads=[blob.b], writes=[rt.b])
            op("act", lambda e: e.activation(out=rt[:, 2, :], in_=rt[:, 0, :], func=AF.Copy, accum_out=sm[:, 4:5]), reads=[rt.b], writes=[rt.b, sm.b])
            op("act", lambda e: e.activation(out=rt[:, 2, :], in_=rt[:, 1, :], func=AF.Copy, accum_out=sm[:, 5:6]), reads=[rt.b], writes=[rt.b, sm.b])
            op("act", lambda e: e.activation(out=sm[:, 6:8], in_=sm[:, 4:6], func=AF.Exp), reads=[sm.b], writes=[sm.b])
            op("dve", lambda e: e.scalar_tensor_tensor(out=sm[:, 8:9], in0=sm[:, 6:7], scalar=lam_init, in1=sm[:, 7:8], op0=ALU.add, op1=ALU.subtract),
               reads=[sm.b], writes=[sm.b])
            op("dve", lambda e: e.tensor_scalar(out=sm[:, 9:10], in0=blob[:, SUBW:SUBW + 1], scalar1=(1.0 - lam_init), scalar2=None, op0=ALU.mult),
               reads=[blob.b], writes=[sm.b])
            if is_sample:
                dma("sp", lambda e: e.dma_start(out=cs[:, 0, 0:8], in_=coss[:, :]), writes=[cs.b])
                dma("sp", lambda e: e.dma_start(out=cs[:, 0, 8:16], in_=sins[:, :]), writes=[cs.b])
            else:
                r0 = tile_i * 512
                dma("sp", lambda e: e.dma_start(out=cs[:, :, 0:8], in_=cosp[r0:r0 + 512, :].rearrange("(s p) d -> p s d", p=128)), writes=[cs.b])
                dma("sp", lambda e: e.dma_start(out=cs[:, :, 8:16], in_=sinp[r0:r0 + 512, :].rearrange("(s p) d -> p s d", p=128)), writes=[cs.b])

            W = wie[j]
            pgT, qT0, qT1, agT, PTa, dbf = tb
            set_gen(bk)
            op("dve", lambda e: e.memset(qT0[64:128, :, :], 0.0), writes=[qT0.b])
            op("dve", lambda e: e.memset(qT1[0:64, :, :], 0.0), writes=[qT1.b])
            w = load_w(W[:, 512:1024])
            for m in range(4):
                ps = gen()
                proj_F(w, m, ps, TS)
                op("act", lambda e, m=m, ps=ps: e.activation(out=pgT[:, m, 0:TS], in_=ps[:, 0:TS], func=AF.Silu), reads=[ps.b], writes=[pgT.b])
            if DBG_STAGE == 1:
                return
            w = load_w(W[:, 0:512])
            if is_sample:
                for half in range(2):
                    hb = tf[0]
                    for q8 in range(8):
                        dma("sp", lambda e, half=half, q8=q8: e.dma_start(out=hb[q8 * 15:(q8 + 1) * 15, 0:512], in_=spst[half * 8 + q8, j, :, :]),
                            writes=[hb.b])
                    for g in range(4):
                        ps = gen()
                        op("pe", lambda e, g=g, ps=ps: e.transpose(out=ps[:, 0:120], in_=hb[0:120, g * 128:(g + 1) * 128], identity=ident[0:120, 0:120]),
                           reads=[hb.b, ident.b], writes=[ps.b])
                        op("dve", lambda e, g=g, ps=ps, half=half: e.tensor_copy(
                            out=puT[:, g, 0:368].rearrange("p (s t) -> p s t", t=23)[:, half * 8:(half + 1) * 8, 0:15],
                            in_=ps[:, 0:120].rearrange("p (s t) -> p s t", t=15)), reads=[ps.b], writes=[puT.b])
                    dma("sp", lambda e, half=half: e.dma_start(out=nps[half * 8:(half + 1) * 8, j, 0:7, :], in_=spst[half * 8:(half + 1) * 8, j, 8:15, :]))
            elif tile_i == 0:
                op("dve", lambda e: e.memset(puT[:, :, 0:15], 0.0), writes=[puT.b])
            for g in range(4):
                ps = gen()
                proj_F(w, g, ps, TS)
                if is_sample:
                    op("act", lambda e, g=g, ps=ps: e.activation(out=puT[:, g, 0:368].rearrange("p (s t) -> p s t", t=23)[:, :, 15:23],
                                                                 in_=ps[:, 0:128].rearrange("p (s t) -> p s t", t=8), func=AF.Copy),
                       reads=[ps.b], writes=[puT.b])
                else:
                    op("act", lambda e, g=g, ps=ps: e.activation(out=puT[:, g, 15:527], in_=ps[:, 0:512], func=AF.Copy), reads=[ps.b], writes=[puT.b])
            if is_sample:
                ps = gen()
                proj_T(w, 0, ps)
                ob = tf[1]
                op("act", lambda e, ps=ps: e.activation(out=ob[:, 0:512], in_=ps[:, :], func=AF.Copy), reads=[ps.b], writes=[ob.b])
                for sq in range(16):
                    dma("sp", lambda e, sq=sq: e.dma_start(out=nps[sq, j, 7:15, :], in_=ob[sq * 8:(sq + 1) * 8, 0:512]), reads=[ob.b])
            elif tile_i == NPT - 1:
                ps = gen()
                for kc in range(8):
                    op("pe", lambda e, kc=kc, ps=ps: e.matmul(ps[0:15, :], lhsT=xT[:, kc, 497:512], rhs=w[:, kc, :], start=(kc == 0), stop=(kc == 7)),
                       reads=[w.b, xT.b], writes=[ps.b], inc=(kc == 7))
                ob = tf[1]
                op("act", lambda e, ps=ps: e.activation(out=ob[0:15, 0:512], in_=ps[0:15, :], func=AF.Copy), reads=[ps.b], writes=[ob.b])
                dma("sp", lambda e: e.dma_start(out=npp[j, :, :], in_=ob[0:15, 0:512]), reads=[ob.b])
            L = 23 if is_sample else 527

            def seg(ap2d):
                return ap2d.rearrange("p (s t) -> p s t", t=L)

            for g in range(4):
                wnd = (2, 4, 8, 16)[g]
                cur = seg(puT[:, g, 0:nseg * L])
                curb = puT.b
                tmpi = 0
                sh = 1
                while sh < wnd:
                    dst_t = tf[tmpi]
                    dst = seg(dst_t[:, 0:nseg * L])
                    op("dve", lambda e, dst=dst, cur=cur, sh=sh: e.tensor_tensor(out=dst[:, :, sh:L], in0=cur[:, :, sh:L], in1=cur[:, :, 0:L - sh], op=ALU.add),
                       reads=[curb], writes=[dst_t.b])
                    if sh < 15:
                        op("dve", lambda e, dst=dst, cur=cur, sh=sh: e.tensor_copy(out=dst[:, :, 0:sh], in_=cur[:, :, 0:sh]), reads=[curb], writes=[dst_t.b])
                    cur, curb = dst, dst_t.b
                    tmpi ^= 1
                    sh *= 2
                dd = tf[2]
                ddv = dd[:, 0:TS].rearrange("p (s t) -> p s t", t=seglen)
                uv = seg(puT[:, g, 0:nseg * L])[:, :, 15:L]
                op("dve", lambda e, ddv=ddv, cur=cur, uv=uv, wnd=wnd: e.scalar_tensor_tensor(
                    out=ddv, in0=cur[:, :, 15:L], scalar=1.0 / wnd, in1=uv, op0=ALU.mult, op1=ALU.subtract),
                   reads=[curb, puT.b], writes=[dd.b])
                if (not is_sample) and tile_i == 0:
                    op("dve", lambda e, cur=cur, g=g: e.tensor_tensor(out=rt[:, 3, 0:16], in0=cur[:, 0, 15:31], in1=rc0[:, g * 16:(g + 1) * 16], op=ALU.mult),
                       reads=[curb, rc0.b], writes=[rt.b])
                    op("dve", lambda e, g=g: e.tensor_tensor(out=dd[:, 0:16], in0=rt[:, 3, 0:16], in1=puT[:, g, 15:31], op=ALU.subtract),
                       reads=[rt.b, puT.b], writes=[dd.b])
                op("act", lambda e, g=g: e.activation(out=dbf[:, g, 0:TS], in_=dd[:, 0:TS], func=AF.Copy), reads=[dd.b], writes=[dbf.b])
                ps = gen()
                op("pe", lambda e, g=g, ps=ps: e.matmul(ps[:, 0:TS], lhsT=wplb[:, g, :], rhs=dbf[:, g, 0:TS], start=True, stop=True),
                   reads=[wplb.b, dbf.b], writes=[ps.b])
                op("dve", lambda e, g=g, ps=ps: e.scalar_tensor_tensor(out=mixT[:, g, 0:TS], in0=ps[:, 0:TS], scalar=blob[:, PSC + g:PSC + g + 1],
                                                                       in1=pgT[:, g, 0:TS], op0=ALU.mult, op1=ALU.mult),
                   reads=[ps.b, blob.b, pgT.b], writes=[mixT.b])
            if not is_sample:
                op("dve", lambda e: e.tensor_copy(out=puT[:, :, 0:15], in_=puT[:, :, 512:527]), reads=[puT.b], writes=[puT.b])

            def rotary(src_ps, dst, s, scale):
                op("act", lambda e: e.activation(out=dst[:, 0:512], in_=src_ps[:, :], func=AF.Copy, scale=scale), reads=[src_ps.b], writes=[dst.b])
                dv = dst[:, 0:512].rearrange("p (g d) -> p g d", d=64)
                x1 = dv[:, :, 0:8]
                x2 = dv[:, :, 8:16]
                cosb = cs[:, s, 0:8].unsqueeze(1).broadcast_to([128, 8, 8])
                sinb = cs[:, s, 8:16].unsqueeze(1).broadcast_to([128, 8, 8])
                r = [rt[:, i, :].rearrange("p (g d) -> p g d", d=8) for i in range(4)]
                op("dve", lambda e: e.tensor_tensor(out=r[0], in0=x1, in1=cosb, op=ALU.mult), reads=[dst.b, cs.b], writes=[rt.b])
                op("dve", lambda e: e.tensor_tensor(out=r[1], in0=x2, in1=sinb, op=ALU.mult), reads=[dst.b, cs.b], writes=[rt.b])
                op("dve", lambda e: e.tensor_tensor(out=r[2], in0=x2, in1=cosb, op=ALU.mult), reads=[dst.b, cs.b], writes=[rt.b])
                op("dve", lambda e: e.tensor_tensor(out=r[3], in0=x1, in1=sinb, op=ALU.mult), reads=[dst.b, cs.b], writes=[rt.b])
                op("dve", lambda e: e.tensor_tensor(out=x1, in0=r[0], in1=r[1], op=ALU.subtract), reads=[rt.b], writes=[dst.b])
                op("dve", lambda e: e.tensor_tensor(out=x2, in0=r[2], in1=r[3], op=ALU.add), reads=[rt.b], writes=[dst.b])

            if DBG_STAGE == 2:
                return
            kbase = 0 if is_sample else tile_i * 512
            w = load_w(W[:, 1024:1536])
            for s in range(nsub):
                ps = gen()
                proj_T(w, s, ps)
                qt = tf[0] if s % 2 == 0 else tf[2]
                rotary(ps, qt, s, 0.125)
                for h in range(4):
                    ps2 = gen()
                    op("pe", lambda e, h=h, ps2=ps2: e.transpose(out=ps2[:, 0:128], in_=qt[:, h * 128:(h + 1) * 128], identity=ident[:, :]),
                       reads=[qt.b, ident.b], writes=[ps2.b])
                    op("dve", lambda e, h=h, s=s, ps2=ps2: e.tensor_copy(out=qT0[0:64, h, s * 128:(s + 1) * 128], in_=ps2[0:64, 0:128]),
                       reads=[ps2.b], writes=[qT0.b])
                    if DBG_STAGE != 7:
                        op("act", lambda e, h=h, s=s, ps2=ps2: e.activation(out=qT1[64:128, h, s * 128:(s + 1) * 128], in_=ps2[64:128, 0:128], func=AF.Copy),
                           reads=[ps2.b], writes=[qT1.b])
            if DBG_STAGE in (6, 7):
                return
            w = load_w(W[:, 1536:2048])
            KTd = KTs if is_sample else KT
            for s in range(nsub):
                ps = gen()
                proj_T(w, s, ps)
                kt = tf[1] if s % 2 == 0 else tf[2]
                rotary(ps, kt, s, 1.0)
                if is_sample:
                    for sq in range(16):
                        dma("sp", lambda e, sq=sq: e.dma_start(out=nks[sq, j, :, :], in_=kt[sq * 8:(sq + 1) * 8, 0:512]), reads=[kt.b])
                else:
                    r0 = tile_i * 512 + s * 128
                    dma("sp", lambda e, r0=r0: e.dma_start(out=nkp[j, r0:r0 + 128, :], in_=kt[:, 0:512]), reads=[kt.b])
                transpose4(KTd[:, :, kbase + s * 128:kbase + (s + 1) * 128], KTd.b, [kt[:, h * 128:(h + 1) * 128] for h in range(4)], kt.b,
                           eng=("dve" if s % 2 == 0 else "act"))
            if DBG_STAGE == 8:
                return
            w = load_w(W[:, 2048:2560])
            for s in range(nsub):
                ps = gen()
                proj_T(w, s, ps)
                vt = tf[3] if s % 2 == 0 else tf[0]
                op("act", lambda e, ps=ps: e.activation(out=vt[:, 0:512], in_=ps[:, :], func=AF.Copy), reads=[ps.b], writes=[vt.b])
                if is_sample:
                    op("dve", lambda e, ps=ps: e.tensor_copy(out=Vs[:].rearrange("p h e -> p (h e)"), in_=ps[:, :]), reads=[ps.b], writes=[Vs.b])
                    for sq in range(16):
                        dma("sp", lambda e, sq=sq: e.dma_start(out=nvs[sq, j, :, :], in_=vt[sq * 8:(sq + 1) * 8, 0:512]), reads=[vt.b])
                else:
                    blk = tile_i * 4 + s
                    op("dve", lambda e, ps=ps, blk=blk: e.tensor_copy(out=VX[:, blk, :, :].rearrange("p h e -> p (h e)"), in_=ps[:, :]),
                       reads=[ps.b], writes=[VX.b])
                    r0 = tile_i * 512 + s * 128
                    dma("sp", lambda e, r0=r0: e.dma_start(out=nvp[j, r0:r0 + 128, :], in_=vt[:, 0:512]), reads=[vt.b])
            w = load_w(W[:, 2560:3072])
            for m in range(4):
                ps = gen()
                proj_F(w, m, ps, TS)
                op("act", lambda e, m=m, ps=ps: e.activation(out=agT[:, m, 0:TS], in_=ps[:, 0:TS], func=AF.Silu), reads=[ps.b], writes=[agT.b])

            if DBG_STAGE == 3:
                return
            qTc = (qT0, qT1)

            def score_block(h, klhsT, klb, c0, kmask=None, pt_i=0):
                for c in range(2):
                    psc = pS2[c][pt_i]
                    ptb_ = PTb[pt_i * 2 + c]
                    op("pe", lambda e, c=c, psc=psc: e.matmul(psc[:, c0:TS], lhsT=klhsT, rhs=qTc[c][:, h, c0:TS], start=True, stop=True),
                       reads=[klb, qTc[c].b], writes=[psc.b])
                    op("act", lambda e, c=c, psc=psc: e.activation(out=PTa[:, pt_i * 2 + c, c0:TS], in_=psc[:, c0:TS], func=AF.Exp),
                       reads=[psc.b], writes=[ptb_])
                    if kmask is not None:
                        op("pool", lambda e, c=c: e.tensor_tensor(out=PTa[:, pt_i * 2 + c, c0:c0 + 128], in0=PTa[:, pt_i * 2 + c, c0:c0 + 128],
                                                                  in1=kmask[:, :], op=ALU.mult), reads=[ptb_, kmask.b], writes=[ptb_])

            def av_block(vrhs, vb, c0, first, last, pt_i=0):
                for c in range(2):
                    ptb_ = PTb[pt_i * 2 + c]
                    op("pe", lambda e, c=c: e.matmul(pO[c][:, c0:TS], lhsT=vrhs, rhs=PTa[:, pt_i * 2 + c, c0:TS], start=first, stop=last),
                       reads=[vb, ptb_], writes=[pO[c].b])
                    op("pe", lambda e, c=c: e.matmul(pZ[c][:, c0:TS], lhsT=ones_bf[:, :], rhs=PTa[:, pt_i * 2 + c, c0:TS], start=first, stop=last),
                       reads=[ones_bf.b, ptb_], writes=[pZ[c].b])

            def finalize(h, c0, c1, src=None):
                n = c1 - c0
                a1, a2, a3 = tf[0], tf[1], tf[2]
                if src is None:
                    src = [(pO[0][:, c0:c1], pO[0].b), (pO[1][:, c0:c1], pO[1].b), (pZ[0][:, c0:c1], pZ[0].b), (pZ[1][:, c0:c1], pZ[1].b)]
                (o1, o1b), (o2, o2b), (z1, z1b), (z2, z2b) = src
                op("act", lambda e: e.activation(out=a1[:, 0:n], in_=z1, func=AF.Ln), reads=[z1b], writes=[a1.b])
                op("act", lambda e: e.activation(out=a1[:, 0:n], in_=a1[:, 0:n], func=AF.Exp, scale=-1.0), reads=[a1.b], writes=[a1.b])
                op("act", lambda e: e.activation(out=a2[:, 0:n], in_=z2, func=AF.Ln), reads=[z2b], writes=[a2.b])
                op("act", lambda e: e.activation(out=a2[:, 0:n], in_=a2[:, 0:n], func=AF.Exp, scale=-1.0), reads=[a2.b], writes=[a2.b])
                op("dve", lambda e: e.tensor_tensor(out=a1[:, 0:n], in0=a1[:, 0:n], in1=o1, op=ALU.mult), reads=[a1.b, o1b], writes=[a1.b])
                op("dve", lambda e: e.tensor_tensor(out=a2[:, 0:n], in0=a2[:, 0:n], in1=o2, op=ALU.mult), reads=[a2.b, o2b], writes=[a2.b])
                op("dve", lambda e: e.scalar_tensor_tensor(out=a1[:, 0:n], in0=a2[:, 0:n], scalar=sm[:, 8:9], in1=a1[:, 0:n], op0=ALU.mult, op1=ALU.subtract),
                   reads=[a1.b, a2.b, sm.b], writes=[a1.b])
                sq = dbf
                op("dve", lambda e: e.tensor_tensor(out=sq[:, 0, 0:n], in0=a1[:, 0:n], in1=a1[:, 0:n], op=ALU.mult), reads=[a1.b], writes=[sq.b])
                ps = gen()
                op("pe", lambda e: e.matmul(ps[:, 0:n], lhsT=ones_bf[:, :], rhs=sq[:, 0, 0:n], start=True, stop=True), reads=[ones_bf.b, sq.b], writes=[ps.b])
                op("act", lambda e: e.activation(out=a3[:, 0:n], in_=ps[:, 0:n], func=AF.Ln, bias=1e-5, scale=1.0 / 128.0), reads=[ps.b], writes=[a3.b])
                op("act", lambda e: e.activation(out=a3[:, 0:n], in_=a3[:, 0:n], func=AF.Exp, scale=-0.5), reads=[a3.b], writes=[a3.b])
                op("dve", lambda e: e.scalar_tensor_tensor(out=a1[:, 0:n], in0=a1[:, 0:n], scalar=-1.0, in1=a3[:, 0:n], op0=ALU.mult, op1=ALU.mult),
                   reads=[a1.b, a3.b], writes=[a1.b])
                op("dve", lambda e: e.scalar_tensor_tensor(out=mixT[:, 4 + h, c0:c1], in0=a1[:, 0:n], scalar=sm[:, 9:10], in1=agT[:, h, c0:c1],
                                                           op0=ALU.mult, op1=ALU.mult), reads=[a1.b, sm.b, agT.b], writes=[mixT.b])

            set_gen(bk[0:4])
            if not is_sample:
                nkb = tile_i * 4 + 4
                for h in range(4):
                    prev = None
                    for kb in range(nkb):
                        jl = kb - tile_i * 4
                        c0 = 0 if jl < 0 else jl * 128
                        score_block(h, KT[:, h, kb * 128:(kb + 1) * 128], KT.b, c0, kmask=(tri_bf if jl >= 0 else None), pt_i=kb % 2)
                        if prev is not None:
                            av_block(*prev)
                        prev = (VX[:, kb, h, :], VX.b, c0, (kb == 0), (kb == nkb - 1), kb % 2)
                    av_block(*prev)
                    finalize(h, 0, 512)
            else:
                KTq = KT
                Oa = (tf[3], tf[4])
                pi = 0
                stg = [(x_tok[:, 1, 0:512], Buf("stg0", parent=x_tok.b)), (x_tok[:, 1, 512:1024], Buf("stg1", parent=x_tok.b)),
                       (x_tok[:, 2, 0:512], Buf("stg2", parent=x_tok.b)), (x_tok[:, 2, 512:1024], Buf("stg3", parent=x_tok.b)),
                       (x_tok[:, 3, 0:512], Buf("stg4", parent=x_tok.b)), (x_tok[:, 3, 512:1024], Buf("stg5", parent=x_tok.b))]
                KTh = [Buf("kth0", parent=KT.b), Buf("kth1", parent=KT.b)]
                VXh = [Buf("vxh0", parent=VX.b), Buf("vxh1", parent=VX.b)]
                for sq in range(16):
                    sp_ = sq % 2
                    kof = sp_ * 2048
                    vof = sp_ * 16
                    for pgi in range(16):
                        n = sq * 16 + pgi
                        kp, kpb = stg[(pi % 3) * 2]
                        vp, vpb = stg[(pi % 3) * 2 + 1]
                        pi += 1
                        dma("pool", lambda e, kp=kp, n=n: e.indirect_dma_start(
                            out=kp, out_offset=None, in_=ck, in_offset=bass.IndirectOffsetOnAxis(ap=idx[:, j, n:n + 1], axis=0)),
                            reads=[idx.b], writes=[kpb])
                        dma("pool", lambda e, vp=vp, n=n: e.indirect_dma_start(
                            out=vp, out_offset=None, in_=cv, in_offset=bass.IndirectOffsetOnAxis(ap=idx[:, j, n:n + 1], axis=0)),
                            reads=[idx.b], writes=[vpb])
                        transpose4(KTq[:, :, kof + pgi * 128:kof + (pgi + 1) * 128], KTh[sp_], [kp[:, h * 128:(h + 1) * 128] for h in range(4)], kpb,
                                   eng=("dve" if pgi % 2 == 0 else "act"))
                        if pgi % 2 == 0:
                            op("act", lambda e, vp=vp, pgi=pgi: e.activation(out=VX[:, vof + pgi, :, :].rearrange("p h e -> p (h e)"), in_=vp, func=AF.Copy),
                               reads=[vpb], writes=[VXh[sp_]])
                        else:
                            op("dve", lambda e, vp=vp, pgi=pgi: e.tensor_copy(out=VX[:, vof + pgi, :, :].rearrange("p h e -> p (h e)"), in_=vp),
                               reads=[vpb], writes=[VXh[sp_]])
                    q0 = sq * 8
                    for h in range(4):
                        par = (sq * 4 + h) % 2
                        psc = gen()
                        ptb_ = PTb[par]
                        for pgi in range(17):
                            lhs = KTq[:, h, kof + pgi * 128:kof + (pgi + 1) * 128] if pgi < 16 else KTs[:, h, :]
                            lb = KTh[sp_] if pgi < 16 else KTs.b
                            for c in range(2):
                                col = pgi * 16 + c * 8
                                op("pe", lambda e, c=c, col=col, lhs=lhs, psc=psc: e.matmul(psc[:, col:col + 8], lhsT=lhs, rhs=qTc[c][:, h, q0:q0 + 8],
                                                                                           start=True, stop=True), reads=[lb, qTc[c].b], writes=[psc.b],
                                   inc=(pgi == 16 and c == 1))
                        op("act", lambda e, psc=psc: e.activation(out=PTa[:, par, 0:272], in_=psc[:, 0:272], func=AF.Exp), reads=[psc.b], writes=[ptb_])
                        op("pool", lambda e: e.tensor_tensor(out=PTa[:, par, 256:272].rearrange("p (c q) -> p c q", c=2),
                                                             in0=PTa[:, par, 256:272].rearrange("p (c q) -> p c q", c=2),
                                                             in1=triblk_bf[:, q0:q0 + 8].unsqueeze(1).broadcast_to([128, 2, 8]), op=ALU.mult),
                           reads=[ptb_, triblk_bf.b], writes=[ptb_])
                        po, pz = pO[par], pZ[par]
                        for pgi in range(17):
                            vr = VX[:, vof + pgi, h, :] if pgi < 16 else Vs[:, h, :]
                            vb = VXh[sp_] if pgi < 16 else Vs.b
                            op("pe", lambda e, pgi=pgi, vr=vr, po=po: e.matmul(po[:, 0:16], lhsT=vr, rhs=PTa[:, par, pgi * 16:(pgi + 1) * 16],
                                                                             start=(pgi == 0), stop=(pgi == 16)), reads=[vb, ptb_], writes=[po.b], inc=(pgi == 16))
                        op("dve", lambda e: e.tensor_reduce(out=zr[:, par * 16:(par + 1) * 16], in_=PTa[:, par, 0:272].rearrange("p (g c) -> p c g", c=16),
                                                            axis=mybir.AxisListType.X, op=ALU.add), reads=[ptb_], writes=[zr.b])
                        op("pe", lambda e, pz=pz: e.matmul(pz[:, 0:16], lhsT=ones_f[:, :], rhs=zr[:, par * 16:(par + 1) * 16], start=True, stop=True),
                           reads=[ones_f.b, zr.b], writes=[pz.b])
                        op("dve", lambda e, po=po: e.tensor_copy(out=Oa[0][:, h * 256:(h + 1) * 256].rearrange("p (c q) -> p c q", c=2)[:, :, q0:q0 + 8],
                                                                in_=po[:, 0:16].rearrange("p (c q) -> p c q", c=2)), reads=[po.b], writes=[Oa[0].b])
                        op("act", lambda e, pz=pz: e.activation(out=Oa[1][:, h * 256:(h + 1) * 256].rearrange("p (c q) -> p c q", c=2)[:, :, q0:q0 + 8],
                                                               in_=pz[:, 0:16].rearrange("p (c q) -> p c q", c=2), func=AF.Copy), reads=[pz.b], writes=[Oa[1].b])
                for h in range(4):
                    finalize(h, 0, 128, src=[(Oa[0][:, h * 256:h * 256 + 128], Oa[0].b), (Oa[0][:, h * 256 + 128:h * 256 + 256], Oa[0].b),
                                             (Oa[1][:, h * 256:h * 256 + 128], Oa[1].b), (Oa[1][:, h * 256 + 128:h * 256 + 256], Oa[1].b)])
            set_gen(bk)
            if DBG_STAGE == 4:
                return
            out_proj(woe[j], nsub)
            if DBG_STAGE == 5:
                return
            layer_norm(nsub, 0, 1024)

        def vgh(s, h):
            t_ = tb[1] if h < 2 else tb[2]
            return t_[:, s, (h % 2) * 256:(h % 2 + 1) * 256], t_.b

        gs = sb("gs", [128, 4, 1024], BF16)
        Zq = sb("Zq", [128, 4, 128], BF16)
        op("dve", lambda e: e.memset(Zq[:], 0.0), writes=[Zq.b])

        def odd_layer(j, li, tile_i, is_sample):
            nsub = 1 if is_sample else 4
            TS = 128 * nsub
            dma("sp", lambda e: e.dma_start(out=blob[:, 0:NO], in_=blo[j, :, :]), writes=[blob.b])
            dma("pool", lambda e: e.dma_start(out=wgab[:], in_=wga[j].rearrange("(kc p) r -> p kc r", p=128)), writes=[wgab.b])
            dma("pool", lambda e: e.dma_start(out=wgbb[:], in_=wgb[j, :, :]), writes=[wgbb.b])
            W = wio[j]
            qTg, kTg, ktk, keT = tb[0], tb[3], tb[4], tb[5]
            set_gen(bk)
            msk = triblk if is_sample else tri
            if (not is_sample) and tile_i == 0:
                op("dve", lambda e: e.memset(Sst[:], 0.0), writes=[Sst.b])
                op("dve", lambda e: e.memset(Sbf[:], 0.0), writes=[Sbf.b])
            ps = gen()
            for kc in range(8):
                op("pe", lambda e, kc=kc, ps=ps: e.matmul(ps[0:16, 0:TS], lhsT=wgab[:, kc, :], rhs=xT[:, kc, 0:TS], start=(kc == 0), stop=(kc == 7)),
                   reads=[wgab.b, xT.b], writes=[ps.b], inc=(kc == 7))
            op("act", lambda e, ps=ps: e.activation(out=zT[0:16, 0:TS], in_=ps[0:16, 0:TS], func=AF.Copy), reads=[ps.b], writes=[zT.b])
            w = load_w(W[:, 0:512])
            for h in range(4):
                ps = gen()
                proj_F(w, h, ps, TS)
                op("act", lambda e, h=h, ps=ps: e.activation(out=qTg[:, h, 0:TS], in_=ps[:, 0:TS], func=AF.Copy, scale=128.0 ** -0.5), reads=[ps.b], writes=[qTg.b])
            w = load_w(W[:, 512:1024])
            for h in range(4):
                ps = gen()
                proj_F(w, h, ps, TS)
                op("dve", lambda e, h=h, ps=ps: e.tensor_copy(out=kTg[:, h, 0:TS], in_=ps[:, 0:TS]), reads=[ps.b], writes=[kTg.b])
            for s in range(nsub):
                ps = gen()
                proj_T(w, s, ps)
                op("act", lambda e, s=s, ps=ps: e.activation(out=ktk[:, s, :], in_=ps[:, :], func=AF.Copy), reads=[ps.b], writes=[ktk.b])
            for nb in range(2):
                w = load_w(W[:, 1024 + nb * 512:1536 + nb * 512])
                for s in range(nsub):
                    ps = gen()
                    proj_T(w, s, ps)
                    op("dve", lambda e, s=s, nb=nb, ps=ps: e.tensor_copy(out=tb[1 + nb][:, s, :], in_=ps[:, :]), reads=[ps.b], writes=[tb[1 + nb].b])
            for nb in range(2):
                w = load_w(W[:, 2048 + nb * 512:2560 + nb * 512])
                for s in range(nsub):
                    ps = gen()
                    proj_T(w, s, ps)
                    op("act", lambda e, s=s, nb=nb, ps=ps: e.activation(out=gs[:, s, nb * 512:(nb + 1) * 512], in_=ps[:, :], func=AF.Silu), reads=[ps.b], writes=[gs.b])
            ob = (pO[0], pO[1], pZ[0], pZ[1])
            set_gen(bk[0:4])
            for s in range(nsub):
                c0 = s * 128
                la, ecb, eq, ek, mt = tf[0], tf[1], tf[2], tf[3], tf[4]
                ps = gen()
                op("pe", lambda e, ps=ps: e.matmul(ps[:, :], lhsT=zT[0:32, c0:c0 + 128], rhs=wgbb[0:32, :], start=True, stop=True), reads=[zT.b, wgbb.b], writes=[ps.b])
                op("act", lambda e, ps=ps: e.activation(out=la[:, 0:512], in_=ps[:, :], func=AF.Exp, scale=-1.0), reads=[ps.b], writes=[la.b])
                op("act", lambda e: e.activation(out=la[:, 0:512], in_=la[:, 0:512], func=AF.Ln, bias=1.0, scale=1.0), reads=[la.b], writes=[la.b])
                op("dve", lambda e: e.tensor_scalar(out=la[:, 0:512], in0=la[:, 0:512], scalar1=-1.0 / 16.0, scalar2=None, op0=ALU.mult), reads=[la.b], writes=[la.b])
                ps = gen()
                op("pe", lambda e, ps=ps: e.matmul(ps[:, :], lhsT=msk[:, :], rhs=la[:, 0:512], start=True, stop=True), reads=[msk.b, la.b], writes=[ps.b])
                op("act", lambda e, ps=ps: e.activation(out=ecb[:, 0:512], in_=ps[:, :], func=AF.Exp, scale=-1.0), reads=[ps.b], writes=[ecb.b])
                op("dve", lambda e, s=s: e.tensor_tensor(out=ktk[:, s, :], in0=ktk[:, s, :], in1=ecb[:, 0:512], op=ALU.mult), reads=[ktk.b, ecb.b], writes=[ktk.b])
                ps = gen()
                for h in range(4):
                    op("pe", lambda e, h=h, ps=ps: e.matmul(ps[:, h * 128:(h + 1) * 128], lhsT=la[:, h * 128:(h + 1) * 128], rhs=msk[:, :], start=True, stop=True),
                       reads=[msk.b, la.b], writes=[ps.b])
                op("act", lambda e, ps=ps: e.activation(out=eq[:, 0:512], in_=ps[:, :], func=AF.Exp), reads=[ps.b], writes=[eq.b])
                op("act", lambda e, ps=ps: e.activation(out=ek[:, 0:512], in_=ps[:, :], func=AF.Exp, scale=-1.0), reads=[ps.b], writes=[ek.b])
                op("dve", lambda e: e.tensor_tensor(out=keT[:, :, 0:128], in0=qTg[:, :, c0:c0 + 128], in1=eq[:, 0:512].rearrange("p (h i) -> p h i", h=4), op=ALU.mult),
                   reads=[qTg.b, eq.b], writes=[keT.b])
                op("dve", lambda e: e.tensor_tensor(out=keT[:, :, 128:256], in0=kTg[:, :, c0:c0 + 128], in1=ek[:, 0:512].rearrange("p (h i) -> p h i", h=4), op=ALU.mult),
                   reads=[kTg.b, ek.b], writes=[keT.b])
                for h in range(4):
                    ps = gen()
                    op("pe", lambda e, h=h, ps=ps: e.matmul(ps[:, 0:128], lhsT=keT[:, h, 128:256], rhs=keT[:, h, 0:128], start=True, stop=True), reads=[keT.b], writes=[ps.b])
                    op("dve", lambda e, h=h, ps=ps: e.tensor_tensor(out=keT[:, h, 256:384], in0=ps[:, 0:128], in1=msk[:, :], op=ALU.mult), reads=[ps.b, msk.b], writes=[keT.b])
                if not is_sample:
                    for h in range(4):
                        o = ob[h]
                        op("pe", lambda e, h=h, o=o: e.matmul(o[:, 0:256], lhsT=keT[:, h, 0:128], rhs=Sbf[:, h, :], start=True, stop=False), reads=[keT.b, Sbf.b], writes=[o.b], inc=False)
                        op("pe", lambda e, h=h, o=o: e.matmul(o[:, 0:256], lhsT=keT[:, h, 256:384], rhs=vgh(s, h)[0], start=False, stop=True), reads=[keT.b, vgh(s, h)[1]], writes=[o.b])
                    for h in range(4):
                        ps = gen()
                        op("pe", lambda e, h=h, ps=ps: e.matmul(ps[:, 0:256], lhsT=ktk[:, s, h * 128:(h + 1) * 128], rhs=vgh(s, h)[0], start=True, stop=True),
                           reads=[ktk.b, vgh(s, h)[1]], writes=[ps.b])
                        ecl = eq[:, h * 128 + 127:h * 128 + 128]
                        op("dve", lambda e, h=h, ecl=ecl: e.tensor_scalar(out=Sst[:, h, :], in0=Sst[:, h, :], scalar1=ecl, scalar2=None, op0=ALU.mult), reads=[Sst.b, eq.b], writes=[Sst.b])
                        op("dve", lambda e, h=h, ps=ps, ecl=ecl: e.scalar_tensor_tensor(out=Sst[:, h, :], in0=ps[:, 0:256], scalar=ecl, in1=Sst[:, h, :], op0=ALU.mult, op1=ALU.add),
                           reads=[ps.b, eq.b, Sst.b], writes=[Sst.b])
                        op("act", lambda e, h=h: e.activation(out=Sbf[:, h, :], in_=Sst[:, h, :], func=AF.Copy), reads=[Sst.b], writes=[Sbf.b])
                    if tile_i == NPT - 1 and s == nsub - 1:
                        dma("sp", lambda e: e.dma_start(out=ngp[j].rearrange("h k v -> k h v"), in_=Sst[:]), reads=[Sst.b])
                else:
                    for sq in range(16):
                        q0 = sq * 8
                        dma("sp", lambda e, sq=sq: e.dma_start(out=Sst[:], in_=sgst[sq, j].rearrange("h k v -> k h v")), writes=[Sst.b])
                        op("act", lambda e: e.activation(out=Sbf[:], in_=Sst[:], func=AF.Copy), reads=[Sst.b], writes=[Sbf.b])
                        op("dve", lambda e, q0=q0: e.tensor_copy(out=Zq[:, :, q0:q0 + 8], in_=keT[:, :, q0:q0 + 8]), reads=[keT.b], writes=[Zq.b])
                        for h in range(4):
                            o = ob[h]
                            op("pe", lambda e, h=h, o=o, sq=sq: e.matmul(o[:, 0:256], lhsT=Zq[:, h, :], rhs=Sbf[:, h, :], start=(sq == 0), stop=False),
                               reads=[Zq.b, Sbf.b], writes=[o.b])
                        op("dve", lambda e, q0=q0: e.memset(Zq[:, :, q0:q0 + 8], 0.0), writes=[Zq.b])
                        km = tb[0]
                        km = mixT
                        op("dve", lambda e, sq=sq: e.tensor_scalar(out=km[:, 0, :], in0=ktk[:, 0, :], scalar1=seqm[:, sq:sq + 1], scalar2=None, op0=ALU.mult),
                           reads=[ktk.b, seqm.b], writes=[km.b])
                        for h in range(4):
                            ps = gen()
                            op("pe", lambda e, h=h, ps=ps: e.matmul(ps[:, 0:256], lhsT=km[:, 0, h * 128:(h + 1) * 128], rhs=vgh(0, h)[0], start=True, stop=True),
                               reads=[km.b, vgh(0, h)[1]], writes=[ps.b])
                            ecl = eq[:, h * 128 + q0 + 7:h * 128 + q0 + 8]
                            op("dve", lambda e, h=h, ecl=ecl: e.tensor_scalar(out=Sst[:, h, :], in0=Sst[:, h, :], scalar1=ecl, scalar2=None, op0=ALU.mult), reads=[Sst.b, eq.b], writes=[Sst.b])
                            op("dve", lambda e, h=h, ps=ps, ecl=ecl: e.scalar_tensor_tensor(out=Sst[:, h, :], in0=ps[:, 0:256], scalar=ecl, in1=Sst[:, h, :], op0=ALU.mult, op1=ALU.add),
                               reads=[ps.b, eq.b, Sst.b], writes=[Sst.b])
                        dma("sp", lambda e, sq=sq: e.dma_start(out=ngs[sq, j].rearrange("h k v -> k h v"), in_=Sst[:]), reads=[Sst.b])
                    for h in range(4):
                        o = ob[h]
                        op("pe", lambda e, h=h, o=o: e.matmul(o[:, 0:256], lhsT=keT[:, h, 256:384], rhs=vgh(0, h)[0], start=False, stop=True), reads=[keT.b, vgh(0, h)[1]], writes=[o.b])
                for h in range(4):
                    o = ob[h]
                    op("act", lambda e, h=h, o=o: e.activation(out=mt[:, h * 256:(h + 1) * 256], in_=o[:, 0:256], func=AF.Square, accum_out=sm[:, 16 + h:17 + h]),
                       reads=[o.b], writes=[mt.b, sm.b])
                op("act", lambda e: e.activation(out=sm[:, 20:24], in_=sm[:, 16:20], func=AF.Ln, bias=1e-5, scale=1.0 / 256.0), reads=[sm.b], writes=[sm.b])
                op("act", lambda e: e.activation(out=sm[:, 20:24], in_=sm[:, 20:24], func=AF.Exp, scale=-0.5), reads=[sm.b], writes=[sm.b])
                for h in range(4):
                    o = ob[h]
                    op("dve", lambda e, h=h, o=o: e.scalar_tensor_tensor(out=mt[:, h * 256:(h + 1) * 256], in0=o[:, 0:256], scalar=sm[:, 20 + h:21 + h],
                                                                        in1=blob[:, 2048:2304], op0=ALU.mult, op1=ALU.mult), reads=[o.b, sm.b, blob.b], writes=[mt.b])
                op("pool", lambda e, s=s: e.tensor_tensor(out=mt[:, 0:1024], in0=mt[:, 0:1024], in1=gs[:, s, :], op=ALU.mult), reads=[mt.b, gs.b], writes=[mt.b])
                for k4 in range(2):
                    transpose4(mixT[:, k4 * 4:(k4 + 1) * 4, c0:c0 + 128], mixT.b, [mt[:, kc * 128:(kc + 1) * 128] for kc in range(k4 * 4, k4 * 4 + 4)], mt.b,
                               eng=("dve" if k4 == 0 else "act"))
            set_gen(bk)
            out_proj(woo[j], nsub)
            layer_norm(nsub, 0, 1024)

        n_even = 0
        for gi_, grp in enumerate(groups):
            first_grp = (gi_ == 0)
            last_grp = (gi_ == len(groups) - 1)
            tiles = [(t, False) for t in range(NPT)] + ([(0, True)] if do_sample else [])
            for (t, is_s) in tiles:
                nsub = 1 if is_s else 4
                if is_s:
                    src = xs if first_grp else xms
                    dma("sp", lambda e, src=src: e.dma_start(out=x_tok[:, 0, :], in_=src[:, :]), writes=[x_tok.b])
                else:
                    src = xp if first_grp else xmp
                    dma("sp", lambda e, src=src, t=t: e.dma_start(out=x_tok[:], in_=src[t * 512:(t + 1) * 512, :].rearrange("(s p) d -> p s d", p=128)), writes=[x_tok.b])
                make_xT(nsub)
                for li in grp:
                    if li % 2 == 0:
                        even_layer(li // 2, li, t, is_s)
                    else:
                        odd_layer(li // 2, li, t, is_s)
                    if li != grp[-1]:
                        make_xT(nsub)
                if is_s:
                    dst = ys if last_grp else xms
                    dma("sp", lambda e, dst=dst: e.dma_start(out=dst[:, :], in_=x_tok[:, 0, :]), reads=[x_tok.b])
                else:
                    dst = yp if last_grp else xmp
                    dma("sp", lambda e, dst=dst, t=t: e.dma_start(out=dst[t * 512:(t + 1) * 512, :].rearrange("(s p) d -> p s d", p=128), in_=x_tok[:]), reads=[x_tok.b])
        P.finish()
        with nc.Block() as block:
            P.replay(block)
    return nc, P


def _consts(SEQ):
    inv = np.power(np.float32(500000.0), -np.arange(0, 16, 2, dtype=np.float32) / 16).astype(np.float32)
    pos = np.arange(SEQ, dtype=np.float32)
    ang = pos[:, None] * inv[None, :]
    poss = (2048 + np.arange(8, dtype=np.float32))
    angs = np.tile(poss[:, None] * inv[None, :], (16, 1)).astype(np.float32)
    jj = np.arange(128)
    tri = (jj[:, None] <= jj[None, :]).astype(np.float32)
    same = (jj[:, None] // 8) == (jj[None, :] // 8)
    triblk = (tri * same).astype(np.float32)
    seqm = ((jj[:, None] // 8) == np.arange(16)[None, :]).astype(np.float32)
    rc0 = np.zeros((128, 64), np.float32)
    for g, w in enumerate((2, 4, 8, 16)):
        rc0[:, g * 16:(g + 1) * 16] = 1.0 / np.minimum(np.arange(16) + 1, w).astype(np.float32)
    return dict(ident=np.eye(128, dtype=np.float32), tri=tri, triblk=triblk, seqm=seqm,
                iota=np.arange(128, dtype=np.float32).reshape(128, 1),
                cosp=np.cos(ang).astype(np.float32), sinp=np.sin(ang).astype(np.float32),
                coss=np.cos(angs).astype(np.float32), sins=np.sin(angs).astype(np.float32), rc0=rc0)


def _weights(w_in_even, w_pool_lin, pool_scale, diff_lambda_params, subln_w, w_out_even, w_in_odd, w_gate_a, w_gate_b,
             b_gate, gla_norm_w, w_out_odd, ln_g, ln_b):
    f = np.float32
    ble = np.zeros((2, 128, NE), f)
    blo = np.zeros((2, 128, NO), f)
    for j in range(2):
        ble[j, :, 0:1024] = ln_g[2 * j][None, :]
        ble[j, :, 1024:2048] = ln_b[2 * j][None, :]
        ble[j, :, 2048:2304] = diff_lambda_params[j].reshape(1, 256)
        ble[j, :, 2304:2308] = pool_scale[j].reshape(4, 128).T
        ble[j, :, 2308] = subln_w[j]
        blo[j, :, 0:1024] = ln_g[2 * j + 1][None, :]
        blo[j, :, 1024:2048] = ln_b[2 * j + 1][None, :]
        blo[j, :, 2048:2304] = gla_norm_w[j][None, :]
    wgb = np.zeros((2, 32, 512), f)
    wgb[:, 0:16] = w_gate_b
    wgb[:, 16] = b_gate
    c = np.ascontiguousarray
    return dict(wie=c(w_in_even, dtype=f), woe=c(w_out_even, dtype=f), wio=c(w_in_odd, dtype=f), woo=c(w_out_odd, dtype=f),
                wpl=c(w_pool_lin, dtype=f), wga=c(w_gate_a, dtype=f), wgb=wgb, ble=ble, blo=blo)


_CACHE = {}


def kernel(x_prompt, x_sample, cache_k, cache_v, state_pool, state_gla, page_table,
           w_in_even, w_pool_lin, pool_scale, diff_lambda_params, subln_w, w_out_even,
           w_in_odd, w_gate_a, w_gate_b, b_gate, gla_norm_w, w_out_odd, ln_g, ln_b):
    f = np.float32
    if "nc" not in _CACHE:
        _CACHE["nc"] = build()[0]
    nc = _CACHE["nc"]
    common = _weights(np.asarray(w_in_even), np.asarray(w_pool_lin), np.asarray(pool_scale), np.asarray(diff_lambda_params),
                      np.asarray(subln_w), np.asarray(w_out_even), np.asarray(w_in_odd), np.asarray(w_gate_a), np.asarray(w_gate_b),
                      np.asarray(b_gate), np.asarray(gla_norm_w), np.asarray(w_out_odd), np.asarray(ln_g), np.asarray(ln_b))
    common.update(_consts(4096))
    common["ck"] = np.asarray(cache_k, dtype=f).reshape(2560 * 2 * 128, 512)
    common["cv"] = np.asarray(cache_v, dtype=f).reshape(2560 * 2 * 128, 512)
    xp = np.asarray(x_prompt, dtype=f)
    xs = np.asarray(x_sample, dtype=f)
    sp = np.asarray(state_pool, dtype=f)
    sg = np.asarray(state_gla, dtype=f)
    pt = np.asarray(page_table).astype(np.int32)
    in_maps = []
    for c in range(8):
        m = dict(common)
        m["xp"] = np.ascontiguousarray(xp[c % 4])
        m["xs"] = np.ascontiguousarray(xs[c * 16:(c + 1) * 16].reshape(128, 1024))
        m["spst"] = np.ascontiguousarray(sp[c * 16:(c + 1) * 16])
        m["sgst"] = np.ascontiguousarray(sg[c * 16:(c + 1) * 16])
        m["ptb"] = np.ascontiguousarray(np.broadcast_to(pt[c * 16:(c + 1) * 16].reshape(1, 256), (128, 256)))
        in_maps.append(m)
    res = run_bass_kernel_spmd(nc, in_maps, core_ids=list(range(8))).results
    yp = np.stack([res[b]["yp"] for b in range(4)])
    ys = np.concatenate([res[c]["ys"].reshape(16, 8, 1024) for c in range(8)])
    nkp = np.stack([res[b]["nkp"].reshape(2, 4096, 4, 128) for b in range(4)])
    nvp = np.stack([res[b]["nvp"].reshape(2, 4096, 4, 128) for b in range(4)])
    npp = np.stack([res[b]["npp"] for b in range(4)])
    ngp = np.stack([res[b]["ngp"] for b in range(4)])
    nks = np.concatenate([res[c]["nks"].reshape(16, 2, 8, 4, 128) for c in range(8)])
    nvs = np.concatenate([res[c]["nvs"].reshape(16, 2, 8, 4, 128) for c in range(8)])
    nps = np.concatenate([res[c]["nps"] for c in range(8)])
    ngs = np.concatenate([res[c]["ngs"] for c in range(8)])
    return (yp, ys, nkp, nvp, npp, ngp, nks, nvs, nps, ngs)
```
